# Optimizing a Trainium2 kernel written in Bass

```python
import jax, jax.numpy as jnp
from jax import lax
import numpy as np

D_MODEL = 1024
BATCH = 32
SEQ = 2048
DEPTH = 2
DEC_BATCH = 2
DEC_SEQ = 16384
PAST_LEN = 128

EPS = 1e-6
CHUNK = 64

GLA_HEADS = 4
GLA_DK = 32
GLA_DV = 64
GLA_RANK = 16
GLA_GATE_NORM = 16.0
GLA_QK = GLA_HEADS * GLA_DK
GLA_V = GLA_HEADS * GLA_DV
GLA_COLS = 2 * GLA_QK + 2 * GLA_V + 2 * GLA_RANK

RWKV_HEADS = 4
RWKV_HD = 64
RWKV_W = RWKV_HEADS * RWKV_HD
RWKV_W_RANK = 64
RWKV_A_RANK = 64
RWKV_G_RANK = 128
RWKV_LN_EPS = 64e-5
RWKV_COLS = 3 * RWKV_W + 2 * RWKV_W_RANK + 2 * RWKV_A_RANK + RWKV_G_RANK

SSD_HEADS = 8
SSD_HD = 64
SSD_INNER = SSD_HEADS * SSD_HD
SSD_STATE = 128
SSD_GROUPS = 2
SSD_CONV = 5
SSD_CONV_CH = SSD_INNER + 2 * SSD_GROUPS * SSD_STATE
SSD_COLS = SSD_INNER + SSD_CONV_CH + 2 * SSD_HEADS

D_MIX = GLA_V + RWKV_W + SSD_INNER
D_IN = GLA_COLS + RWKV_COLS + SSD_COLS
D_FF = -(-8 * D_MODEL // (3 * 256)) * 256

kernel_name = 'hybrid_gla_rwkv7_ssd_bidir_encoder'


def _split(t, sizes):
    idx = np.cumsum(sizes)[:-1].tolist()
    return jnp.split(t, idx, axis=-1)


def _rmsnorm(x, g):
    x32 = x.astype(jnp.float32)
    y = x32 * lax.rsqrt(jnp.mean(x32 * x32, axis=-1, keepdims=True) + EPS)
    return (y * g.astype(jnp.float32)).astype(x.dtype)


def _head_group_norm(y, g, b):
    y32 = y.astype(jnp.float32)
    mean = jnp.mean(y32, axis=-1, keepdims=True)
    var = jnp.mean(jnp.square(y32 - mean), axis=-1, keepdims=True)
    out = (y32 - mean) * lax.rsqrt(var + RWKV_LN_EPS) * g.astype(jnp.float32) + b.astype(jnp.float32)
    return out.astype(y.dtype)


def _l2_normalize(x):
    x32 = x.astype(jnp.float32)
    return (x32 * lax.rsqrt(jnp.sum(x32 * x32, axis=-1, keepdims=True) + 1e-12)).astype(x.dtype)


def _centred_shift(p):
    prev = jnp.pad(p, ((0, 0), (1, 0), (0, 0)))[:, :-1]
    nxt = jnp.pad(p, ((0, 0), (0, 1), (0, 0)))[:, 1:]
    return 0.5 * (prev + nxt)


def _chunk_mask(strict):
    return jnp.tril(jnp.ones((CHUNK, CHUNK), dtype=bool), k=-1 if strict else 0)


def _gla_chunked(q, k, v, log_a, strict):
    Bsz, L, H, DK = q.shape
    DV = v.shape[-1]
    N = L // CHUNK
    q = q.reshape(Bsz, N, CHUNK, H, DK)
    k = k.reshape(Bsz, N, CHUNK, H, DK)
    v = v.reshape(Bsz, N, CHUNK, H, DV)
    b = jnp.cumsum(log_a.reshape(Bsz, N, CHUNK, H, DK), axis=2)
    b_last = b[:, :, -1:]
    q_dec = q * jnp.exp(b)
    k_dec = k * jnp.exp(-b)
    scores = jnp.einsum('bnihk,bnjhk->bnhij', q_dec, k_dec)
    scores = jnp.where(_chunk_mask(strict), scores, jnp.zeros_like(scores))
    o_intra = jnp.einsum('bnhij,bnjhv->bnihv', scores, v)
    u = jnp.einsum('bnjhk,bnjhv->bnhkv', k * jnp.exp(b_last - b), v)
    d = jnp.exp(b_last[:, :, 0])

    def step(state, inp):
        dc, uc = inp
        return (dc[..., None] * state + uc).astype(state.dtype), state

    s0 = jnp.zeros((Bsz, H, DK, DV), u.dtype)
    _, s_prev = lax.scan(step, s0, (jnp.moveaxis(d, 1, 0), jnp.moveaxis(u, 1, 0)))
    s_prev = jnp.moveaxis(s_prev, 0, 1)
    o_inter = jnp.einsum('bnihk,bnhkv->bnihv', q_dec, s_prev)
    return (o_intra + o_inter).reshape(Bsz, L, H, DV)


def _gla_mixer(p, a_up, a_bias, norm_g):
    Bsz, L, _ = p.shape
    q, k, v, g, af, ab = _split(p, [GLA_QK, GLA_QK, GLA_V, GLA_V, GLA_RANK, GLA_RANK])
    q = q.reshape(Bsz, L, GLA_HEADS, GLA_DK) * (GLA_DK ** -0.5)
    k = k.reshape(Bsz, L, GLA_HEADS, GLA_DK)
    v = v.reshape(Bsz, L, GLA_HEADS, GLA_DV)
    la_f = (jax.nn.log_sigmoid(af @ a_up[0] + a_bias[0]) / GLA_GATE_NORM).reshape(Bsz, L, GLA_HEADS, GLA_DK)
    la_b = (jax.nn.log_sigmoid(ab @ a_up[1] + a_bias[1]) / GLA_GATE_NORM).reshape(Bsz, L, GLA_HEADS, GLA_DK)
    o_f = _gla_chunked(q, k, v, la_f, False)
    o_b = jnp.flip(_gla_chunked(jnp.flip(q, 1), jnp.flip(k, 1), jnp.flip(v, 1), jnp.flip(la_b, 1), True), 1)
    o = _rmsnorm(o_f + o_b, norm_g) * jax.nn.silu(g.reshape(Bsz, L, GLA_HEADS, GLA_DV))
    return o.reshape(Bsz, L, GLA_V)


def _rwkv7_scan(r, w, k, v, a_vec, b_vec):
    def step(S, inp):
        r_t, w_t, k_t, v_t, a_t, b_t = inp
        sa = jnp.einsum('dbhvk,dbhk->dbhv', S, a_t)
        S = (S * w_t[..., None, :] + sa[..., :, None] * b_t[..., None, :] + v_t[..., :, None] * k_t[..., None, :]).astype(S.dtype)
        return S, jnp.einsum('dbhvk,dbhk->dbhv', S, r_t)

    xs = tuple(jnp.moveaxis(t, 2, 0) for t in (r, w, k, v, a_vec, b_vec))
    D2, Bsz, _, H, HD = r.shape
    s0 = jnp.zeros((D2, Bsz, H, HD, HD), v.dtype)
    _, y = lax.scan(step, s0, xs)
    return jnp.moveaxis(y, 0, 2)


def _rwkv7_mixer(p, mu, w0, w_up, a0, a_up, g_up, k_k, k_a, r_k, ln_g, ln_b):
    Bsz, L, _ = p.shape
    H, HD = RWKV_HEADS, RWKV_HD
    p = p + mu * (_centred_shift(p) - p)
    r, k, v, wf, wb, af, ab, gd = _split(p, [RWKV_W] * 3 + [RWKV_W_RANK] * 2 + [RWKV_A_RANK] * 2 + [RWKV_G_RANK])
    r = r.reshape(Bsz, L, H, HD)
    k = k.reshape(Bsz, L, H, HD)
    v = v.reshape(Bsz, L, H, HD)
    w_raw = jnp.stack([w0[0] + jnp.tanh(wf) @ w_up[0], w0[1] + jnp.tanh(wb) @ w_up[1]])
    decay = jnp.exp(-jnp.exp(-jax.nn.softplus(-w_raw) - 0.5)).reshape(2, Bsz, L, H, HD)
    a = jax.nn.sigmoid(jnp.stack([a0[0] + af @ a_up[0], a0[1] + ab @ a_up[1]])).reshape(2, Bsz, L, H, HD)
    kk = _l2_normalize(k * k_k.reshape(H, HD))
    k_dir = k[None] * (1 + (a - 1) * k_a.reshape(H, HD))
    b_vec = kk[None] * a

    def both(t_f, t_b):
        return jnp.stack([t_f, jnp.flip(t_b, axis=1)])

    y = _rwkv7_scan(both(r, r), both(decay[0], decay[1]), both(k_dir[0], k_dir[1]),
                    both(v, v), both(-kk, -kk), both(b_vec[0], b_vec[1]))
    y_f = y[0]
    y_b = jnp.flip(y[1], axis=1) - v * jnp.sum(k_dir[1] * r, axis=-1, keepdims=True)
    o = _head_group_norm(y_f + y_b, ln_g.reshape(H, HD), ln_b.reshape(H, HD))
    o = o + jnp.sum(r * k * r_k.reshape(H, HD), axis=-1, keepdims=True) * v
    g = jax.nn.sigmoid(gd) @ g_up
    return o.reshape(Bsz, L, RWKV_W) * g


def _ssd_chunked(x, dt, A, Bm, Cm, strict):
    Bsz, L, H, P = x.shape
    G, S = Bm.shape[2], Bm.shape[3]
    R = H // G
    N = L // CHUNK
    x = x.reshape(Bsz, N, CHUNK, G, R, P)
    dt = dt.reshape(Bsz, N, CHUNK, G, R)
    Bm = Bm.reshape(Bsz, N, CHUNK, G, S)
    Cm = Cm.reshape(Bsz, N, CHUNK, G, S)
    acum = jnp.cumsum(dt * A.reshape(G, R), axis=2)
    a_t = jnp.moveaxis(acum, 2, -1)
    seg = jnp.exp(jnp.where(_chunk_mask(strict), a_t[..., :, None] - a_t[..., None, :], -jnp.inf))
    cb = jnp.einsum('bnigs,bnjgs->bngij', Cm, Bm)
    xdt = x * dt[..., None]
    y_diag = jnp.einsum('bngrij,bnjgrp->bnigrp', cb[:, :, :, None] * seg, xdt)
    a_last = acum[:, :, -1]
    u = jnp.einsum('bncgs,bncgrp->bngrps', Bm, xdt * jnp.exp(a_last[:, :, None] - acum)[..., None])

    def step(state, inp):
        dc, uc = inp
        return (dc[..., None, None] * state + uc).astype(state.dtype), state

    s0 = jnp.zeros((Bsz, G, R, P, S), u.dtype)
    _, s_prev = lax.scan(step, s0, (jnp.moveaxis(jnp.exp(a_last), 1, 0), jnp.moveaxis(u, 1, 0)))
    s_prev = jnp.moveaxis(s_prev, 0, 1)
    y_off = jnp.einsum('bncgs,bngrps->bncgrp', Cm, s_prev) * jnp.exp(acum)[..., None]
    return (y_diag + y_off).reshape(Bsz, L, H, P)


def _ssd_mixer(p, conv_w, conv_b, dt_bias, A_log, D, norm_g):
    Bsz, L, _ = p.shape
    z, xbc, dtf, dtb = _split(p, [SSD_INNER, SSD_CONV_CH, SSD_HEADS, SSD_HEADS])
    xbc = lax.conv_general_dilated(xbc, conv_w[:, None, :], window_strides=(1,),
                                   padding=[(SSD_CONV // 2, SSD_CONV // 2)],
                                   dimension_numbers=('NWC', 'WIO', 'NWC'),
                                   feature_group_count=SSD_CONV_CH)
    xbc = jax.nn.silu(xbc + conv_b)
    xs, Bm, Cm = _split(xbc, [SSD_INNER, SSD_GROUPS * SSD_STATE, SSD_GROUPS * SSD_STATE])
    xs = xs.reshape(Bsz, L, SSD_HEADS, SSD_HD)
    Bm = Bm.reshape(Bsz, L, SSD_GROUPS, SSD_STATE)
    Cm = Cm.reshape(Bsz, L, SSD_GROUPS, SSD_STATE)
    dt_f = jax.nn.softplus(dtf + dt_bias[0])
    dt_b = jax.nn.softplus(dtb + dt_bias[1])
    A = -jnp.exp(A_log)
    y_f = _ssd_chunked(xs, dt_f, A[0], Bm, Cm, False)
    y_b = jnp.flip(_ssd_chunked(jnp.flip(xs, 1), jnp.flip(dt_b, 1), A[1], jnp.flip(Bm, 1), jnp.flip(Cm, 1), True), 1)
    y = y_f + y_b + D[:, None] * xs
    y = y.reshape(Bsz, L, SSD_INNER) * jax.nn.silu(z)
    return _rmsnorm(y, norm_g)


def _trunk(x, prm):
    for l in range(DEPTH):
        h = _rmsnorm(x, prm['norm_mix'][l])
        p = h @ prm['w_in'][l]
        pg, pr, ps = _split(p, [GLA_COLS, RWKV_COLS, SSD_COLS])
        o_gla = _gla_mixer(pg, prm['gla_a_up'][l], prm['gla_a_bias'][l], prm['gla_norm'][l])
        o_rwkv = _rwkv7_mixer(pr, prm['rwkv_mu'][l], prm['rwkv_w0'][l], prm['rwkv_w_up'][l],
                              prm['rwkv_a0'][l], prm['rwkv_a_up'][l], prm['rwkv_g_up'][l],
                              prm['rwkv_k_k'][l], prm['rwkv_k_a'][l], prm['rwkv_r_k'][l],
                              prm['rwkv_ln_g'][l], prm['rwkv_ln_b'][l])
        o_ssd = _ssd_mixer(ps, prm['ssd_conv_w'][l], prm['ssd_conv_b'][l], prm['ssd_dt_bias'][l],
                           prm['ssd_A_log'][l], prm['ssd_D'][l], prm['ssd_norm'][l])
        mix = jnp.concatenate([o_gla, o_rwkv, o_ssd], axis=-1)
        x = x + mix @ prm['w_out'][l]
        h = _rmsnorm(x, prm['norm_ffn'][l])
        x = x + (jax.nn.silu(h @ prm['ffn_gate'][l]) * (h @ prm['ffn_up'][l])) @ prm['ffn_down'][l]
    return _rmsnorm(x, prm['final_norm'])


def setup_inputs(seed: int = 0) -> dict:
    key = jax.random.key(seed)
    ks = iter(jax.random.split(key, 40))
    f32 = jnp.float32

    def nrm(shape, scale):
        return jax.random.normal(next(ks), shape, f32) * scale

    def gain(shape):
        return 1.0 + nrm(shape, 0.05)

    dt_init = jnp.exp(jax.random.uniform(next(ks), (DEPTH, 2, SSD_HEADS), f32, np.log(1e-3), np.log(1e-1)))
    return {
        'x_prompt': nrm((BATCH, SEQ, D_MODEL), 1.0),
        'x_sample': nrm((DEC_BATCH, DEC_SEQ, D_MODEL), 1.0),
        'norm_mix': gain((DEPTH, D_MODEL)),
        'w_in': nrm((DEPTH, D_MODEL, D_IN), D_MODEL ** -0.5),
        'w_out': nrm((DEPTH, D_MIX, D_MODEL), D_MIX ** -0.5),
        'gla_a_up': nrm((DEPTH, 2, GLA_RANK, GLA_QK), GLA_RANK ** -0.5),
        'gla_a_bias': 1.0 + nrm((DEPTH, 2, GLA_QK), 0.5),
        'gla_norm': gain((DEPTH, GLA_DV)),
        'rwkv_mu': jax.random.uniform(next(ks), (DEPTH, RWKV_COLS), f32),
        'rwkv_w0': jax.random.uniform(next(ks), (DEPTH, 2, RWKV_W), f32, -6.0, -1.0),
        'rwkv_w_up': nrm((DEPTH, 2, RWKV_W_RANK, RWKV_W), 0.5 * RWKV_W_RANK ** -0.5),
        'rwkv_a0': nrm((DEPTH, 2, RWKV_W), 0.1),
        'rwkv_a_up': nrm((DEPTH, 2, RWKV_A_RANK, RWKV_W), 0.5 * RWKV_A_RANK ** -0.5),
        'rwkv_g_up': nrm((DEPTH, RWKV_G_RANK, RWKV_W), RWKV_G_RANK ** -0.5),
        'rwkv_k_k': 0.85 + nrm((DEPTH, RWKV_W), 0.05),
        'rwkv_k_a': gain((DEPTH, RWKV_W)),
        'rwkv_r_k': nrm((DEPTH, RWKV_W), 0.1),
        'rwkv_ln_g': gain((DEPTH, RWKV_W)),
        'rwkv_ln_b': nrm((DEPTH, RWKV_W), 0.02),
        'ssd_conv_w': nrm((DEPTH, SSD_CONV, SSD_CONV_CH), SSD_CONV ** -0.5),
        'ssd_conv_b': nrm((DEPTH, SSD_CONV_CH), 0.02),
        'ssd_dt_bias': dt_init + jnp.log(-jnp.expm1(-dt_init)),
        'ssd_A_log': jnp.log(jax.random.uniform(next(ks), (DEPTH, 2, SSD_HEADS), f32, 1.0, 16.0)),
        'ssd_D': gain((DEPTH, SSD_HEADS)),
        'ssd_norm': gain((DEPTH, SSD_INNER)),
        'norm_ffn': gain((DEPTH, D_MODEL)),
        'ffn_gate': nrm((DEPTH, D_MODEL, D_FF), D_MODEL ** -0.5),
        'ffn_up': nrm((DEPTH, D_MODEL, D_FF), D_MODEL ** -0.5),
        'ffn_down': nrm((DEPTH, D_FF, D_MODEL), D_FF ** -0.5),
        'final_norm': gain((D_MODEL,)),
    }


def reference(x_prompt, x_sample, norm_mix, w_in, w_out, gla_a_up, gla_a_bias, gla_norm,
              rwkv_mu, rwkv_w0, rwkv_w_up, rwkv_a0, rwkv_a_up, rwkv_g_up, rwkv_k_k, rwkv_k_a,
              rwkv_r_k, rwkv_ln_g, rwkv_ln_b, ssd_conv_w, ssd_conv_b, ssd_dt_bias, ssd_A_log,
              ssd_D, ssd_norm, norm_ffn, ffn_gate, ffn_up, ffn_down, final_norm):
    prm = {
        'norm_mix': norm_mix, 'w_in': w_in, 'w_out': w_out,
        'gla_a_up': gla_a_up, 'gla_a_bias': gla_a_bias, 'gla_norm': gla_norm,
        'rwkv_mu': rwkv_mu, 'rwkv_w0': rwkv_w0, 'rwkv_w_up': rwkv_w_up, 'rwkv_a0': rwkv_a0,
        'rwkv_a_up': rwkv_a_up, 'rwkv_g_up': rwkv_g_up, 'rwkv_k_k': rwkv_k_k, 'rwkv_k_a': rwkv_k_a,
        'rwkv_r_k': rwkv_r_k, 'rwkv_ln_g': rwkv_ln_g, 'rwkv_ln_b': rwkv_ln_b,
        'ssd_conv_w': ssd_conv_w, 'ssd_conv_b': ssd_conv_b, 'ssd_dt_bias': ssd_dt_bias,
        'ssd_A_log': ssd_A_log, 'ssd_D': ssd_D, 'ssd_norm': ssd_norm,
        'norm_ffn': norm_ffn, 'ffn_gate': ffn_gate, 'ffn_up': ffn_up, 'ffn_down': ffn_down,
        'final_norm': final_norm,
    }
    y_prompt = _trunk(x_prompt, prm)
    y_sample = _trunk(x_sample, prm)
    return (y_prompt, y_sample)
```

```python
import numpy as np
import concourse.bass as bass
import concourse.mybir as mybir
from concourse.bass_utils import run_bass_kernel_spmd
from contextlib import ExitStack

F32 = mybir.dt.float32
BF16 = mybir.dt.bfloat16
ALU = mybir.AluOpType
AF = mybir.ActivationFunctionType
AX = mybir.AxisListType

ENGS = ("tensor", "vector", "scalar", "gpsimd", "sync")
DMA_RING = 6
import os
DBG = os.environ.get("KDBG", "")
DBG2 = os.environ.get("KDBG2", "")
DBG3 = os.environ.get("KDBG3", "")
C = 128
D = 1024
DFF = 2816
NFC = DFF // 128
EPS = 1e-6


def _a(*args, **kw):
    return (args, kw)


def _call(fn, e):
    if isinstance(fn, tuple):
        return getattr(e, fn[0])(*fn[1][0], **fn[1][1])
    return fn(e)


class Op:
    __slots__ = ("eng", "fn", "reads", "writes", "dma", "idx", "deps", "sig", "ring", "ringn")


class Prog:
    def __init__(self, nc):
        self.nc = nc
        self.ops = []
        self.stack = ExitStack()
        self.last_w = {}
        self.readers = {}
        self.ndma = {e: 0 for e in ENGS}
        self.last_eng = {}
        self.ring_last = {}
        self.bar_deps = []
        self.bar_seen = {e: True for e in ENGS}

    def sb(self, name, shape, dt, stack=None):
        return (stack or self.stack).enter_context(self.nc.sbuf_tensor(name, list(shape), dt))

    def ps(self, name, shape, dt):
        return self.stack.enter_context(self.nc.psum_tensor(name, list(shape), dt))

    def barrier(self):
        deps = [v for v in self.last_eng.values()] + [v for v in self.ring_last.values()]
        self.bar_deps = sorted(set(deps))
        self.bar_seen = {e: False for e in ENGS}
        self.last_w = {}
        self.readers = {}

    def op(self, eng, fn, reads=(), writes=(), dma=False):
        o = Op()
        o.eng, o.fn, o.reads, o.writes, o.dma = eng, fn, tuple(reads), tuple(writes), dma
        o.sig = o.ring = o.ringn = None
        o.idx = len(self.ops)
        deps = set()
        if not self.bar_seen[eng]:
            deps.update(self.bar_deps)
            self.bar_seen[eng] = True
        for k in o.reads:
            w = self.last_w.get(k)
            if w is not None:
                deps.add(w)
        for k in o.writes:
            w = self.last_w.get(k)
            if w is not None:
                deps.add(w)
            deps.update(self.readers.get(k, ()))
        for k in o.reads:
            self.readers.setdefault(k, []).append(o.idx)
        for k in o.writes:
            self.last_w[k] = o.idx
            self.readers[k] = []
        deps.discard(o.idx)
        o.deps = sorted(deps)
        if dma:
            n = self.ndma[eng]
            o.ring = n % DMA_RING
            o.ringn = n // DMA_RING + 1
            self.ndma[eng] = n + 1
            self.ring_last[(eng, o.ring)] = o.idx
        else:
            self.last_eng[eng] = o.idx
        self.ops.append(o)
        return o

    def dma(self, eng, out, in_, reads=(), writes=()):
        return self.op(eng, lambda e: e.dma_start(out=out, in_=in_), reads, writes, dma=True)

    def emit(self):
        nc, ops = self.nc, self.ops
        needed = set()
        for o in ops:
            for d in o.deps:
                p = ops[d]
                if p.dma:
                    continue
                if p.eng == "tensor" and o.eng == "tensor" and not o.dma:
                    continue
                needed.add(d)
        cnt = {e: 0 for e in ENGS}
        for o in ops:
            if (not o.dma) and o.idx in needed:
                cnt[o.eng] += 1
                o.sig = cnt[o.eng]
        per = {e: [o for o in ops if o.eng == e] for e in ENGS}
        st = self.stack
        csem = {e: st.enter_context(nc.semaphore("c_" + e)) for e in ("tensor", "vector", "scalar", "gpsimd")}
        dsem = {}
        for e in ENGS:
            if self.ndma[e] > 0:
                dsem[e] = [st.enter_context(nc.semaphore("d_%s_%d" % (e, i))) for i in range(DMA_RING)]
        block = st.enter_context(nc.Block())
        ndma = self.ndma

        def run(ename, eobj):
            waited = {}

            def wait(sem, val):
                key = id(sem)
                if waited.get(key, 0) >= val:
                    return
                waited[key] = val
                eobj.wait_ge(sem, val)

            for o in per[ename]:
                for d in o.deps:
                    p = ops[d]
                    if p.dma:
                        wait(dsem[p.eng][p.ring], 16 * p.ringn)
                    else:
                        if p.eng == "tensor" and ename == "tensor" and not o.dma:
                            continue
                        wait(csem[p.eng], p.sig)
                if o.dma:
                    if o.ringn > 1:
                        wait(dsem[ename][o.ring], 16 * (o.ringn - 1))
                    _call(o.fn, eobj).then_inc(dsem[ename][o.ring], 16)
                else:
                    ins = _call(o.fn, eobj)
                    if o.sig is not None:
                        ins.then_inc(csem[ename], 1)
            if ename == "sync":
                for qe, sems in dsem.items():
                    n = ndma[qe]
                    for r in range(DMA_RING):
                        if n > r:
                            wait(sems[r], 16 * ((n - r + DMA_RING - 1) // DMA_RING))

        for en in ("tensor", "vector", "scalar", "gpsimd", "sync"):
            if per[en] or en == "sync":
                getattr(block, en)(lambda e, en=en: run(en, e))


G0, R0, S0 = 0, 800, 1952


def fm_blocks(d):
    bl = [(S0 + 512 + 128 * b, 128) for b in range(8)]
    bl += [(R0 + 128 * b, 128) for b in range(6)]
    bl += [(R0 + 768 + 64 * d, 64), (R0 + 896 + 64 * d, 64)]
    if d == 1:
        bl += [(R0 + 1024, 128)]
    bl += [(G0, 128), (G0 + 128, 128), (G0 + 768 + 16 * d, 16)]
    return bl


def tm_cols(d):
    cols = [(G0 + 256, 256)]
    if d == 1:
        cols += [(G0 + 512, 256), (S0, 512)]
    cols += [(S0 + 1536 + 8 * d, 8)]
    return cols


def col_index(blocks):
    return np.concatenate([np.arange(s, s + w) for s, w in blocks])


def build_program(NSLOT, SLOT, DEPTH=2):
    NT = NSLOT * SLOT
    NCH = NT // C
    CPS = SLOT // C
    nc = bass.Bass("TRN2", target_bir_lowering=False)
    P = Prog(nc)

    def din(name, shape):
        return nc.dram_tensor(name, list(shape), F32, kind="ExternalInput").ap()

    x_in = din("x_in", [NT, D])
    y_out = nc.dram_tensor("y_out", [NT, D], F32, kind="ExternalOutput").ap()
    mixd = nc.dram_tensor("mixd", [NT, D], BF16, kind="Internal").ap()
    xnext = nc.dram_tensor("xnext", [NT, D], F32, kind="Internal").ap()
    ofd = nc.dram_tensor("ofd", [NT, D], F32, kind="Internal").ap()
    flag_d = din("flag", [128, 1])
    NFM = [sum(w for _, w in fm_blocks(d)) for d in (0, 1)]
    NTM = [sum(w for _, w in tm_cols(d)) for d in (0, 1)]
    W = {}
    for l in range(DEPTH):
        for d in (0, 1):
            W["wfm", l, d] = din("wfm_%d_%d" % (l, d), [D, NFM[d]])
            W["wtm", l, d] = din("wtm_%d_%d" % (l, d), [D, NTM[d]])
            W["gla_aup", l, d] = din("gla_aup_%d_%d" % (l, d), [16, 128])
            W["gla_ab", l, d] = din("gla_ab_%d_%d" % (l, d), [1, 128])
            W["mu", l, d] = din("mu_%d_%d" % (l, d), [128, 9])
            W["w0", l, d] = din("w0_%d_%d" % (l, d), [1, 256])
            W["wup", l, d] = din("wup_%d_%d" % (l, d), [64, 256])
            W["a0", l, d] = din("a0_%d_%d" % (l, d), [128, 2])
            W["aup", l, d] = din("aup_%d_%d" % (l, d), [64, 256])
            W["dtb", l, d] = din("dtb_%d_%d" % (l, d), [128, 8])
            W["alog", l, d] = din("alog_%d_%d" % (l, d), [128, 8])
        W["wout", l] = din("wout_%d" % l, [D, D])
        W["wg", l] = din("wg_%d" % l, [D, DFF])
        W["wu", l] = din("wu_%d" % l, [D, DFF])
        W["wd", l] = din("wd_%d" % l, [DFF, D])
        W["gmix", l] = din("gmix_%d" % l, [128, 8])
        W["gffn", l] = din("gffn_%d" % l, [128, 8])
        W["gla_norm", l] = din("gla_norm_%d" % l, [128, 64])
        W["gup", l] = din("gup_%d" % l, [128, 256])
        W["kk3", l] = din("kk3_%d" % l, [128, 6])
        W["lng", l] = din("lng_%d" % l, [128, 256])
        W["lnb", l] = din("lnb_%d" % l, [128, 256])
        W["convw", l] = din("convw_%d" % l, [128, 40])
        W["convb", l] = din("convb_%d" % l, [128, 8])
        W["ssdD", l] = din("ssdD_%d" % l, [128, 8])
        W["ssdn", l] = din("ssdn_%d" % l, [128, 512])
    W["gfin"] = din("gfin", [128, D])

    V = lambda fn, r=(), w=(): P.op("vector", fn, r, w)
    A = lambda fn, r=(), w=(): P.op("scalar", fn, r, w)
    G = lambda fn, r=(), w=(): P.op("gpsimd", fn, r, w)
    T = lambda fn, r=(), w=(): P.op("tensor", fn, r, w)

    def mm(out, lhsT, rhs, start, stop, r, w):
        T(lambda e: e.matmul(out, lhsT=lhsT, rhs=rhs, start=start, stop=stop), r, w)

    banks = [P.ps("bank%d" % i, [128, 512], F32) for i in range(8)]
    bstate = {"i": 0}

    def bank():
        i = bstate["i"]
        bstate["i"] = (i + 1) % 7
        return banks[i], "bank%d" % i

    def bf(b):
        return b[:].bitcast(BF16)

    def asel(out, in_, pattern, op, fill, base, cm, key):
        G(("affine_select", _a(out=out, in_=in_, pattern=pattern, compare_op=op, fill=fill, base=base,
                                    channel_multiplier=cm)), [key], [key])

    def mk_consts(sbf, full):
        c = {}
        ident32 = sbf("ident32", [128, 128], F32)
        J32 = sbf("J32", [128, 128], F32)
        identb = sbf("identb", [128, 128], BF16)
        Jb = sbf("Jb", [128, 128], BF16)
        G(("memset", _a(ident32[:], 1.0)), [], ["ident32"])
        asel(ident32[:], ident32[:], [[-1, 128]], ALU.is_equal, 0.0, 0, 1, "ident32")
        G(("memset", _a(J32[:], 1.0)), [], ["J32"])
        asel(J32[:], J32[:], [[1, 128]], ALU.is_equal, 0.0, -127, 1, "J32")
        V(("tensor_copy", _a(out=identb[:], in_=ident32[:])), ["ident32"], ["identb"])
        V(("tensor_copy", _a(out=Jb[:], in_=J32[:])), ["J32"], ["Jb"])
        c.update(ident32=ident32, J32=J32, identb=identb, Jb=Jb)
        if not full:
            return c
        tri2f = sbf("tri2", [128, 256], F32)
        tri2 = tri2f[:].rearrange("p (x t) -> p x t", x=2)
        ones32 = sbf("ones32", [128, 128], F32)
        negmf = [sbf("negm%d" % d, [128, 512], BF16) for d in (0, 1)]
        negm = [t_[:].rearrange("p (h t) -> p h t", h=4) for t_ in negmf]
        qmaskf = sbf("qmask", [128, 512], BF16)
        qmask = qmaskf[:].rearrange("p (h t) -> p h t", h=4)
        bm4 = sbf("bm4", [128, 256], F32)
        bm2 = sbf("bm2", [128, 128], F32)
        bm2b = sbf("bm2b", [128, 128], BF16)
        hsel = sbf("hsel", [128, 2], BF16)
        m2 = sbf("m2", [128, 2], F32)
        flag = sbf("flagsb", [128, 1], F32)
        G(("memset", _a(ones32[:], 1.0)), [], ["ones32"])
        G(("memset", _a(tri2f[:], 1.0)), [], ["tri2"])
        asel(tri2[:, 0, :], tri2[:, 0, :], [[1, 128]], ALU.is_ge, 0.0, 0, -1, "tri2")
        asel(tri2[:, 1, :], tri2[:, 1, :], [[1, 128]], ALU.is_gt, 0.0, 0, -1, "tri2")
        for d in (0, 1):
            G(("memset", _a(negmf[d][:], 0.0)), [], ["negm%d" % d])
            for h in range(4):
                asel(negm[d][:, h, :], negm[d][:, h, :], [[1, 128]], ALU.is_ge if d == 0 else ALU.is_gt, -30000.0, 0, -1,
                     "negm%d" % d)
        G(("memset", _a(qmaskf[:], 1.0)), [], ["qmask"])
        G(("memset", _a(bm4[:], 1.0)), [], ["bm4"])
        for h in range(4):
            asel(qmask[:, h, :], qmask[:, h, :], [[0, 128]], ALU.is_ge, 0.0, -32 * h, 1, "qmask")
            asel(qmask[:, h, :], qmask[:, h, :], [[0, 128]], ALU.is_ge, 0.0, 32 * h + 31, -1, "qmask")
            asel(bm4[:, 64 * h:64 * h + 64], bm4[:, 64 * h:64 * h + 64], [[0, 64]], ALU.is_ge, 0.0, -32 * h, 1, "bm4")
            asel(bm4[:, 64 * h:64 * h + 64], bm4[:, 64 * h:64 * h + 64], [[0, 64]], ALU.is_ge, 0.0, 32 * h + 31, -1, "bm4")
        G(("memset", _a(bm2[:], 1.0)), [], ["bm2"])
        asel(bm2[:, 0:64], bm2[:, 0:64], [[0, 64]], ALU.is_ge, 0.0, 63, -1, "bm2")
        asel(bm2[:, 64:128], bm2[:, 64:128], [[0, 64]], ALU.is_ge, 0.0, -64, 1, "bm2")
        V(("tensor_copy", _a(out=bm2b[:], in_=bm2[:])), ["bm2"], ["bm2b"])
        G(("memset", _a(m2[:], 1.0)), [], ["m2"])
        asel(m2[:, 0:1], m2[:, 0:1], [[0, 1]], ALU.is_ge, 0.0, 63, -1, "m2")
        asel(m2[:, 1:2], m2[:, 1:2], [[0, 1]], ALU.is_ge, 0.0, -64, 1, "m2")
        V(("tensor_copy", _a(out=hsel[:], in_=m2[:])), ["m2"], ["hsel"])
        P.dma("sync", flag[:], flag_d[:, :], [], ["flag"])
        mskB = sbf("mskB", [128, 2, 2, 128], BF16)
        mskK = [sbf("mskK%d" % d, [128, 2, 2, 128], BF16) for d in (0, 1)]
        for hh in range(2):
            V(("tensor_copy", _a(out=mskB[:, hh, 0, :], in_=tri2[:, 1, :])), ["tri2"], ["mskB"])
            V(("tensor_copy", _a(out=mskB[:, hh, 1, :], in_=tri2[:, 0, :])), ["tri2"], ["mskB"])
            for d in (0, 1):
                V(("tensor_copy", _a(out=mskK[d][:, hh, 0, :], in_=tri2[:, 1, :])), ["tri2"], ["mskK%d" % d])
                V(("tensor_copy", _a(out=mskK[d][:, hh, 1, :], in_=tri2[:, d, :])), ["tri2"], ["mskK%d" % d])
        c.update(tri2f=tri2f, tri2=tri2, ones32=ones32, negmf=negmf, negm=negm, qmaskf=qmaskf, qmask=qmask, bm4=bm4,
                 bm2=bm2, bm2b=bm2b, hsel=hsel, m2=m2, flag=flag, mskB=mskB, mskK=mskK)
        return c

    ld = {"i": 0, "stg": None, "w": 1024}

    def set_stg(sbf, width):
        ld["stg"] = [sbf("stg%d" % i, [128, width], F32) for i in range(2)]
        ld["w"] = width

    def load_cast(dst_ap, src_ap, ncols, dkey, np_=128):
        i = ld["i"]
        ld["i"] += 1
        s = ld["stg"][i % 2]
        sk = "stg%d" % (i % 2)
        P.dma("sync" if i % 2 == 0 else "gpsimd", s[0:np_, 0:ncols], src_ap, [], [sk])
        eng = ("vector", "gpsimd", "scalar")[i % 3]
        if eng == "scalar":
            A(("copy", _a(out=dst_ap, in_=s[0:np_, 0:ncols])), [sk], [dkey])
        else:
            P.op(eng, ("tensor_copy", _a(out=dst_ap, in_=s[0:np_, 0:ncols])), [sk], [dkey])

    def load_w(dst, dkey, src, K, N):
        wdt = ld["w"]
        for kc in range(K // 128):
            for n0 in range(0, N, wdt):
                n1 = min(N, n0 + wdt)
                load_cast(dst[:, kc, n0:n1], src[kc * 128:(kc + 1) * 128, n0:n1], n1 - n0, dkey)

    for l in range(DEPTH):
        xsrc = x_in if l == 0 else xnext
        for d in ((0, 0) if DBG == "FF" else (0, 1)):
            if DBG == "Bonly" and d == 0:
                continue
            P.barrier()
            with ExitStack() as ss:
                sb = lambda name, shape, dt: P.sb("L%dD%d_%s_%d" % (l, d, name, len(P.ops)), shape, dt, ss)
                cst = mk_consts(sb, True)
                ident32, J32, identb, Jb = cst["ident32"], cst["J32"], cst["identb"], cst["Jb"]
                tri2f, tri2, ones32, negmf, negm = cst["tri2f"], cst["tri2"], cst["ones32"], cst["negmf"], cst["negm"]
                qmaskf, qmask, bm4, bm2, bm2b = cst["qmaskf"], cst["qmask"], cst["bm4"], cst["bm2"], cst["bm2b"]
                hsel, m2, flag, mskB, mskK = cst["hsel"], cst["m2"], cst["flag"], cst["mskB"], cst["mskK"]
                set_stg(sb, 1024)
                NB = 17 if d == 1 else 16
                fmb = fm_blocks(d)
                fmoff = np.concatenate([[0], np.cumsum([w for _, w in fmb])]).tolist()
                tmc = tm_cols(d)
                ntm = NTM[d]
                wfm = sb("wfm", [128, 8, NFM[d]], BF16)
                wtm = sb("wtm", [128, 8, ntm], BF16)
                load_w(wfm, "wfm", W["wfm", l, d], D, NFM[d])
                load_w(wtm, "wtm", W["wtm", l, d], D, ntm)
                gmix = sb("gmix", [128, 8], F32)
                P.dma("sync", gmix[:], W["gmix", l][:, :], [], ["gmix"])
                aupg = sb("aupg", [16, 128], BF16)
                load_cast(aupg[:], W["gla_aup", l, d][:, :], 128, "aupg", 16)
                abg = sb("abg", [1, 128], F32)
                P.dma("sync", abg[:], W["gla_ab", l, d][:, :], [], ["abg"])
                mu = sb("mu", [128, 9], F32)
                P.dma("sync", mu[:], W["mu", l, d][:, :], [], ["mu"])
                w0r = sb("w0r", [1, 256], F32)
                P.dma("sync", w0r[:], W["w0", l, d][:, :], [], ["w0r"])
                wup = sb("wup", [64, 256], BF16)
                load_cast(wup[:], W["wup", l, d][:, :], 256, "wup", 64)
                a0 = sb("a0", [128, 2], F32)
                P.dma("sync", a0[:], W["a0", l, d][:, :], [], ["a0"])
                aup = sb("aup", [64, 256], BF16)
                load_cast(aup[:], W["aup", l, d][:, :], 256, "aup", 64)
                dtb = sb("dtb", [128, 8], F32)
                P.dma("sync", dtb[:], W["dtb", l, d][:, :], [], ["dtb"])
                alog = sb("alog", [128, 8], F32)
                P.dma("sync", alog[:], W["alog", l, d][:, :], [], ["alog"])
                expA = sb("expA", [128, 8], F32)
                A(("activation", _a(out=expA[:], in_=alog[:], func=AF.Exp)), ["alog"], ["expA"])
                kk3 = sb("kk3", [128, 3, 2], F32)
                P.dma("sync", kk3[:], W["kk3", l].rearrange("p (a b) -> p a b", a=3), [], ["kk3"])
                omka = sb("omka", [128, 2], F32)
                V(("tensor_scalar", _a(out=omka[:], in0=kk3[:, 1, :], scalar1=-1.0, scalar2=1.0, op0=ALU.mult,
                                            op1=ALU.add)), ["kk3"], ["omka"])
                convw = sb("convw", [128, 8, 5], F32)
                P.dma("sync", convw[:], W["convw", l].rearrange("p (b j) -> p b j", b=8), [], ["convw"])
                convb = sb("convb", [128, 8], F32)
                P.dma("sync", convb[:], W["convb", l][:, :], [], ["convb"])
                dconv = sb("dconv", [128, 8, 5, 128], BF16)
                for b in range(8):
                    for j in range(5):
                        jj = j if d == 0 else 4 - j
                        V(("tensor_scalar", _a(out=dconv[:, b, j, :], in0=ident32[:],
                                                                      scalar1=convw[:, b, jj:jj + 1], scalar2=None,
                                                                      op0=ALU.mult)), ["ident32", "convw"], ["dconv"])
                omu = sb("omu", [128, 9], F32)
                hmu = sb("hmu", [128, 9], F32)
                V(("tensor_scalar", _a(out=omu[:], in0=mu[:], scalar1=-1.0, scalar2=1.0, op0=ALU.mult, op1=ALU.add)),
                  ["mu"], ["omu"])
                V(("tensor_scalar", _a(out=hmu[:], in0=mu[:], scalar1=0.5, scalar2=None, op0=ALU.mult)), ["mu"], ["hmu"])
                dsh = sb("dsh", [128, 9, 2, 128], BF16)
                for b in range(9):
                    V(("tensor_scalar", _a(out=dsh[:, b, 0, :], in0=ident32[:], scalar1=omu[:, b:b + 1],
                                                     scalar2=None, op0=ALU.mult)), ["ident32", "omu"], ["dsh"])
                    V(("tensor_scalar", _a(out=dsh[:, b, 1, :], in0=ident32[:], scalar1=hmu[:, b:b + 1],
                                                     scalar2=None, op0=ALU.mult)), ["ident32", "hmu"], ["dsh"])
                if d == 1:
                    glan = sb("glan", [128, 64], F32)
                    P.dma("sync", glan[:], W["gla_norm", l][:, :], [], ["glan"])
                    gup = sb("gup", [128, 256], BF16)
                    load_cast(gup[:], W["gup", l][:, :], 256, "gup")
                    lng = sb("lng", [128, 256], F32)
                    P.dma("sync", lng[:], W["lng", l][:, :], [], ["lng"])
                    lnb = sb("lnb", [128, 256], F32)
                    P.dma("sync", lnb[:], W["lnb", l][:, :], [], ["lnb"])
                    ssdD = sb("ssdD", [128, 8], F32)
                    P.dma("sync", ssdD[:], W["ssdD", l][:, :], [], ["ssdD"])
                    ssdn = sb("ssdn", [128, 512], F32)
                    P.dma("sync", ssdn[:], W["ssdn", l][:, :], [], ["ssdn"])
                xt = [sb("xt%d" % i, [128, D], F32) for i in range(2)]
                raw = [sb("raw%d" % i, [128, NB, 132], BF16) for i in range(2)]
                glaT = [sb("glaT%d" % i, [128, 3, 128], BF16) for i in range(2)]
                vtok = [sb("vtok%d" % i, [128, 256], BF16) for i in range(2)]
                dtt = [sb("dtt%d" % i, [128, 8], F32) for i in range(2)]
                if d == 1:
                    sgz = [sb("sgz%d" % i, [128, 768], F32) for i in range(2)]
                hn = sb("hn", [128, D], BF16)
                hT = sb("hT", [128, 8, 128], BF16)
                st8 = sb("st8", [128, 8], F32)
                dtmp = sb("dtmp", [128, 8], F32)
                S32 = sb("S32", [128, 256], F32)
                S16 = sb("S16", [128, 256], BF16)
                H32 = sb("H32", [128, 512], F32)
                H16 = sb("H16", [128, 512], BF16)
                R32 = sb("R32", [128, 2, 128], F32)
                R16 = sb("R16", [128, 2, 128], BF16)
                for t_, k_ in ((S32, "S32"), (S16, "S16"), (H32, "H32"), (H16, "H16"), (R32, "R32"), (R16, "R16")):
                    G(("memset", _a(t_[:], 0.0)), [], [k_])
                for i in range(2):
                    G(("memset", _a(raw[i][:], 0.0)), [], ["raw%d" % i])
                xbcT = sb("xbcT", [128, 8, 128], BF16)
                xsB = sb("xsB", [128, 768], BF16)
                dtA = sb("dtA", [128, 8], F32)
                ac = sb("ac", [128, 5, 8], F32)
                Ztf = sb("Zt", [128, 1024], F32)
                Zt = Ztf[:].rearrange("p (h t) -> p h t", h=8)
                seg = sb("seg", [128, 8, 128], BF16)
                cb = sb("cb", [128, 2, 128], BF16)
                MT = sb("MT", [128, 8, 128], BF16)
                xdt = sb("xdt", [128, 512], BF16)
                xdtw = sb("xdtw", [128, 512], BF16)
                ytmp = sb("ytmp", [128, 512], F32)
                ysum = sb("ysum", [128, 512], F32)
                sg = sb("sg", [128, 128], F32)
                nl = sb("nl", [128, 128], F32)
                epos = sb("epos", [128, 128], F32)
                eneg = sb("eneg", [128, 128], F32)
                qd = sb("qd", [128, 128], BF16)
                kd = sb("kd", [128, 128], BF16)
                kt = sb("kt", [128, 128], BF16)
                Qblkf = sb("Qblk", [128, 512], BF16)
                Qblk = Qblkf[:].rearrange("p (h t) -> p h t", h=4)
                Sm = sb("Sm", [128, 4, 128], BF16)
                kttok = sb("kttok", [128, 128], BF16)
                utmp = sb("utmp", [128, 256], F32)
                rkv = sb("rkv", [128, 6, 128], F32)
                twT = sb("twT", [64, 128], BF16)
                afT = sb("afT", [64, 128], BF16)
                sgd = sb("sgd", [128, 128], BF16)
                nlw = sb("nlw", [128, 256], F32)
                gin = sb("gin", [128, 2, 128], F32)
                gex = sb("gex", [128, 2, 128], F32)
                ginv = sb("ginv", [128, 2, 128], F32)
                alpha = sb("alpha", [128, 2, 128], F32)
                kkk = sb("kkk", [128, 2, 128], F32)
                sq = sb("sq", [128, 2, 128], BF16)
                rinv = sb("rinv", [128, 2, 128], F32)
                kkn = sb("kkn", [128, 2, 128], F32)
                t1 = sb("t1", [128, 2, 128], F32)
                kdir = sb("kdir", [128, 2, 128], F32)
                bvec = sb("bvec", [128, 2, 128], F32)
                RtT = sb("RtT", [128, 2, 128], BF16)
                AtT = sb("AtT", [128, 2, 128], BF16)
                BtT = sb("BtT", [128, 2, 128], BF16)
                KtT = sb("KtT", [128, 2, 128], BF16)
                fmT = sb("fmT", [128, 4, 2, 128], BF16)
                tok4 = sb("tok4", [128, 4, 256], BF16)
                RAbf = sb("RAb", [128, 2, 512], BF16)
                RAb = RAbf[:].rearrange("p b (h x t) -> p b h x t", h=2, x=2)
                SCB = sb("SCB", [128, 2, 2, 2, 128], BF16)
                SCK = sb("SCK", [128, 2, 2, 2, 128], BF16)
                Ak = [sb("Ak%d" % i, [128, 4, 128], BF16) for i in range(2)]
                Nk = [sb("Nk%d" % i, [128, 4, 128], BF16) for i in range(2)]
                T32 = sb("T32", [128, 4, 128], F32)
                T16f = sb("T16", [128, 512], BF16)
                T16 = T16f[:].rearrange("p (h t) -> p h t", h=4)
                WtT = sb("WtT", [128, 2, 128], BF16)
                X2 = sb("X2", [128, 256], BF16)
                U16 = sb("U16", [128, 256], BF16)
                rtmp = sb("rtmp", [128, 2, 128], F32)
                if d == 0:
                    osb = sb("osb", [128, 512], F32)
                if d == 1:
                    ofs = sb("ofs", [128, D], F32)
                    mix = sb("mix", [128, D], BF16)
                    fa = sb("fa", [128, 512], F32)
                    fb = sb("fb", [128, 512], F32)
                    st4 = sb("st4", [128, 16], F32)
                    prodT = sb("prodT", [128, 2, 128], BF16)

                identX = identb if (d == 0 or DBG3 == "noJ") else Jb
                order = list(range(NCH)) if d == 0 else list(range(NCH - 1, -1, -1))

                def stageA(i):
                    ci = order[i]
                    s3, s2 = i % 2, i % 2
                    xk, rk_, gk, vk, dk = "xt%d" % s3, "raw%d" % s3, "glaT%d" % s2, "vtok%d" % s2, "dtt%d" % s2
                    P.dma("sync", xt[s3][:], xsrc[ci * C:(ci + 1) * C, :], [], [xk])
                    A(("activation", _a(out=hn[:], in_=xt[s3][:], func=AF.Square, accum_out=st8[:, 0:1])),
                      [xk], ["hn", "st8a"])
                    A(("activation", _a(out=st8[:, 1:2], in_=st8[:, 0:1], func=AF.Sqrt, scale=1.0 / D, bias=EPS)),
                      ["st8a"], ["st8b"])
                    V(("reciprocal", _a(out=st8[:, 2:3], in_=st8[:, 1:2])), ["st8b"], ["st8c"])
                    V(("tensor_scalar", _a(out=hn[:], in0=xt[s3][:], scalar1=st8[:, 2:3], scalar2=None, op0=ALU.mult)),
                      [xk, "st8c"], ["hn"])
                    b_, bk = bank()
                    for kc in range(8):
                        T(("transpose", _a(bf(b_)[:, kc * 128:(kc + 1) * 128], hn[:, kc * 128:(kc + 1) * 128],
                                                       identX[:])), ["hn", "identb", "Jb"], [bk])
                    V(("tensor_tensor", _a(out=hT[:], in0=bf(b_).rearrange("p (a b) -> p a b", a=8),
                                                in1=gmix[:].unsqueeze(2).to_broadcast([128, 8, 128]), op=ALU.mult)),
                      [bk, "gmix"], ["hT"])
                    nblk = len(fmb)
                    if DBG3 == "noFM4":
                        nblk = 16
                    if DBG3 == "noFM":
                        nblk = 0
                    fgroups = [list(range(g0, min(NB, g0 + 4))) for g0 in range(0, NB, 4)] + [list(range(NB, len(fmb)))]
                    if nblk < len(fmb):
                        fgroups = [g_ for g_ in fgroups if g_ and g_[-1] < nblk]
                    for grp in fgroups:
                        b_, bk = bank()
                        for j, bi in enumerate(grp):
                            wdt = fmb[bi][1]
                            for kc in range(8):
                                mm(b_[0:wdt, j * 128:(j + 1) * 128], wfm[:, kc, fmoff[bi]:fmoff[bi] + wdt], hT[:, kc, :],
                                   kc == 0, kc == 7, ["wfm", "hT"], [bk])
                        for j, bi in enumerate(grp):
                            wdt = fmb[bi][1]
                            if bi < NB:
                                A(("copy", _a(out=raw[s3][0:wdt, bi, 2:130],
                                                                        in_=b_[0:wdt, j * 128:(j + 1) * 128])), [bk], [rk_])
                            else:
                                V(("tensor_copy", _a(out=glaT[s2][0:wdt, bi - NB, :],
                                                                                in_=b_[0:wdt, j * 128:(j + 1) * 128])),
                                  [bk], [gk])
                    off = 0
                    groups = [[(0, 256), (256 if d == 0 else 1024, 8)]] if d == 0 else \
                        [[(0, 256), (256, 256)], [(512, 512)], [(1024, 8)]]
                    if DBG3 == "noTM":
                        groups = groups[:1]
                    if DBG3 == "noTM0":
                        groups = []
                    for grp in groups:
                        b_, bk = bank()
                        o = 0
                        for (c0, wdt) in grp:
                            for kc in range(8):
                                mm(b_[:, o:o + wdt], hT[:, kc, :], wtm[:, kc, c0:c0 + wdt], kc == 0, kc == 7, ["hT", "wtm"], [bk])
                            if c0 == 0:
                                V(("tensor_copy", _a(out=vtok[s2][:], in_=b_[:, o:o + 256])), [bk], [vk])
                            elif wdt == 8:
                                V(("tensor_tensor", _a(out=dtmp[:], in0=b_[:, o:o + 8], in1=dtb[:], op=ALU.add)),
                                  [bk, "dtb"], ["dtmp"])
                                A(("activation", _a(out=dtmp[:], in_=dtmp[:], func=AF.Exp)), ["dtmp"], ["dtmp"])
                                A(("activation", _a(out=dtt[s2][:], in_=dtmp[:], func=AF.Ln, bias=1.0)), ["dtmp"], [dk])
                            elif c0 == 256:
                                A(("activation", _a(out=sgz[s2][:, 0:256], in_=b_[:, o:o + 256], func=AF.Silu)),
                                  [bk], ["sgz%d" % s2])
                            else:
                                A(("activation", _a(out=sgz[s2][:, 256:768], in_=b_[:, o:o + 512], func=AF.Silu)),
                                  [bk], ["sgz%d" % s2])
                            o += wdt
                    if i > 0:
                        p3 = (i - 1) % 2
                        pk = "raw%d" % p3
                        if i % CPS == 0:
                            G(("tensor_scalar", _a(out=raw[s3][:, :, 0:2], in0=raw[p3][:, :, 128:130], scalar1=flag[:, 0:1],
                                                        scalar2=None, op0=ALU.mult)), [pk, rk_, "flag"], [rk_ + "h"])
                            G(("tensor_scalar", _a(out=raw[p3][:, :, 130:132], in0=raw[s3][:, :, 2:4], scalar1=flag[:, 0:1],
                                                        scalar2=None, op0=ALU.mult)), [pk, rk_, "flag"], [pk + "h"])
                        else:
                            G(("tensor_copy", _a(out=raw[s3][:, :, 0:2], in_=raw[p3][:, :, 128:130])), [pk, rk_], [rk_ + "h"])
                            G(("tensor_copy", _a(out=raw[p3][:, :, 130:132], in_=raw[s3][:, :, 2:4])), [pk, rk_], [pk + "h"])
                    else:
                        G(("memset", _a(raw[s3][:, :, 0:2], 0.0)), [rk_], [rk_ + "h"])
                    if i == NCH - 1:
                        G(("memset", _a(raw[s3][:, :, 130:132], 0.0)), [rk_], [rk_ + "h"])

                def stageB(i):
                    ci = order[i]
                    s3, s2 = i % 2, i % 2
                    xk, rk_, gk, vk, dk = "xt%d" % s3, "raw%d" % s3, "glaT%d" % s2, "vtok%d" % s2, "dtt%d" % s2
                    rw = raw[s3]
                    rr = [rk_, rk_ + "h"]
                    boundary = (i % CPS == CPS - 1)
                    useJ = (d == 1) and DBG not in ("B1noJ",)
                    if d == 1 and DBG != "B1noDMA":
                        P.dma("gpsimd", ofs[:], ofd[ci * C:(ci + 1) * C, :], [], ["ofs"])
                    for g0 in (0, 4):
                        b_, bk = bank()
                        for j in range(4):
                            b = g0 + j
                            for tp in range(5):
                                mm(b_[:, j * 128:(j + 1) * 128], dconv[:, b, tp, :], rw[:, b, tp:tp + 128], tp == 0, tp == 4,
                                   ["dconv"] + rr, [bk])
                        for j in range(4):
                            b = g0 + j
                            A(("activation", _a(out=xbcT[:, b, :], in_=b_[:, j * 128:(j + 1) * 128], func=AF.Silu,
                                                               bias=convb[:, b:b + 1])), [bk, "convb"], ["xbcT"])
                    b_, bk = bank()
                    for b in range(6):
                        T(("transpose", _a(bf(b_)[:, b * 128:(b + 1) * 128], xbcT[:, b, :], identb[:])), ["xbcT", "identb"], [bk])
                    V(("tensor_copy", _a(out=xsB[:], in_=bf(b_)[:, 0:768])), [bk], ["xsB"])
                    V(("scalar_tensor_tensor", _a(out=dtA[:], in0=dtt[s2][:], scalar=-1.0, in1=expA[:], op0=ALU.mult, op1=ALU.mult)),
                      [dk, "expA"], ["dtA"])
                    b_, bk = bank()
                    mm(b_[:, 0:8], tri2[:, 0, :], dtA[:], True, True, ["tri2", "dtA"], [bk])
                    mm(b_[:, 8:16], ones32[:], dtA[:], True, True, ["ones32", "dtA"], [bk])
                    V(("tensor_copy", _a(out=ac[:, 0, :], in_=b_[:, 0:8])), [bk], ["ac0"])
                    V(("tensor_scalar", _a(out=ac[:, 1, :], in0=b_[:, 0:8], scalar1=-1.0, scalar2=None, op0=ALU.mult)), [bk], ["ac1"])
                    A(("activation", _a(out=ac[:, 2, :], in_=b_[:, 0:8], func=AF.Exp)), [bk], ["ac2"])
                    A(("activation", _a(out=ac[:, 4, :], in_=b_[:, 8:16], func=AF.Exp)), [bk], ["ac4"])
                    V(("tensor_tensor", _a(out=ac[:, 3, :], in0=b_[:, 8:16], in1=ac[:, 0, :], op=ALU.subtract)), [bk, "ac0"], ["ac3"])
                    A(("activation", _a(out=ac[:, 3, :], in_=ac[:, 3, :], func=AF.Exp)), ["ac3"], ["ac3"])
                    V(("tensor_tensor", _a(out=ac[:, 3, :], in0=ac[:, 3, :], in1=dtt[s2][:], op=ALU.mult)), ["ac3", dk], ["ac3"])
                    G(("tensor_tensor", _a(out=Zt, in0=tri2[:, 0:1, :].to_broadcast([128, 8, 128]),
                                                in1=dtA[:].unsqueeze(2).to_broadcast([128, 8, 128]), op=ALU.mult)),
                      ["tri2", "dtA"], ["Zt"])
                    for hg in range(2):
                        b_, bk = bank()
                        mm(b_[:, :], ones32[:], Ztf[:, 512 * hg:512 * hg + 512], True, False, ["ones32", "Zt"], [bk])
                        mm(b_[:, :], identb[:], negmf[d][:], False, True, ["identb", "negm%d" % d], [bk])
                        for hh in range(4):
                            h = 4 * hg + hh
                            A(("activation", _a(out=seg[:, h, :], in_=b_[:, hh * 128:(hh + 1) * 128], func=AF.Exp,
                                                                      bias=ac[:, 1, h:h + 1])), [bk, "ac1"], ["seg"])
                    b_, bk = bank()
                    for g in range(2):
                        mm(b_[:, g * 128:(g + 1) * 128], xbcT[:, 4 + g, :], xbcT[:, 6 + g, :], True, True, ["xbcT"], [bk])
                    V(("tensor_copy", _a(out=cb[:], in_=b_[:, 0:256])), [bk], ["cb"])
                    V(("tensor_tensor", _a(out=MT[:].rearrange("p (g r) t -> p g r t", g=2),
                                                in0=seg[:].rearrange("p (g r) t -> p g r t", g=2),
                                                in1=cb[:].unsqueeze(2).to_broadcast([128, 2, 4, 128]), op=ALU.mult)),
                      ["seg", "cb"], ["MT"])
                    V(("tensor_tensor", _a(out=xdt[:].rearrange("p (h q) -> p h q", h=8),
                                                in0=xsB[:, 0:512].rearrange("p (h q) -> p h q", h=8),
                                                in1=dtt[s2][:].unsqueeze(2).to_broadcast([128, 8, 64]), op=ALU.mult)),
                      ["xsB", dk], ["xdt"])
                    G(("tensor_tensor", _a(out=xdtw[:].rearrange("p (h q) -> p h q", h=8),
                                                in0=xsB[:, 0:512].rearrange("p (h q) -> p h q", h=8),
                                                in1=ac[:, 3, :].unsqueeze(2).to_broadcast([128, 8, 64]), op=ALU.mult)),
                      ["xsB", "ac3"], ["xdtw"])
                    by_, byk = bank()
                    if useJ:
                        mm(by_[:, :], J32[:], ofs[:, 512:1024], True, False, ["J32", "ofs"], [byk])
                    for h in range(8):
                        mm(by_[:, 64 * h:64 * h + 64], MT[:, h, :], xdt[:, 64 * h:64 * h + 64], (not useJ) and h == 0, h == 7, ["MT", "xdt"], [byk])
                    b_, bk = bank()
                    for g in range(2):
                        mm(b_[:, 256 * g:256 * g + 256], xbcT[:, 6 + g, :], H16[:, 256 * g:256 * g + 256], True, True, ["xbcT", "H16"], [bk])
                    V(("tensor_tensor", _a(out=ytmp[:].rearrange("p (h q) -> p h q", h=8),
                                                in0=b_[:, :].rearrange("p (h q) -> p h q", h=8),
                                                in1=ac[:, 2, :].unsqueeze(2).to_broadcast([128, 8, 64]), op=ALU.mult)),
                      [bk, "ac2"], ["ytmp"])
                    V(("tensor_tensor", _a(out=ysum[:], in0=by_[:, :], in1=ytmp[:], op=ALU.add)), [byk, "ytmp"], ["ysum"])
                    b_, bk = bank()
                    for g in range(2):
                        mm(b_[:, 256 * g:256 * g + 256], xsB[:, 512 + 128 * g:512 + 128 * g + 128], xdtw[:, 256 * g:256 * g + 256], True, True,
                           ["xsB", "xdtw"], [bk])
                    G(("tensor_tensor", _a(out=ytmp[:].rearrange("p (h q) -> p h q", h=8),
                                                in0=H32[:].rearrange("p (h q) -> p h q", h=8),
                                                in1=ac[:, 4, :].unsqueeze(2).to_broadcast([128, 8, 64]), op=ALU.mult)),
                      ["H32", "ac4", "ytmp"], ["ytmp"])
                    V(("tensor_tensor", _a(out=H32[:], in0=b_[:, :], in1=ytmp[:], op=ALU.add)), [bk, "ytmp"], ["H32"])
                    if boundary:
                        V(("tensor_scalar", _a(out=H32[:], in0=H32[:], scalar1=flag[:, 0:1], scalar2=None, op0=ALU.mult)),
                          ["H32", "flag"], ["H32"])
                    A(("copy", _a(out=H16[:], in_=H32[:])), ["H32"], ["H16"])
                    if d == 0:
                        P.dma("gpsimd", ofd[ci * C:(ci + 1) * C, 512:1024], ysum[:], ["ysum"], [])

                    if DBG == "ssd" or DBG2 == "ssd":
                        return
                    gT = glaT[s2]
                    b_, bk = bank()
                    mm(b_[:, 0:128], gT[0:16, 2, :], aupg[:], True, False, [gk, "aupg"], [bk])
                    mm(b_[:, 0:128], ones32[0:1, :], abg[:], False, True, ["ones32", "abg"], [bk])
                    A(("activation", _a(out=sg[:], in_=b_[:, 0:128], func=AF.Sigmoid)), [bk], ["sg"])
                    A(("activation", _a(out=nl[:], in_=sg[:], func=AF.Ln)), ["sg"], ["nl"])
                    b_, bk = bank()
                    mm(b_[:, 0:128], nl[:], tri2[:, 0, :], True, True, ["nl", "tri2"], [bk])
                    A(("activation", _a(out=epos[:], in_=b_[:, 0:128], func=AF.Exp, scale=1.0 / 16)), [bk], ["epos"])
                    A(("activation", _a(out=eneg[:], in_=b_[:, 0:128], func=AF.Exp, scale=-1.0 / 16)), [bk], ["eneg"])
                    V(("scalar_tensor_tensor", _a(out=qd[:], in0=gT[:, 0, :], scalar=32.0 ** -0.5, in1=epos[:], op0=ALU.mult, op1=ALU.mult)),
                      [gk, "epos"], ["qd"])
                    V(("tensor_tensor", _a(out=kd[:], in0=gT[:, 1, :], in1=eneg[:], op=ALU.mult)), [gk, "eneg"], ["kd"])
                    G(("tensor_scalar", _a(out=kt[:], in0=kd[:], scalar1=epos[:, 127:128], scalar2=None, op0=ALU.mult)), ["kd", "epos"], ["kt"])
                    G(("tensor_tensor", _a(out=Qblk, in0=qd[:].unsqueeze(1).to_broadcast([128, 4, 128]), in1=qmask, op=ALU.mult)),
                      ["qd", "qmask"], ["Qblk"])
                    b_, bk = bank()
                    mm(b_[:, :], kd[:], Qblkf[:], True, True, ["kd", "Qblk"], [bk])
                    V(("tensor_tensor", _a(out=Sm[:], in0=b_[:, :].rearrange("p (h t) -> p h t", h=4),
                                                in1=tri2[:, d:d + 1, :].to_broadcast([128, 4, 128]), op=ALU.mult)), [bk, "tri2"], ["Sm"])
                    b_, bk = bank()
                    T(("transpose", _a(bf(b_)[:, 0:128], kt[:], identb[:])), ["kt", "identb"], [bk])
                    A(("copy", _a(out=kttok[:], in_=bf(b_)[:, 0:128])), [bk], ["kttok"])
                    bo_, bok = banks[7], "bank7"
                    if useJ:
                        mm(bo_[:, :], J32[:], ofs[:, 0:512], True, False, ["J32", "ofs"], [bok])
                    mm(bo_[:, 0:256], qd[:], S16[:], not useJ, False, ["qd", "S16"], [bok])
                    for h in range(4):
                        mm(bo_[:, 64 * h:64 * h + 64], Sm[:, h, :], vtok[s2][:, 64 * h:64 * h + 64], False, False, ["Sm", vk], [bok])
                    b_, bk = bank()
                    mm(b_[:, 0:256], kttok[:], vtok[s2][:], True, True, ["kttok", vk], [bk])
                    V(("tensor_tensor", _a(out=utmp[:], in0=b_[:, 0:256], in1=bm4[:], op=ALU.mult)), [bk, "bm4"], ["utmp"])
                    V(("scalar_tensor_tensor", _a(out=S32[:], in0=S32[:], scalar=epos[:, 127:128], in1=utmp[:], op0=ALU.mult, op1=ALU.add)),
                      ["S32", "epos", "utmp"], ["S32"])
                    if boundary:
                        V(("tensor_scalar", _a(out=S32[:], in0=S32[:], scalar1=flag[:, 0:1], scalar2=None, op0=ALU.mult)),
                          ["S32", "flag"], ["S32"])
                    G(("tensor_copy", _a(out=S16[:], in_=S32[:])), ["S32"], ["S16"])

                    if DBG == "gla" or DBG2 == "gla":
                        return
                    nrb = 9 if d == 1 else 8
                    bsh = []
                    for g0 in range(0, nrb, 4):
                        b_, bk = bank()
                        bsh.append((b_, bk))
                        for j in range(min(4, nrb - g0)):
                            bb = g0 + j
                            m = 64 if bb in (6, 7) else 128
                            rb = 8 + bb
                            mm(b_[0:m, j * 128:(j + 1) * 128], dsh[0:m, bb, 0, 0:m], rw[0:m, rb, 2:130], True, False, ["dsh"] + rr, [bk])
                            mm(b_[0:m, j * 128:(j + 1) * 128], dsh[0:m, bb, 1, 0:m], rw[0:m, rb, 1:129], False, False, ["dsh"] + rr, [bk])
                            mm(b_[0:m, j * 128:(j + 1) * 128], dsh[0:m, bb, 1, 0:m], rw[0:m, rb, 3:131], False, True, ["dsh"] + rr, [bk])
                    A(("copy", _a(out=rkv[:, 0:4, :], in_=bsh[0][0][:, :].rearrange("p (a t) -> p a t", a=4))), [bsh[0][1]], ["rkv"])
                    V(("tensor_copy", _a(out=rkv[:, 4:6, :], in_=bsh[1][0][:, 0:256].rearrange("p (a t) -> p a t", a=2))), [bsh[1][1]], ["rkv"])
                    A(("activation", _a(out=twT[:], in_=bsh[1][0][0:64, 256:384], func=AF.Tanh)), [bsh[1][1]], ["twT"])
                    V(("tensor_copy", _a(out=afT[:], in_=bsh[1][0][0:64, 384:512])), [bsh[1][1]], ["afT"])
                    if d == 1:
                        A(("activation", _a(out=sgd[:], in_=bsh[2][0][:, 0:128], func=AF.Sigmoid)), [bsh[2][1]], ["sgd"])
                    b_, bk = bank()
                    mm(b_[:, 0:256], twT[:], wup[:], True, False, ["twT", "wup"], [bk])
                    mm(b_[:, 0:256], ones32[0:1, :], w0r[:], False, True, ["ones32", "w0r"], [bk])
                    A(("activation", _a(out=nlw[:], in_=b_[:, 0:256], func=AF.Sigmoid)), [bk], ["nlw"])
                    b_, bk = bank()
                    for bb in range(2):
                        mm(b_[:, 256 * bb:256 * bb + 256], nlw[:, 128 * bb:128 * bb + 128], tri2f[:], True, True, ["nlw", "tri2"], [bk])
                    cw = b_[:, :].rearrange("p (b x t) -> p b x t", b=2, x=2)
                    LW = 0.6065306597126334
                    A(("activation", _a(out=gin[:], in_=cw[:, :, 0, :], func=AF.Exp, scale=-LW)), [bk], ["gin"])
                    A(("activation", _a(out=gex[:], in_=cw[:, :, 1, :], func=AF.Exp, scale=-LW)), [bk], ["gex"])
                    A(("activation", _a(out=ginv[:], in_=cw[:, :, 0, :], func=AF.Exp, scale=LW)), [bk], ["ginv"])
                    b_, bk = bank()
                    for bb in range(2):
                        mm(b_[:, 128 * bb:128 * bb + 128], aup[:, 128 * bb:128 * bb + 128], afT[:], True, True, ["aup", "afT"], [bk])
                    for bb in range(2):
                        A(("activation", _a(out=alpha[:, bb, :], in_=b_[:, 128 * bb:128 * bb + 128], func=AF.Sigmoid,
                                                             bias=a0[:, bb:bb + 1])), [bk, "a0"], ["alpha"])
                    V(("tensor_tensor", _a(out=kkk[:], in0=rkv[:, 2:4, :], in1=kk3[:, 0, :].unsqueeze(2).to_broadcast([128, 2, 128]), op=ALU.mult)),
                      ["rkv", "kk3"], ["kkk"])
                    G(("tensor_tensor", _a(out=sq[:], in0=kkk[:], in1=kkk[:], op=ALU.mult)), ["kkk"], ["sq"])
                    b_, bk = bank()
                    for bb in range(2):
                        mm(b_[:, 128 * bb:128 * bb + 128], bm2b[:], sq[:, bb, :], True, True, ["bm2b", "sq"], [bk])
                    A(("activation", _a(out=rinv[:], in_=b_[:, 0:256].rearrange("p (b t) -> p b t", b=2), func=AF.Sqrt, bias=1e-12)),
                      [bk], ["rinv"])
                    V(("reciprocal", _a(out=rinv[:], in_=rinv[:])), ["rinv"], ["rinv"])
                    V(("tensor_tensor", _a(out=kkn[:], in0=kkk[:], in1=rinv[:], op=ALU.mult)), ["kkk", "rinv"], ["kkn"])
                    G(("tensor_tensor", _a(out=t1[:], in0=alpha[:], in1=kk3[:, 1, :].unsqueeze(2).to_broadcast([128, 2, 128]), op=ALU.mult)),
                      ["alpha", "kk3"], ["t1"])
                    G(("tensor_tensor", _a(out=t1[:], in0=t1[:], in1=omka[:].unsqueeze(2).to_broadcast([128, 2, 128]), op=ALU.add)),
                      ["t1", "omka"], ["t1"])
                    G(("tensor_tensor", _a(out=kdir[:], in0=rkv[:, 2:4, :], in1=t1[:], op=ALU.mult)), ["rkv", "t1"], ["kdir"])
                    V(("tensor_tensor", _a(out=bvec[:], in0=kkn[:], in1=alpha[:], op=ALU.mult)), ["kkn", "alpha"], ["bvec"])
                    V(("tensor_tensor", _a(out=RtT[:], in0=rkv[:, 0:2, :], in1=gin[:], op=ALU.mult)), ["rkv", "gin"], ["RtT"])
                    V(("scalar_tensor_tensor", _a(out=AtT[:], in0=kkn[:], scalar=-1.0, in1=gex[:], op0=ALU.mult, op1=ALU.mult)),
                      ["kkn", "gex"], ["AtT"])
                    V(("tensor_tensor", _a(out=BtT[:], in0=bvec[:], in1=ginv[:], op=ALU.mult)), ["bvec", "ginv"], ["BtT"])
                    G(("tensor_tensor", _a(out=KtT[:], in0=kdir[:], in1=ginv[:], op=ALU.mult)), ["kdir", "ginv"], ["KtT"])
                    G(("tensor_copy", _a(out=fmT[:, 0, :, :], in_=AtT[:])), ["AtT"], ["fmT0"])
                    V(("tensor_tensor", _a(out=fmT[:, 1, :, :], in0=BtT[:], in1=gin[:, :, 127:128].to_broadcast([128, 2, 128]), op=ALU.mult)),
                      ["BtT", "gin"], ["fmT1"])
                    G(("tensor_tensor", _a(out=fmT[:, 2, :, :], in0=KtT[:], in1=gin[:, :, 127:128].to_broadcast([128, 2, 128]), op=ALU.mult)),
                      ["KtT", "gin"], ["fmT2"])
                    G(("tensor_copy", _a(out=fmT[:, 3, :, :], in_=rkv[:, 4:6, :])), ["rkv"], ["fmT3"])
                    b_, bk = bank()
                    for a in range(4):
                        for bb in range(2):
                            T(("transpose", _a(bf(b_)[:, (2 * a + bb) * 128:(2 * a + bb + 1) * 128], fmT[:, a, bb, :], identb[:])),
                              ["fmT%d" % a, "identb"], [bk])
                    V(("tensor_copy", _a(out=tok4[:].rearrange("p a c -> p (a c)"), in_=bf(b_)[:, :])), [bk], ["tok4"])
                    for bb in range(2):
                        G(("tensor_tensor", _a(out=RAb[:, bb, :, 0, :], in0=AtT[:, bb:bb + 1, :].to_broadcast([128, 2, 128]),
                                                           in1=m2[:].unsqueeze(2).to_broadcast([128, 2, 128]), op=ALU.mult)), ["AtT", "m2"], ["RAb"])
                        V(("tensor_tensor", _a(out=RAb[:, bb, :, 1, :], in0=RtT[:, bb:bb + 1, :].to_broadcast([128, 2, 128]),
                                                           in1=m2[:].unsqueeze(2).to_broadcast([128, 2, 128]), op=ALU.mult)), ["RtT", "m2"], ["RAb"])
                    for bb in range(2):
                        b_, bk = bank()
                        mm(b_[:, :], BtT[:, bb, :], RAbf[:, bb, :], True, True, ["BtT", "RAb"], [bk])
                        V(("tensor_tensor", _a(out=SCB[:, bb, :, :, :], in0=b_[:, :].rearrange("p (h x t) -> p h x t", h=2, x=2),
                                                                in1=mskB[:], op=ALU.mult)), [bk, "mskB"], ["SCB"])
                        b_, bk = bank()
                        mm(b_[:, :], KtT[:, bb, :], RAbf[:, bb, :], True, True, ["KtT", "RAb"], [bk])
                        V(("tensor_tensor", _a(out=SCK[:, bb, :, :, :], in0=b_[:, :].rearrange("p (h x t) -> p h x t", h=2, x=2),
                                                                in1=mskK[d][:], op=ALU.mult)), [bk, "mskK%d" % d], ["SCK"])
                    N0v = SCB[:].rearrange("p b h x t -> p (b h) x t")[:, :, 0, :]
                    G(("tensor_copy", _a(out=Nk[0][:], in_=N0v)), ["SCB"], ["Nk0"])
                    b_, bk = bank()
                    for h in range(4):
                        T(("transpose", _a(bf(b_)[:, h * 128:(h + 1) * 128], Nk[0][:, h, :], identb[:])), ["Nk0", "identb"], [bk])
                    A(("copy", _a(out=Ak[0][:], in_=bf(b_)[:, 0:512].rearrange("p (h t) -> p h t", h=4))), [bk], ["Ak0"])
                    V(("tensor_tensor", _a(out=T32[:], in0=Nk[0][:], in1=ident32[:].unsqueeze(1).to_broadcast([128, 4, 128]), op=ALU.add)),
                      ["Nk0", "ident32"], ["T32"])
                    G(("tensor_copy", _a(out=T16, in_=T32[:])), ["T32"], ["T16"])
                    for lev in range(6):
                        c_, n_ = lev % 2, (lev + 1) % 2
                        ba_, bak = bank()
                        bn_, bnk = bank()
                        for h in range(4):
                            mm(ba_[:, h * 128:(h + 1) * 128], Nk[c_][:, h, :], Ak[c_][:, h, :], True, True, ["Nk%d" % c_, "Ak%d" % c_], [bak])
                        for h in range(4):
                            mm(bn_[:, h * 128:(h + 1) * 128], Ak[c_][:, h, :], Nk[c_][:, h, :], True, True, ["Nk%d" % c_, "Ak%d" % c_], [bnk])
                        A(("copy", _a(out=Ak[n_][:], in_=ba_[:, :].rearrange("p (h t) -> p h t", h=4))), [bak], ["Ak%d" % n_])
                        if lev < 5:
                            V(("tensor_copy", _a(out=Nk[n_][:], in_=bn_[:, :].rearrange("p (h t) -> p h t", h=4))), [bnk], ["Nk%d" % n_])
                        bt_, btk = bank()
                        for h in range(4):
                            mm(bt_[:, h * 128:(h + 1) * 128], Ak[n_][:, h, :], T16[:, h, :], True, True, ["Ak%d" % n_, "T16"], [btk])
                        V(("tensor_tensor", _a(out=T32[:], in0=bt_[:, :].rearrange("p (h t) -> p h t", h=4), in1=T32[:], op=ALU.add)),
                          [btk, "T32"], ["T32"])
                        G(("tensor_copy", _a(out=T16, in_=T32[:])), ["T32"], ["T16"])
                    for bb in range(2):
                        b_, bk = bank()
                        mm(b_[:, 0:256], tok4[:, 0, 128 * bb:128 * bb + 128], T16f[:, 256 * bb:256 * bb + 256], True, True, ["tok4", "T16"], [bk])
                        V(("tensor_copy", _a(out=WtT[0:64, bb, :], in_=b_[0:64, 0:128])), [bk], ["WtT"])
                        A(("copy", _a(out=WtT[64:128, bb, :], in_=b_[64:128, 128:256])), [bk], ["WtT"])
                    b_, bk = bank()
                    for h in range(4):
                        mm(b_[:, 64 * h:64 * h + 64], SCK[:, h // 2, h % 2, 0, :], tok4[:, 3, 64 * h:64 * h + 64], True, True, ["SCK", "tok4"], [bk])
                    V(("tensor_copy", _a(out=X2[:], in_=b_[:, 0:256])), [bk], ["X2"])
                    b_, bk = bank()
                    for h in range(4):
                        mm(b_[:, 64 * h:64 * h + 64], T16[:, h, :], X2[:, 64 * h:64 * h + 64], h == 0, False, ["T16", "X2"], [bk])
                    for bb in range(2):
                        mm(b_[:, 128 * bb:128 * bb + 128], WtT[:, bb, :], R16[:, bb, :], False, bb == 1, ["WtT", "R16"], [bk])
                    A(("copy", _a(out=U16[:], in_=b_[:, 0:256])), [bk], ["U16"])
                    for bb in range(2):
                        mm(bo_[:, 256 + 128 * bb:256 + 128 * bb + 128], RtT[:, bb, :], R16[:, bb, :], False, False, ["RtT", "R16"], [bok])
                    for h in range(4):
                        mm(bo_[:, 256 + 64 * h:256 + 64 * h + 64], SCB[:, h // 2, h % 2, 1, :], U16[:, 64 * h:64 * h + 64], False, False, ["SCB", "U16"], [bok])
                        mm(bo_[:, 256 + 64 * h:256 + 64 * h + 64], SCK[:, h // 2, h % 2, 1, :], tok4[:, 3, 64 * h:64 * h + 64], False, h == 3, ["SCK", "tok4"], [bok])
                    b_, bk = bank()
                    for bb in range(2):
                        mm(b_[:, 128 * bb:128 * bb + 128], tok4[:, 1, 128 * bb:128 * bb + 128], U16[:, 128 * bb:128 * bb + 128], True, False, ["tok4", "U16"], [bk])
                        mm(b_[:, 128 * bb:128 * bb + 128], tok4[:, 2, 128 * bb:128 * bb + 128], tok4[:, 3, 128 * bb:128 * bb + 128], False, True, ["tok4"], [bk])
                    V(("tensor_tensor", _a(out=rtmp[:], in0=b_[:, 0:256].rearrange("p (b t) -> p b t", b=2),
                                                      in1=bm2[:].unsqueeze(1).to_broadcast([128, 2, 128]), op=ALU.mult)), [bk, "bm2"], ["rtmp"])
                    G(("tensor_tensor", _a(out=R32[:], in0=R32[:], in1=gin[:, :, 127:128].to_broadcast([128, 2, 128]), op=ALU.mult)),
                      ["R32", "gin"], ["R32"])
                    G(("tensor_tensor", _a(out=R32[:], in0=R32[:], in1=rtmp[:], op=ALU.add)), ["R32", "rtmp"], ["R32"])
                    if boundary:
                        G(("tensor_scalar", _a(out=R32[:], in0=R32[:], scalar1=flag[:, 0:1], scalar2=None, op0=ALU.mult)),
                          ["R32", "flag"], ["R32"])
                    G(("tensor_copy", _a(out=R16[:], in_=R32[:])), ["R32"], ["R16"])

                    if d == 0:
                        V(("tensor_copy", _a(out=osb[:], in_=bo_[:, :])), [bok], ["osb"])
                        P.dma("gpsimd", ofd[ci * C:(ci + 1) * C, 0:512], osb[:], ["osb"], [])
                        return
                    if DBG in ("B1", "B1noJ", "B1noDMA"):
                        return
                    A(("copy", _a(out=fa[:, 0:256], in_=bo_[:, 0:256])), [bok], ["fa"])
                    G(("tensor_tensor", _a(out=fb[:, 0:256], in0=fa[:, 0:256], in1=fa[:, 0:256], op=ALU.mult)), ["fa"], ["fb"])
                    V(("tensor_reduce", _a(out=st4[:, 0:4], in_=fb[:, 0:256].rearrange("p (h q) -> p h q", h=4), axis=AX.X, op=ALU.add)),
                      ["fb"], ["st4a"])
                    A(("activation", _a(out=st4[:, 0:4], in_=st4[:, 0:4], func=AF.Sqrt, scale=1.0 / 64, bias=EPS)), ["st4a"], ["st4a"])
                    V(("reciprocal", _a(out=st4[:, 0:4], in_=st4[:, 0:4])), ["st4a"], ["st4a"])
                    V(("tensor_tensor", _a(out=fa[:, 0:256].rearrange("p (h q) -> p h q", h=4), in0=fa[:, 0:256].rearrange("p (h q) -> p h q", h=4),
                                                in1=st4[:, 0:4].unsqueeze(2).to_broadcast([128, 4, 64]), op=ALU.mult)), ["fa", "st4a"], ["fa"])
                    V(("tensor_tensor", _a(out=fa[:, 0:256].rearrange("p (h q) -> p h q", h=4), in0=fa[:, 0:256].rearrange("p (h q) -> p h q", h=4),
                                                in1=glan[:].unsqueeze(1).to_broadcast([128, 4, 64]), op=ALU.mult)), ["fa", "glan"], ["fa"])
                    V(("tensor_tensor", _a(out=mix[:, 0:256], in0=fa[:, 0:256], in1=sgz[s2][:, 0:256], op=ALU.mult)), ["fa", "sgz%d" % s2], ["mixg"])
                    ya = fa[:, 256:512]
                    yb = fb[:, 256:512]
                    A(("copy", _a(out=ya, in_=bo_[:, 256:512])), [bok], ["ya"])
                    V(("tensor_reduce", _a(out=st4[:, 4:8], in_=ya.rearrange("p (h q) -> p h q", h=4), axis=AX.X, op=ALU.add)), ["ya"], ["st4b"])
                    V(("tensor_scalar", _a(out=st4[:, 4:8], in0=st4[:, 4:8], scalar1=1.0 / 64, scalar2=None, op0=ALU.mult)), ["st4b"], ["st4b"])
                    V(("tensor_tensor", _a(out=ya.rearrange("p (h q) -> p h q", h=4), in0=ya.rearrange("p (h q) -> p h q", h=4),
                                                in1=st4[:, 4:8].unsqueeze(2).to_broadcast([128, 4, 64]), op=ALU.subtract)), ["ya", "st4b"], ["ya"])
                    G(("tensor_tensor", _a(out=yb, in0=ya, in1=ya, op=ALU.mult)), ["ya"], ["yb"])
                    V(("tensor_reduce", _a(out=st4[:, 8:12], in_=yb.rearrange("p (h q) -> p h q", h=4), axis=AX.X, op=ALU.add)), ["yb"], ["st4c"])
                    A(("activation", _a(out=st4[:, 8:12], in_=st4[:, 8:12], func=AF.Sqrt, scale=1.0 / 64, bias=64e-5)), ["st4c"], ["st4c"])
                    V(("reciprocal", _a(out=st4[:, 8:12], in_=st4[:, 8:12])), ["st4c"], ["st4c"])
                    V(("tensor_tensor", _a(out=ya.rearrange("p (h q) -> p h q", h=4), in0=ya.rearrange("p (h q) -> p h q", h=4),
                                                in1=st4[:, 8:12].unsqueeze(2).to_broadcast([128, 4, 64]), op=ALU.mult)), ["ya", "st4c"], ["ya"])
                    V(("tensor_tensor", _a(out=ya, in0=ya, in1=lng[:], op=ALU.mult)), ["ya", "lng"], ["ya"])
                    V(("tensor_tensor", _a(out=ya, in0=ya, in1=lnb[:], op=ALU.add)), ["ya", "lnb"], ["ya"])
                    G(("tensor_tensor", _a(out=rtmp[:], in0=rkv[:, 0:2, :], in1=rkv[:, 2:4, :], op=ALU.mult)), ["rkv", "rtmp"], ["rtmp"])
                    G(("tensor_tensor", _a(out=prodT[:], in0=rtmp[:], in1=kk3[:, 2, :].unsqueeze(2).to_broadcast([128, 2, 128]), op=ALU.mult)),
                      ["rtmp", "kk3"], ["prodT"])
                    b_, bk = bank()
                    for bb in range(2):
                        mm(b_[:, 2 * bb:2 * bb + 2], prodT[:, bb, :], hsel[:], True, True, ["prodT", "hsel"], [bk])
                    V(("tensor_copy", _a(out=st4[:, 12:16], in_=b_[:, 0:4])), [bk], ["st4d"])
                    V(("tensor_tensor", _a(out=yb.rearrange("p (h q) -> p h q", h=4), in0=tok4[:, 3, :].rearrange("p (h q) -> p h q", h=4),
                                                in1=st4[:, 12:16].unsqueeze(2).to_broadcast([128, 4, 64]), op=ALU.mult)), ["tok4", "st4d", "yb"], ["yb"])
                    V(("tensor_tensor", _a(out=ya, in0=ya, in1=yb, op=ALU.add)), ["ya", "yb"], ["ya"])
                    b_, bk = bank()
                    mm(b_[:, 0:256], sgd[:], gup[:], True, True, ["sgd", "gup"], [bk])
                    V(("tensor_tensor", _a(out=mix[:, 256:512], in0=ya, in1=b_[:, 0:256], op=ALU.mult)), ["ya", bk], ["mixr"])
                    G(("tensor_tensor", _a(out=ytmp[:].rearrange("p (h q) -> p h q", h=8), in0=xsB[:, 0:512].rearrange("p (h q) -> p h q", h=8),
                                                in1=ssdD[:].unsqueeze(2).to_broadcast([128, 8, 64]), op=ALU.mult)), ["xsB", "ssdD", "ytmp"], ["ytmp"])
                    V(("tensor_tensor", _a(out=ysum[:], in0=ysum[:], in1=ytmp[:], op=ALU.add)), ["ysum", "ytmp"], ["ysum"])
                    V(("tensor_tensor", _a(out=ysum[:], in0=ysum[:], in1=sgz[s2][:, 256:768], op=ALU.mult)), ["ysum", "sgz%d" % s2], ["ysum"])
                    A(("activation", _a(out=ytmp[:], in_=ysum[:], func=AF.Square, accum_out=st8[:, 3:4])), ["ysum", "ytmp"], ["ytmp", "st8e"])
                    A(("activation", _a(out=st8[:, 3:4], in_=st8[:, 3:4], func=AF.Sqrt, scale=1.0 / 512, bias=EPS)), ["st8e"], ["st8e"])
                    V(("reciprocal", _a(out=st8[:, 3:4], in_=st8[:, 3:4])), ["st8e"], ["st8e"])
                    V(("scalar_tensor_tensor", _a(out=mix[:, 512:1024], in0=ysum[:], scalar=st8[:, 3:4], in1=ssdn[:], op0=ALU.mult, op1=ALU.mult)),
                      ["ysum", "st8e", "ssdn"], ["mixs"])
                    if DBG != "B2":
                        P.dma("sync", mixd[ci * C:(ci + 1) * C, :], mix[:], ["mixg", "mixr", "mixs"], [])

                if DBG == "consts" or DBG3 == "setup":
                    continue
                stageA(0)
                for i in range(NCH):
                    if i + 1 < NCH:
                        stageA(i + 1)
                    if DBG != "A" and DBG2 != "A":
                        stageB(i)
                if DBG in ("A", "ssd", "gla", "F"):
                    break
                if DBG == "Bonly":
                    break

        if DBG in ("A", "ssd", "gla", "F", "consts", "B", "B1", "B2", "B1noJ", "B1noDMA", "FF", "Bonly"):
            break
        P.barrier()
        with ExitStack() as ss:
            sb = lambda name, shape, dt: P.sb("L%dC_%s" % (l, name), shape, dt, ss)
            cst = mk_consts(sb, False)
            identb, Jb = cst["identb"], cst["Jb"]
            set_stg(sb, 1024)
            wout = sb("wout", [128, 8, D], BF16)
            wg = sb("wg", [128, 8, DFF], BF16)
            wu = sb("wu", [128, 8, DFF], BF16)
            wd = sb("wd", [128, NFC, D], BF16)
            load_w(wout, "wout", W["wout", l], D, D)
            load_w(wg, "wg", W["wg", l], D, DFF)
            load_w(wu, "wu", W["wu", l], D, DFF)
            load_w(wd, "wd", W["wd", l], DFF, D)
            gffn = sb("gffn", [128, 8], F32)
            P.dma("sync", gffn[:], W["gffn", l][:, :], [], ["gffn"])
            last = (l == DEPTH - 1)
            if last:
                gfin = sb("gfin", [128, D], F32)
                P.dma("sync", gfin[:], W["gfin"][:, :], [], ["gfin"])
            TT = min(2, NCH)
            xc = [sb("xc%d" % i, [128, D], F32) for i in range(TT)]
            mixc = [sb("mixc%d" % i, [128, D], BF16) for i in range(2)]
            mixT = sb("mixT", [128, 8, 128], BF16)
            hTc = sb("hTc", [128, 8, TT * 128], BF16)
            actT = sb("actT", [128, NFC, TT * 128], BF16)
            sgc = [sb("sgc%d" % i, [128, TT * 128], BF16) for i in range(2)]
            hn = sb("hn", [128, D], BF16)
            st8 = sb("st8", [128, 8], F32)
            xo = sb("xo", [128, D], F32)
            dst = y_out if last else xnext
            ntile = (NCH + TT - 1) // TT
            for ti in range(ntile):
                nsub = min(TT, NCH - ti * TT)
                NTK = nsub * 128
                for sub in range(nsub):
                    ci = ti * TT + sub
                    xk = "xc%d" % sub
                    mk_ = "mixc%d" % (ci % 2)
                    mc = mixc[ci % 2]
                    P.dma("sync", xc[sub][:], xsrc[ci * C:(ci + 1) * C, :], [], [xk])
                    P.dma("gpsimd", mc[:], mixd[ci * C:(ci + 1) * C, :], [], [mk_])
                    b_, bk = bank()
                    for kc in range(8):
                        T(("transpose", _a(bf(b_)[:, kc * 128:(kc + 1) * 128], mc[:, kc * 128:(kc + 1) * 128], Jb[:])),
                          [mk_, "Jb"], [bk])
                    A(("copy", _a(out=mixT[:].rearrange("p a t -> p (a t)"), in_=bf(b_)[:, :])), [bk], ["mixT"])
                    for n in range(2):
                        b_, bk = bank()
                        for kc in range(8):
                            mm(b_[:, :], mixT[:, kc, :], wout[:, kc, 512 * n:512 * n + 512], kc == 0, kc == 7, ["mixT", "wout"], [bk])
                        V(("tensor_tensor", _a(out=xc[sub][:, 512 * n:512 * n + 512], in0=b_[:, :], in1=xc[sub][:, 512 * n:512 * n + 512], op=ALU.add)),
                          [bk, xk], [xk])
                    A(("activation", _a(out=hn[:], in_=xc[sub][:], func=AF.Square, accum_out=st8[:, 0:1])), [xk], ["hn", "st8a"])
                    A(("activation", _a(out=st8[:, 1:2], in_=st8[:, 0:1], func=AF.Sqrt, scale=1.0 / D, bias=EPS)), ["st8a"], ["st8b"])
                    V(("reciprocal", _a(out=st8[:, 2:3], in_=st8[:, 1:2])), ["st8b"], ["st8c"])
                    V(("tensor_scalar", _a(out=hn[:], in0=xc[sub][:], scalar1=st8[:, 2:3], scalar2=None, op0=ALU.mult)), [xk, "st8c"], ["hn"])
                    b_, bk = bank()
                    for kc in range(8):
                        T(("transpose", _a(bf(b_)[:, kc * 128:(kc + 1) * 128], hn[:, kc * 128:(kc + 1) * 128], identb[:])), ["hn", "identb"], [bk])
                    V(("tensor_tensor", _a(out=hTc[:, :, sub * 128:(sub + 1) * 128], in0=bf(b_).rearrange("p (a b) -> p a b", a=8),
                                                               in1=gffn[:].unsqueeze(2).to_broadcast([128, 8, 128]), op=ALU.mult)), [bk, "gffn"], ["hTc"])
                for fc in range(NFC):
                    bg_, bgk = bank()
                    bu_, buk = bank()
                    for kc in range(8):
                        mm(bg_[:, 0:NTK], wg[:, kc, fc * 128:(fc + 1) * 128], hTc[:, kc, 0:NTK], kc == 0, kc == 7, ["wg", "hTc"], [bgk])
                    for kc in range(8):
                        mm(bu_[:, 0:NTK], wu[:, kc, fc * 128:(fc + 1) * 128], hTc[:, kc, 0:NTK], kc == 0, kc == 7, ["wu", "hTc"], [buk])
                    sgb = sgc[fc % 2]
                    sgk = "sgc%d" % (fc % 2)
                    A(("activation", _a(out=sgb[:, 0:NTK], in_=bg_[:, 0:NTK], func=AF.Silu)), [bgk], [sgk])
                    V(("tensor_tensor", _a(out=actT[:, fc, 0:NTK], in0=bu_[:, 0:NTK], in1=sgb[:, 0:NTK], op=ALU.mult)),
                      [buk, sgk], ["actT"])
                for sub in range(nsub):
                    ci = ti * TT + sub
                    xk = "xc%d" % sub
                    for n in range(2):
                        b_, bk = bank()
                        for fc in range(NFC):
                            mm(b_[:, :], actT[:, fc, sub * 128:(sub + 1) * 128], wd[:, fc, 512 * n:512 * n + 512], fc == 0, fc == NFC - 1, ["actT", "wd"], [bk])
                        V(("tensor_tensor", _a(out=xo[:, 512 * n:512 * n + 512], in0=b_[:, :], in1=xc[sub][:, 512 * n:512 * n + 512], op=ALU.add)),
                          [bk, xk], ["xo"])
                    if last:
                        A(("activation", _a(out=hn[:], in_=xo[:], func=AF.Square, accum_out=st8[:, 4:5])), ["xo", "hn"], ["hn", "st8f"])
                        A(("activation", _a(out=st8[:, 4:5], in_=st8[:, 4:5], func=AF.Sqrt, scale=1.0 / D, bias=EPS)), ["st8f"], ["st8f"])
                        V(("reciprocal", _a(out=st8[:, 4:5], in_=st8[:, 4:5])), ["st8f"], ["st8f"])
                        V(("scalar_tensor_tensor", _a(out=xo[:], in0=xo[:], scalar=st8[:, 4:5], in1=gfin[:], op0=ALU.mult, op1=ALU.mult)),
                          ["xo", "st8f", "gfin"], ["xo"])
                    P.dma("sync", dst[ci * C:(ci + 1) * C, :], xo[:], ["xo"], [])
    P.emit()
    P.stack.close()
    return nc


def make_weight_maps(prm, DEPTH=2):
    f = lambda a: np.ascontiguousarray(np.asarray(a, dtype=np.float32))
    rep = lambda v, n=128: f(np.broadcast_to(np.asarray(v, np.float32).reshape(1, -1), (n, np.asarray(v).size)))
    fmaj = lambda v, nb: f(np.asarray(v, np.float32).reshape(nb, 128).T)
    m = {}
    for l in range(DEPTH):
        win = np.asarray(prm["w_in"][l], np.float32)
        for d in (0, 1):
            m["wfm_%d_%d" % (l, d)] = f(win[:, col_index(fm_blocks(d))])
            m["wtm_%d_%d" % (l, d)] = f(win[:, col_index(tm_cols(d))])
            m["gla_aup_%d_%d" % (l, d)] = f(prm["gla_a_up"][l][d])
            m["gla_ab_%d_%d" % (l, d)] = f(prm["gla_a_bias"][l][d].reshape(1, 128))
            mu = np.asarray(prm["rwkv_mu"][l], np.float32)
            mub = np.zeros((128, 9), np.float32)
            for b in range(6):
                mub[:, b] = mu[128 * b:128 * b + 128]
            mub[0:64, 6] = mu[768 + 64 * d:768 + 64 * d + 64]
            mub[0:64, 7] = mu[896 + 64 * d:896 + 64 * d + 64]
            mub[:, 8] = mu[1024:1152]
            m["mu_%d_%d" % (l, d)] = mub
            m["w0_%d_%d" % (l, d)] = f(prm["rwkv_w0"][l][d].reshape(1, 256))
            m["wup_%d_%d" % (l, d)] = f(prm["rwkv_w_up"][l][d])
            m["a0_%d_%d" % (l, d)] = fmaj(prm["rwkv_a0"][l][d], 2)
            m["aup_%d_%d" % (l, d)] = f(prm["rwkv_a_up"][l][d])
            m["dtb_%d_%d" % (l, d)] = rep(prm["ssd_dt_bias"][l][d])
            m["alog_%d_%d" % (l, d)] = rep(prm["ssd_A_log"][l][d])
        m["wout_%d" % l] = f(prm["w_out"][l])
        m["wg_%d" % l] = f(prm["ffn_gate"][l])
        m["wu_%d" % l] = f(prm["ffn_up"][l])
        m["wd_%d" % l] = f(prm["ffn_down"][l])
        m["gmix_%d" % l] = fmaj(prm["norm_mix"][l], 8)
        m["gffn_%d" % l] = fmaj(prm["norm_ffn"][l], 8)
        m["gla_norm_%d" % l] = rep(prm["gla_norm"][l])
        m["gup_%d" % l] = f(prm["rwkv_g_up"][l])
        kk3 = np.stack([fmaj(prm["rwkv_k_k"][l], 2), fmaj(prm["rwkv_k_a"][l], 2), fmaj(prm["rwkv_r_k"][l], 2)], axis=1)
        m["kk3_%d" % l] = f(kk3.reshape(128, 6))
        m["lng_%d" % l] = rep(prm["rwkv_ln_g"][l])
        m["lnb_%d" % l] = rep(prm["rwkv_ln_b"][l])
        cw = np.asarray(prm["ssd_conv_w"][l], np.float32)
        m["convw_%d" % l] = f(cw.reshape(5, 8, 128).transpose(2, 1, 0).reshape(128, 40))
        m["convb_%d" % l] = fmaj(prm["ssd_conv_b"][l], 8)
        m["ssdD_%d" % l] = rep(prm["ssd_D"][l])
        m["ssdn_%d" % l] = rep(prm["ssd_norm"][l])
    m["gfin"] = rep(prm["final_norm"])
    return m


_CACHE = {}


def run_cores(core_x, core_flag, prm, NSLOT, SLOT, DEPTH=2, runner=None):
    key = (NSLOT, SLOT, DEPTH)
    nc = build_program(NSLOT, SLOT, DEPTH)
    wm = make_weight_maps(prm, DEPTH)
    in_maps = []
    for x, fl in zip(core_x, core_flag):
        mp = dict(wm)
        mp["x_in"] = np.ascontiguousarray(x, dtype=np.float32)
        mp["flag"] = np.full((128, 1), fl, np.float32)
        in_maps.append(mp)
    if runner is None:
        res = run_bass_kernel_spmd(nc, in_maps, core_ids=list(range(len(in_maps))))
        return [r["y_out"] for r in res.results]
    return [r["y_out"] for r in runner(nc, in_maps)]


def kernel(**inputs):
    xp = np.asarray(inputs["x_prompt"], np.float32)
    xs = np.asarray(inputs["x_sample"], np.float32)
    prm = {k: np.asarray(v) for k, v in inputs.items() if k not in ("x_prompt", "x_sample")}
    NSLOT, SLOT = 8, 2048
    NT = NSLOT * SLOT
    assign = [[], []] + [[] for _ in range(6)]
    for b in range(xp.shape[0]):
        assign[2 + b % 6].append(b)
    core_x, core_flag = [], []
    for c in range(8):
        if c < 2:
            core_x.append(xs[c])
            core_flag.append(1.0)
        else:
            buf = np.empty((NT, D), np.float32)
            for s in range(NSLOT):
                b = assign[c][s % len(assign[c])]
                buf[s * SLOT:(s + 1) * SLOT] = xp[b]
            core_x.append(buf)
            core_flag.append(0.0)
    outs = run_cores(core_x, core_flag, prm, NSLOT, SLOT)
    yp = np.zeros_like(xp)
    ys = np.zeros_like(xs)
    for c in range(8):
        if c < 2:
            ys[c] = outs[c]
        else:
            for s, b in enumerate(assign[c]):
                yp[b] = outs[c][s * SLOT:(s + 1) * SLOT]
    return (yp, ys)
```

```python
import numpy as np
import concourse.bass as bass
import concourse.mybir as mybir
from concourse.bass_utils import run_bass_kernel_spmd
from contextlib import ExitStack

F32 = mybir.dt.float32
BF16 = mybir.dt.bfloat16
ALU = mybir.AluOpType
AF = mybir.ActivationFunctionType
AX = mybir.AxisListType

ENGS = ("tensor", "vector", "scalar", "gpsimd", "sync")
DMA_RING = 6
import os
DBG = os.environ.get("KDBG", "")
DBG2 = os.environ.get("KDBG2", "")
DBG3 = os.environ.get("KDBG3", "")
C = 128
D = 1024
DFF = 2816
NFC = DFF // 128
EPS = 1e-6


def _a(*args, **kw):
    return (args, kw)


def _call(fn, e):
    if isinstance(fn, tuple):
        return getattr(e, fn[0])(*fn[1][0], **fn[1][1])
    return fn(e)


class Op:
    __slots__ = ("eng", "fn", "reads", "writes", "dma", "idx", "deps", "sig", "ring", "ringn")


class Prog:
    def __init__(self, nc):
        self.nc = nc
        self.ops = []
        self.stack = ExitStack()
        self.last_w = {}
        self.readers = {}
        self.ndma = {e: 0 for e in ENGS}
        self.last_eng = {}
        self.ring_last = {}
        self.bar_deps = []
        self.bar_seen = {e: True for e in ENGS}

    def sb(self, name, shape, dt, stack=None):
        return (stack or self.stack).enter_context(self.nc.sbuf_tensor(name, list(shape), dt))

    def ps(self, name, shape, dt):
        return self.stack.enter_context(self.nc.psum_tensor(name, list(shape), dt))

    def barrier(self):
        deps = [v for v in self.last_eng.values()] + [v for v in self.ring_last.values()]
        self.bar_deps = sorted(set(deps))
        self.bar_seen = {e: False for e in ENGS}
        self.last_w = {}
        self.readers = {}

    def op(self, eng, fn, reads=(), writes=(), dma=False):
        o = Op()
        o.eng, o.fn, o.reads, o.writes, o.dma = eng, fn, tuple(reads), tuple(writes), dma
        o.sig = o.ring = o.ringn = None
        o.idx = len(self.ops)
        deps = set()
        if not self.bar_seen[eng]:
            deps.update(self.bar_deps)
            self.bar_seen[eng] = True
        for k in o.reads:
            w = self.last_w.get(k)
            if w is not None:
                deps.add(w)
        for k in o.writes:
            w = self.last_w.get(k)
            if w is not None:
                deps.add(w)
            deps.update(self.readers.get(k, ()))
        for k in o.reads:
            self.readers.setdefault(k, []).append(o.idx)
        for k in o.writes:
            self.last_w[k] = o.idx
            self.readers[k] = []
        deps.discard(o.idx)
        o.deps = sorted(deps)
        if dma:
            n = self.ndma[eng]
            o.ring = n % DMA_RING
            o.ringn = n // DMA_RING + 1
            self.ndma[eng] = n + 1
            self.ring_last[(eng, o.ring)] = o.idx
        else:
            self.last_eng[eng] = o.idx
        self.ops.append(o)
        return o

    def dma(self, eng, out, in_, reads=(), writes=()):
        return self.op(eng, lambda e: e.dma_start(out=out, in_=in_), reads, writes, dma=True)

    def emit(self):
        nc, ops = self.nc, self.ops
        needed = set()
        for o in ops:
            for d in o.deps:
                p = ops[d]
                if p.dma:
                    continue
                if p.eng == "tensor" and o.eng == "tensor" and not o.dma:
                    continue
                needed.add(d)
        cnt = {e: 0 for e in ENGS}
        for o in ops:
            if (not o.dma) and o.idx in needed:
                cnt[o.eng] += 1
                o.sig = cnt[o.eng]
        per = {e: [o for o in ops if o.eng == e] for e in ENGS}
        st = self.stack
        csem = {e: st.enter_context(nc.semaphore("c_" + e)) for e in ("tensor", "vector", "scalar", "gpsimd")}
        dsem = {}
        for e in ENGS:
            if self.ndma[e] > 0:
                dsem[e] = [st.enter_context(nc.semaphore("d_%s_%d" % (e, i))) for i in range(DMA_RING)]
        block = st.enter_context(nc.Block())
        ndma = self.ndma

        def run(ename, eobj):
            waited = {}

            def wait(sem, val):
                key = id(sem)
                if waited.get(key, 0) >= val:
                    return
                waited[key] = val
                eobj.wait_ge(sem, val)

            for o in per[ename]:
                for d in o.deps:
                    p = ops[d]
                    if p.dma:
                        wait(dsem[p.eng][p.ring], 16 * p.ringn)
                    else:
                        if p.eng == "tensor" and ename == "tensor" and not o.dma:
                            continue
                        wait(csem[p.eng], p.sig)
                if o.dma:
                    if o.ringn > 1:
                        wait(dsem[ename][o.ring], 16 * (o.ringn - 1))
                    _call(o.fn, eobj).then_inc(dsem[ename][o.ring], 16)
                else:
                    ins = _call(o.fn, eobj)
                    if o.sig is not None:
                        ins.then_inc(csem[ename], 1)
            if ename == "sync":
                for qe, sems in dsem.items():
                    n = ndma[qe]
                    for r in range(DMA_RING):
                        if n > r:
                            wait(sems[r], 16 * ((n - r + DMA_RING - 1) // DMA_RING))

        for en in ("tensor", "vector", "scalar", "gpsimd", "sync"):
            if per[en] or en == "sync":
                getattr(block, en)(lambda e, en=en: run(en, e))


G0, R0, S0 = 0, 800, 1952


def fm_blocks(d):
    bl = [(S0 + 512 + 128 * b, 128) for b in range(8)]
    bl += [(R0 + 128 * b, 128) for b in range(6)]
    bl += [(R0 + 768 + 64 * d, 64), (R0 + 896 + 64 * d, 64)]
    if d == 1:
        bl += [(R0 + 1024, 128)]
    bl += [(G0, 128), (G0 + 128, 128), (G0 + 768 + 16 * d, 16)]
    return bl


def tm_cols(d):
    cols = [(G0 + 256, 256)]
    if d == 1:
        cols += [(G0 + 512, 256), (S0, 512)]
    cols += [(S0 + 1536 + 8 * d, 8)]
    return cols


def col_index(blocks):
    return np.concatenate([np.arange(s, s + w) for s, w in blocks])


def build_program(NSLOT, SLOT, DEPTH=2):
    NT = NSLOT * SLOT
    NCH = NT // C
    CPS = SLOT // C
    nc = bass.Bass("TRN2", target_bir_lowering=False)
    P = Prog(nc)

    def din(name, shape):
        return nc.dram_tensor(name, list(shape), F32, kind="ExternalInput").ap()

    x_in = din("x_in", [NT, D])
    y_out = nc.dram_tensor("y_out", [NT, D], F32, kind="ExternalOutput").ap()
    mixd = nc.dram_tensor("mixd", [NT, D], BF16, kind="Internal").ap()
    xnext = nc.dram_tensor("xnext", [NT, D], F32, kind="Internal").ap()
    ofd = nc.dram_tensor("ofd", [NT, D], F32, kind="Internal").ap()
    flag_d = din("flag", [128, 1])
    NFM = [sum(w for _, w in fm_blocks(d)) for d in (0, 1)]
    NTM = [sum(w for _, w in tm_cols(d)) for d in (0, 1)]
    W = {}
    for l in range(DEPTH):
        for d in (0, 1):
            W["wfm", l, d] = din("wfm_%d_%d" % (l, d), [D, NFM[d]])
            W["wtm", l, d] = din("wtm_%d_%d" % (l, d), [D, NTM[d]])
            W["gla_aup", l, d] = din("gla_aup_%d_%d" % (l, d), [16, 128])
            W["gla_ab", l, d] = din("gla_ab_%d_%d" % (l, d), [1, 128])
            W["mu", l, d] = din("mu_%d_%d" % (l, d), [128, 9])
            W["w0", l, d] = din("w0_%d_%d" % (l, d), [1, 256])
            W["wup", l, d] = din("wup_%d_%d" % (l, d), [64, 256])
            W["a0", l, d] = din("a0_%d_%d" % (l, d), [128, 2])
            W["aup", l, d] = din("aup_%d_%d" % (l, d), [64, 256])
            W["dtb", l, d] = din("dtb_%d_%d" % (l, d), [128, 8])
            W["alog", l, d] = din("alog_%d_%d" % (l, d), [128, 8])
        W["wout", l] = din("wout_%d" % l, [D, D])
        W["wg", l] = din("wg_%d" % l, [D, DFF])
        W["wu", l] = din("wu_%d" % l, [D, DFF])
        W["wd", l] = din("wd_%d" % l, [DFF, D])
        W["gmix", l] = din("gmix_%d" % l, [128, 8])
        W["gffn", l] = din("gffn_%d" % l, [128, 8])
        W["gla_norm", l] = din("gla_norm_%d" % l, [128, 64])
        W["gup", l] = din("gup_%d" % l, [128, 256])
        W["kk3", l] = din("kk3_%d" % l, [128, 6])
        W["lng", l] = din("lng_%d" % l, [128, 256])
        W["lnb", l] = din("lnb_%d" % l, [128, 256])
        W["convw", l] = din("convw_%d" % l, [128, 40])
        W["convb", l] = din("convb_%d" % l, [128, 8])
        W["ssdD", l] = din("ssdD_%d" % l, [128, 8])
        W["ssdn", l] = din("ssdn_%d" % l, [128, 512])
    W["gfin"] = din("gfin", [128, D])

    V = lambda fn, r=(), w=(): P.op("vector", fn, r, w)
    A = lambda fn, r=(), w=(): P.op("scalar", fn, r, w)
    G = lambda fn, r=(), w=(): P.op("gpsimd", fn, r, w)
    T = lambda fn, r=(), w=(): P.op("tensor", fn, r, w)

    def mm(out, lhsT, rhs, start, stop, r, w):
        T(lambda e: e.matmul(out, lhsT=lhsT, rhs=rhs, start=start, stop=stop), r, w)

    banks = [P.ps("bank%d" % i, [128, 512], F32) for i in range(8)]
    bstate = {"i": 0}

    def bank():
        i = bstate["i"]
        bstate["i"] = (i + 1) % 7
        return banks[i], "bank%d" % i

    def bf(b):
        return b[:].bitcast(BF16)

    def asel(out, in_, pattern, op, fill, base, cm, key):
        G(("affine_select", _a(out=out, in_=in_, pattern=pattern, compare_op=op, fill=fill, base=base,
                                    channel_multiplier=cm)), [key], [key])

    def mk_consts(sbf, full):
        c = {}
        ident32 = sbf("ident32", [128, 128], F32)
        J32 = sbf("J32", [128, 128], F32)
        identb = sbf("identb", [128, 128], BF16)
        Jb = sbf("Jb", [128, 128], BF16)
        G(("memset", _a(ident32[:], 1.0)), [], ["ident32"])
        asel(ident32[:], ident32[:], [[-1, 128]], ALU.is_equal, 0.0, 0, 1, "ident32")
        G(("memset", _a(J32[:], 1.0)), [], ["J32"])
        asel(J32[:], J32[:], [[1, 128]], ALU.is_equal, 0.0, -127, 1, "J32")
        V(("tensor_copy", _a(out=identb[:], in_=ident32[:])), ["ident32"], ["identb"])
        V(("tensor_copy", _a(out=Jb[:], in_=J32[:])), ["J32"], ["Jb"])
        c.update(ident32=ident32, J32=J32, identb=identb, Jb=Jb)
        if not full:
            return c
        tri2f = sbf("tri2", [128, 256], F32)
        tri2 = tri2f[:].rearrange("p (x t) -> p x t", x=2)
        ones32 = sbf("ones32", [128, 128], F32)
        negmf = [sbf("negm%d" % d, [128, 512], BF16) for d in (0, 1)]
        negm = [t_[:].rearrange("p (h t) -> p h t", h=4) for t_ in negmf]
        qmaskf = sbf("qmask", [128, 512], BF16)
        qmask = qmaskf[:].rearrange("p (h t) -> p h t", h=4)
        bm4 = sbf("bm4", [128, 256], F32)
        bm2 = sbf("bm2", [128, 128], F32)
        bm2b = sbf("bm2b", [128, 128], BF16)
        hsel = sbf("hsel", [128, 2], BF16)
        m2 = sbf("m2", [128, 2], F32)
        flag = sbf("flagsb", [128, 1], F32)
        G(("memset", _a(ones32[:], 1.0)), [], ["ones32"])
        G(("memset", _a(tri2f[:], 1.0)), [], ["tri2"])
        asel(tri2[:, 0, :], tri2[:, 0, :], [[1, 128]], ALU.is_ge, 0.0, 0, -1, "tri2")
        asel(tri2[:, 1, :], tri2[:, 1, :], [[1, 128]], ALU.is_gt, 0.0, 0, -1, "tri2")
        for d in (0, 1):
            G(("memset", _a(negmf[d][:], 0.0)), [], ["negm%d" % d])
            for h in range(4):
                asel(negm[d][:, h, :], negm[d][:, h, :], [[1, 128]], ALU.is_ge if d == 0 else ALU.is_gt, -30000.0, 0, -1,
                     "negm%d" % d)
        G(("memset", _a(qmaskf[:], 1.0)), [], ["qmask"])
        G(("memset", _a(bm4[:], 1.0)), [], ["bm4"])
        for h in range(4):
            asel(qmask[:, h, :], qmask[:, h, :], [[0, 128]], ALU.is_ge, 0.0, -32 * h, 1, "qmask")
            asel(qmask[:, h, :], qmask[:, h, :], [[0, 128]], ALU.is_ge, 0.0, 32 * h + 31, -1, "qmask")
            asel(bm4[:, 64 * h:64 * h + 64], bm4[:, 64 * h:64 * h + 64], [[0, 64]], ALU.is_ge, 0.0, -32 * h, 1, "bm4")
            asel(bm4[:, 64 * h:64 * h + 64], bm4[:, 64 * h:64 * h + 64], [[0, 64]], ALU.is_ge, 0.0, 32 * h + 31, -1, "bm4")
        G(("memset", _a(bm2[:], 1.0)), [], ["bm2"])
        asel(bm2[:, 0:64], bm2[:, 0:64], [[0, 64]], ALU.is_ge, 0.0, 63, -1, "bm2")
        asel(bm2[:, 64:128], bm2[:, 64:128], [[0, 64]], ALU.is_ge, 0.0, -64, 1, "bm2")
        V(("tensor_copy", _a(out=bm2b[:], in_=bm2[:])), ["bm2"], ["bm2b"])
        G(("memset", _a(m2[:], 1.0)), [], ["m2"])
        asel(m2[:, 0:1], m2[:, 0:1], [[0, 1]], ALU.is_ge, 0.0, 63, -1, "m2")
        asel(m2[:, 1:2], m2[:, 1:2], [[0, 1]], ALU.is_ge, 0.0, -64, 1, "m2")
        V(("tensor_copy", _a(out=hsel[:], in_=m2[:])), ["m2"], ["hsel"])
        P.dma("sync", flag[:], flag_d[:, :], [], ["flag"])
        mskB = sbf("mskB", [128, 2, 2, 128], BF16)
        mskK = [sbf("mskK%d" % d, [128, 2, 2, 128], BF16) for d in (0, 1)]
        for hh in range(2):
            V(("tensor_copy", _a(out=mskB[:, hh, 0, :], in_=tri2[:, 1, :])), ["tri2"], ["mskB"])
            V(("tensor_copy", _a(out=mskB[:, hh, 1, :], in_=tri2[:, 0, :])), ["tri2"], ["mskB"])
            for d in (0, 1):
                V(("tensor_copy", _a(out=mskK[d][:, hh, 0, :], in_=tri2[:, 1, :])), ["tri2"], ["mskK%d" % d])
                V(("tensor_copy", _a(out=mskK[d][:, hh, 1, :], in_=tri2[:, d, :])), ["tri2"], ["mskK%d" % d])
        c.update(tri2f=tri2f, tri2=tri2, ones32=ones32, negmf=negmf, negm=negm, qmaskf=qmaskf, qmask=qmask, bm4=bm4,
                 bm2=bm2, bm2b=bm2b, hsel=hsel, m2=m2, flag=flag, mskB=mskB, mskK=mskK)
        return c

    ld = {"i": 0, "stg": None, "w": 1024}

    def set_stg(sbf, width):
        ld["stg"] = [sbf("stg%d" % i, [128, width], F32) for i in range(2)]
        ld["w"] = width

    def load_cast(dst_ap, src_ap, ncols, dkey, np_=128):
        i = ld["i"]
        ld["i"] += 1
        s = ld["stg"][i % 2]
        sk = "stg%d" % (i % 2)
        P.dma("sync" if i % 2 == 0 else "gpsimd", s[0:np_, 0:ncols], src_ap, [], [sk])
        eng = ("vector", "gpsimd", "scalar")[i % 3]
        if eng == "scalar":
            A(("copy", _a(out=dst_ap, in_=s[0:np_, 0:ncols])), [sk], [dkey])
        else:
            P.op(eng, ("tensor_copy", _a(out=dst_ap, in_=s[0:np_, 0:ncols])), [sk], [dkey])

    def load_w(dst, dkey, src, K, N):
        wdt = ld["w"]
        for kc in range(K // 128):
            for n0 in range(0, N, wdt):
                n1 = min(N, n0 + wdt)
                load_cast(dst[:, kc, n0:n1], src[kc * 128:(kc + 1) * 128, n0:n1], n1 - n0, dkey)

    for l in range(DEPTH):
        xsrc = x_in if l == 0 else xnext
        for d in ((0, 0) if DBG == "FF" else (0, 1)):
            if DBG == "Bonly" and d == 0:
                continue
            P.barrier()
            with ExitStack() as ss:
                sb = lambda name, shape, dt: P.sb("L%dD%d_%s_%d" % (l, d, name, len(P.ops)), shape, dt, ss)
                cst = mk_consts(sb, True)
                ident32, J32, identb, Jb = cst["ident32"], cst["J32"], cst["identb"], cst["Jb"]
                tri2f, tri2, ones32, negmf, negm = cst["tri2f"], cst["tri2"], cst["ones32"], cst["negmf"], cst["negm"]
                qmaskf, qmask, bm4, bm2, bm2b = cst["qmaskf"], cst["qmask"], cst["bm4"], cst["bm2"], cst["bm2b"]
                hsel, m2, flag, mskB, mskK = cst["hsel"], cst["m2"], cst["flag"], cst["mskB"], cst["mskK"]
                set_stg(sb, 1024)
                NB = 17 if d == 1 else 16
                fmb = fm_blocks(d)
                fmoff = np.concatenate([[0], np.cumsum([w for _, w in fmb])]).tolist()
                tmc = tm_cols(d)
                ntm = NTM[d]
                wfm = sb("wfm", [128, 8, NFM[d]], BF16)
                wtm = sb("wtm", [128, 8, ntm], BF16)
                load_w(wfm, "wfm", W["wfm", l, d], D, NFM[d])
                load_w(wtm, "wtm", W["wtm", l, d], D, ntm)
                gmix = sb("gmix", [128, 8], F32)
                P.dma("sync", gmix[:], W["gmix", l][:, :], [], ["gmix"])
                aupg = sb("aupg", [16, 128], BF16)
                load_cast(aupg[:], W["gla_aup", l, d][:, :], 128, "aupg", 16)
                abg = sb("abg", [1, 128], F32)
                P.dma("sync", abg[:], W["gla_ab", l, d][:, :], [], ["abg"])
                mu = sb("mu", [128, 9], F32)
                P.dma("sync", mu[:], W["mu", l, d][:, :], [], ["mu"])
                w0r = sb("w0r", [1, 256], F32)
                P.dma("sync", w0r[:], W["w0", l, d][:, :], [], ["w0r"])
                wup = sb("wup", [64, 256], BF16)
                load_cast(wup[:], W["wup", l, d][:, :], 256, "wup", 64)
                a0 = sb("a0", [128, 2], F32)
                P.dma("sync", a0[:], W["a0", l, d][:, :], [], ["a0"])
                aup = sb("aup", [64, 256], BF16)
                load_cast(aup[:], W["aup", l, d][:, :], 256, "aup", 64)
                dtb = sb("dtb", [128, 8], F32)
                P.dma("sync", dtb[:], W["dtb", l, d][:, :], [], ["dtb"])
                alog = sb("alog", [128, 8], F32)
                P.dma("sync", alog[:], W["alog", l, d][:, :], [], ["alog"])
                expA = sb("expA", [128, 8], F32)
                A(("activation", _a(out=expA[:], in_=alog[:], func=AF.Exp)), ["alog"], ["expA"])
                kk3 = sb("kk3", [128, 3, 2], F32)
                P.dma("sync", kk3[:], W["kk3", l].rearrange("p (a b) -> p a b", a=3), [], ["kk3"])
                omka = sb("omka", [128, 2], F32)
                V(("tensor_scalar", _a(out=omka[:], in0=kk3[:, 1, :], scalar1=-1.0, scalar2=1.0, op0=ALU.mult,
                                            op1=ALU.add)), ["kk3"], ["omka"])
                convw = sb("convw", [128, 8, 5], F32)
                P.dma("sync", convw[:], W["convw", l].rearrange("p (b j) -> p b j", b=8), [], ["convw"])
                convb = sb("convb", [128, 8], F32)
                P.dma("sync", convb[:], W["convb", l][:, :], [], ["convb"])
                dconv = sb("dconv", [128, 8, 5, 128], BF16)
                for b in range(8):
                    for j in range(5):
                        jj = j if d == 0 else 4 - j
                        V(("tensor_scalar", _a(out=dconv[:, b, j, :], in0=ident32[:],
                                                                      scalar1=convw[:, b, jj:jj + 1], scalar2=None,
                                                                      op0=ALU.mult)), ["ident32", "convw"], ["dconv"])
                omu = sb("omu", [128, 9], F32)
                hmu = sb("hmu", [128, 9], F32)
                V(("tensor_scalar", _a(out=omu[:], in0=mu[:], scalar1=-1.0, scalar2=1.0, op0=ALU.mult, op1=ALU.add)),
                  ["mu"], ["omu"])
                V(("tensor_scalar", _a(out=hmu[:], in0=mu[:], scalar1=0.5, scalar2=None, op0=ALU.mult)), ["mu"], ["hmu"])
                dsh = sb("dsh", [128, 9, 2, 128], BF16)
                for b in range(9):
                    V(("tensor_scalar", _a(out=dsh[:, b, 0, :], in0=ident32[:], scalar1=omu[:, b:b + 1],
                                                     scalar2=None, op0=ALU.mult)), ["ident32", "omu"], ["dsh"])
                    V(("tensor_scalar", _a(out=dsh[:, b, 1, :], in0=ident32[:], scalar1=hmu[:, b:b + 1],
                                                     scalar2=None, op0=ALU.mult)), ["ident32", "hmu"], ["dsh"])
                if d == 1:
                    glan = sb("glan", [128, 64], F32)
                    P.dma("sync", glan[:], W["gla_norm", l][:, :], [], ["glan"])
                    gup = sb("gup", [128, 256], BF16)
                    load_cast(gup[:], W["gup", l][:, :], 256, "gup")
                    lng = sb("lng", [128, 256], F32)
                    P.dma("sync", lng[:], W["lng", l][:, :], [], ["lng"])
                    lnb = sb("lnb", [128, 256], F32)
                    P.dma("sync", lnb[:], W["lnb", l][:, :], [], ["lnb"])
                    ssdD = sb("ssdD", [128, 8], F32)
                    P.dma("sync", ssdD[:], W["ssdD", l][:, :], [], ["ssdD"])
                    ssdn = sb("ssdn", [128, 512], F32)
                    P.dma("sync", ssdn[:], W["ssdn", l][:, :], [], ["ssdn"])
                xt = [sb("xt%d" % i, [128, D], F32) for i in range(2)]
                raw = [sb("raw%d" % i, [128, NB, 132], BF16) for i in range(2)]
                glaT = [sb("glaT%d" % i, [128, 3, 128], BF16) for i in range(2)]
                vtok = [sb("vtok%d" % i, [128, 256], BF16) for i in range(2)]
                dtt = [sb("dtt%d" % i, [128, 8], F32) for i in range(2)]
                if d == 1:
                    sgz = [sb("sgz%d" % i, [128, 768], F32) for i in range(2)]
                hn = sb("hn", [128, D], BF16)
                hT = sb("hT", [128, 8, 128], BF16)
                st8 = sb("st8", [128, 8], F32)
                dtmp = sb("dtmp", [128, 8], F32)
                S32 = sb("S32", [128, 256], F32)
                S16 = sb("S16", [128, 256], BF16)
                H32 = sb("H32", [128, 512], F32)
                H16 = sb("H16", [128, 512], BF16)
                R32 = sb("R32", [128, 2, 128], F32)
                R16 = sb("R16", [128, 2, 128], BF16)
                for t_, k_ in ((S32, "S32"), (S16, "S16"), (H32, "H32"), (H16, "H16"), (R32, "R32"), (R16, "R16")):
                    G(("memset", _a(t_[:], 0.0)), [], [k_])
                for i in range(2):
                    G(("memset", _a(raw[i][:], 0.0)), [], ["raw%d" % i])
                xbcT = sb("xbcT", [128, 8, 128], BF16)
                xsB = sb("xsB", [128, 768], BF16)
                dtA = sb("dtA", [128, 8], F32)
                ac = sb("ac", [128, 5, 8], F32)
                Ztf = sb("Zt", [128, 1024], F32)
                Zt = Ztf[:].rearrange("p (h t) -> p h t", h=8)
                seg = sb("seg", [128, 8, 128], BF16)
                cb = sb("cb", [128, 2, 128], BF16)
                MT = sb("MT", [128, 8, 128], BF16)
                xdt = sb("xdt", [128, 512], BF16)
                xdtw = sb("xdtw", [128, 512], BF16)
                ytmp = sb("ytmp", [128, 512], F32)
                ysum = sb("ysum", [128, 512], F32)
                sg = sb("sg", [128, 128], F32)
                nl = sb("nl", [128, 128], F32)
                epos = sb("epos", [128, 128], F32)
                eneg = sb("eneg", [128, 128], F32)
                qd = sb("qd", [128, 128], BF16)
                kd = sb("kd", [128, 128], BF16)
                kt = sb("kt", [128, 128], BF16)
                Qblkf = sb("Qblk", [128, 512], BF16)
                Qblk = Qblkf[:].rearrange("p (h t) -> p h t", h=4)
                Sm = sb("Sm", [128, 4, 128], BF16)
                kttok = sb("kttok", [128, 128], BF16)
                utmp = sb("utmp", [128, 256], F32)
                rkv = sb("rkv", [128, 6, 128], F32)
                twT = sb("twT", [64, 128], BF16)
                afT = sb("afT", [64, 128], BF16)
                sgd = sb("sgd", [128, 128], BF16)
                nlw = sb("nlw", [128, 256], F32)
                gin = sb("gin", [128, 2, 128], F32)
                gex = sb("gex", [128, 2, 128], F32)
                ginv = sb("ginv", [128, 2, 128], F32)
                alpha = sb("alpha", [128, 2, 128], F32)
                kkk = sb("kkk", [128, 2, 128], F32)
                sq = sb("sq", [128, 2, 128], BF16)
                rinv = sb("rinv", [128, 2, 128], F32)
                kkn = sb("kkn", [128, 2, 128], F32)
                t1 = sb("t1", [128, 2, 128], F32)
                kdir = sb("kdir", [128, 2, 128], F32)
                bvec = sb("bvec", [128, 2, 128], F32)
                RtT = sb("RtT", [128, 2, 128], BF16)
                AtT = sb("AtT", [128, 2, 128], BF16)
                BtT = sb("BtT", [128, 2, 128], BF16)
                KtT = sb("KtT", [128, 2, 128], BF16)
                fmT = sb("fmT", [128, 4, 2, 128], BF16)
                tok4 = sb("tok4", [128, 4, 256], BF16)
                RAbf = sb("RAb", [128, 2, 512], BF16)
                RAb = RAbf[:].rearrange("p b (h x t) -> p b h x t", h=2, x=2)
                SCB = sb("SCB", [128, 2, 2, 2, 128], BF16)
                SCK = sb("SCK", [128, 2, 2, 2, 128], BF16)
                Ak = [sb("Ak%d" % i, [128, 4, 128], BF16) for i in range(2)]
                Nk = [sb("Nk%d" % i, [128, 4, 128], BF16) for i in range(2)]
                T32 = sb("T32", [128, 4, 128], F32)
                T16f = sb("T16", [128, 512], BF16)
                T16 = T16f[:].rearrange("p (h t) -> p h t", h=4)
                WtT = sb("WtT", [128, 2, 128], BF16)
                X2 = sb("X2", [128, 256], BF16)
                U16 = sb("U16", [128, 256], BF16)
                rtmp = sb("rtmp", [128, 2, 128], F32)
                if d == 0:
                    osb = sb("osb", [128, 512], F32)
                if d == 1:
                    ofs = sb("ofs", [128, D], F32)
                    mix = sb("mix", [128, D], BF16)
                    fa = sb("fa", [128, 512], F32)
                    fb = sb("fb", [128, 512], F32)
                    st4 = sb("st4", [128, 16], F32)
                    prodT = sb("prodT", [128, 2, 128], BF16)

                identX = identb if (d == 0 or DBG3 == "noJ") else Jb
                if os.environ.get("KMEM"):
                    print("SBUF remaining after sweep allocs l=%d d=%d:" % (l, d), nc.sbuf_bytes_remaining)
                order = list(range(NCH)) if d == 0 else list(range(NCH - 1, -1, -1))

                def stageA(i):
                    ci = order[i]
                    s3, s2 = i % 2, i % 2
                    xk, rk_, gk, vk, dk = "xt%d" % s3, "raw%d" % s3, "glaT%d" % s2, "vtok%d" % s2, "dtt%d" % s2
                    P.dma("sync", xt[s3][:], xsrc[ci * C:(ci + 1) * C, :], [], [xk])
                    A(("activation", _a(out=hn[:], in_=xt[s3][:], func=AF.Square, accum_out=st8[:, 0:1])),
                      [xk], ["hn", "st8a"])
                    A(("activation", _a(out=st8[:, 1:2], in_=st8[:, 0:1], func=AF.Sqrt, scale=1.0 / D, bias=EPS)),
                      ["st8a"], ["st8b"])
                    V(("reciprocal", _a(out=st8[:, 2:3], in_=st8[:, 1:2])), ["st8b"], ["st8c"])
                    V(("tensor_scalar", _a(out=hn[:], in0=xt[s3][:], scalar1=st8[:, 2:3], scalar2=None, op0=ALU.mult)),
                      [xk, "st8c"], ["hn"])
                    b_, bk = bank()
                    for kc in range(8):
                        T(("transpose", _a(bf(b_)[:, kc * 128:(kc + 1) * 128], hn[:, kc * 128:(kc + 1) * 128],
                                                       identX[:])), ["hn", "identb", "Jb"], [bk])
                    V(("tensor_tensor", _a(out=hT[:], in0=bf(b_).rearrange("p (a b) -> p a b", a=8),
                                                in1=gmix[:].unsqueeze(2).to_broadcast([128, 8, 128]), op=ALU.mult)),
                      [bk, "gmix"], ["hT"])
                    nblk = len(fmb)
                    if DBG3 == "noFM4":
                        nblk = 16
                    if DBG3 == "noFM":
                        nblk = 0
                    fgroups = [list(range(g0, min(NB, g0 + 4))) for g0 in range(0, NB, 4)] + [list(range(NB, len(fmb)))]
                    if nblk < len(fmb):
                        fgroups = [g_ for g_ in fgroups if g_ and g_[-1] < nblk]
                    for grp in fgroups:
                        b_, bk = bank()
                        for j, bi in enumerate(grp):
                            wdt = fmb[bi][1]
                            for kc in range(8):
                                mm(b_[0:wdt, j * 128:(j + 1) * 128], wfm[:, kc, fmoff[bi]:fmoff[bi] + wdt], hT[:, kc, :],
                                   kc == 0, kc == 7, ["wfm", "hT"], [bk])
                        for j, bi in enumerate(grp):
                            wdt = fmb[bi][1]
                            if bi < NB:
                                A(("copy", _a(out=raw[s3][0:wdt, bi, 2:130],
                                                                        in_=b_[0:wdt, j * 128:(j + 1) * 128])), [bk], [rk_])
                            else:
                                V(("tensor_copy", _a(out=glaT[s2][0:wdt, bi - NB, :],
                                                                                in_=b_[0:wdt, j * 128:(j + 1) * 128])),
                                  [bk], [gk])
                    off = 0
                    groups = [[(0, 256), (256 if d == 0 else 1024, 8)]] if d == 0 else \
                        [[(0, 256), (256, 256)], [(512, 512)], [(1024, 8)]]
                    if DBG3 == "noTM":
                        groups = groups[:1]
                    if DBG3 == "noTM0":
                        groups = []
                    for grp in groups:
                        b_, bk = bank()
                        o = 0
                        for (c0, wdt) in grp:
                            for kc in range(8):
                                mm(b_[:, o:o + wdt], hT[:, kc, :], wtm[:, kc, c0:c0 + wdt], kc == 0, kc == 7, ["hT", "wtm"], [bk])
                            if c0 == 0:
                                V(("tensor_copy", _a(out=vtok[s2][:], in_=b_[:, o:o + 256])), [bk], [vk])
                            elif wdt == 8:
                                V(("tensor_tensor", _a(out=dtmp[:], in0=b_[:, o:o + 8], in1=dtb[:], op=ALU.add)),
                                  [bk, "dtb"], ["dtmp"])
                                A(("activation", _a(out=dtmp[:], in_=dtmp[:], func=AF.Exp)), ["dtmp"], ["dtmp"])
                                A(("activation", _a(out=dtt[s2][:], in_=dtmp[:], func=AF.Ln, bias=1.0)), ["dtmp"], [dk])
                            elif c0 == 256:
                                A(("activation", _a(out=sgz[s2][:, 0:256], in_=b_[:, o:o + 256], func=AF.Silu)),
                                  [bk], ["sgz%d" % s2])
                            else:
                                A(("activation", _a(out=sgz[s2][:, 256:768], in_=b_[:, o:o + 512], func=AF.Silu)),
                                  [bk], ["sgz%d" % s2])
                            o += wdt
                    if i > 0:
                        p3 = (i - 1) % 2
                        pk = "raw%d" % p3
                        if i % CPS == 0:
                            G(("tensor_scalar", _a(out=raw[s3][:, :, 0:2], in0=raw[p3][:, :, 128:130], scalar1=flag[:, 0:1],
                                                        scalar2=None, op0=ALU.mult)), [pk, rk_, "flag"], [rk_ + "h"])
                            G(("tensor_scalar", _a(out=raw[p3][:, :, 130:132], in0=raw[s3][:, :, 2:4], scalar1=flag[:, 0:1],
                                                        scalar2=None, op0=ALU.mult)), [pk, rk_, "flag"], [pk + "h"])
                        else:
                            G(("tensor_copy", _a(out=raw[s3][:, :, 0:2], in_=raw[p3][:, :, 128:130])), [pk, rk_], [rk_ + "h"])
                            G(("tensor_copy", _a(out=raw[p3][:, :, 130:132], in_=raw[s3][:, :, 2:4])), [pk, rk_], [pk + "h"])
                    else:
                        G(("memset", _a(raw[s3][:, :, 0:2], 0.0)), [rk_], [rk_ + "h"])
                    if i == NCH - 1:
                        G(("memset", _a(raw[s3][:, :, 130:132], 0.0)), [rk_], [rk_ + "h"])

                def stageB(i):
                    ci = order[i]
                    s3, s2 = i % 2, i % 2
                    xk, rk_, gk, vk, dk = "xt%d" % s3, "raw%d" % s3, "glaT%d" % s2, "vtok%d" % s2, "dtt%d" % s2
                    rw = raw[s3]
                    rr = [rk_, rk_ + "h"]
                    boundary = (i % CPS == CPS - 1)
                    useJ = (d == 1) and DBG not in ("B1noJ",)
                    if d == 1 and DBG != "B1noDMA":
                        P.dma("gpsimd", ofs[:], ofd[ci * C:(ci + 1) * C, :], [], ["ofs"])
                    bo_, bok = banks[7], "bank7"
                    shared = {"gla_o": False}
                    def ssd_gen():
                        for g0 in (0, 4):
                            if g0:
                                yield
                            b_, bk = bank()
                            for j in range(4):
                                b = g0 + j
                                for tp in range(5):
                                    mm(b_[:, j * 128:(j + 1) * 128], dconv[:, b, tp, :], rw[:, b, tp:tp + 128], tp == 0, tp == 4,
                                       ["dconv"] + rr, [bk])
                            for j in range(4):
                                b = g0 + j
                                A(("activation", _a(out=xbcT[:, b, :], in_=b_[:, j * 128:(j + 1) * 128], func=AF.Silu,
                                                                   bias=convb[:, b:b + 1])), [bk, "convb"], ["xbcT"])
                        yield
                        b_, bk = bank()
                        for b in range(6):
                            T(("transpose", _a(bf(b_)[:, b * 128:(b + 1) * 128], xbcT[:, b, :], identb[:])), ["xbcT", "identb"], [bk])
                        V(("tensor_copy", _a(out=xsB[:], in_=bf(b_)[:, 0:768])), [bk], ["xsB"])
                        V(("scalar_tensor_tensor", _a(out=dtA[:], in0=dtt[s2][:], scalar=-1.0, in1=expA[:], op0=ALU.mult, op1=ALU.mult)),
                          [dk, "expA"], ["dtA"])
                        yield
                        b_, bk = bank()
                        mm(b_[:, 0:8], tri2[:, 0, :], dtA[:], True, True, ["tri2", "dtA"], [bk])
                        mm(b_[:, 8:16], ones32[:], dtA[:], True, True, ["ones32", "dtA"], [bk])
                        V(("tensor_copy", _a(out=ac[:, 0, :], in_=b_[:, 0:8])), [bk], ["ac0"])
                        V(("tensor_scalar", _a(out=ac[:, 1, :], in0=b_[:, 0:8], scalar1=-1.0, scalar2=None, op0=ALU.mult)), [bk], ["ac1"])
                        A(("activation", _a(out=ac[:, 2, :], in_=b_[:, 0:8], func=AF.Exp)), [bk], ["ac2"])
                        A(("activation", _a(out=ac[:, 4, :], in_=b_[:, 8:16], func=AF.Exp)), [bk], ["ac4"])
                        V(("tensor_tensor", _a(out=ac[:, 3, :], in0=b_[:, 8:16], in1=ac[:, 0, :], op=ALU.subtract)), [bk, "ac0"], ["ac3"])
                        A(("activation", _a(out=ac[:, 3, :], in_=ac[:, 3, :], func=AF.Exp)), ["ac3"], ["ac3"])
                        V(("tensor_tensor", _a(out=ac[:, 3, :], in0=ac[:, 3, :], in1=dtt[s2][:], op=ALU.mult)), ["ac3", dk], ["ac3"])
                        G(("tensor_tensor", _a(out=Zt, in0=tri2[:, 0:1, :].to_broadcast([128, 8, 128]),
                                                    in1=dtA[:].unsqueeze(2).to_broadcast([128, 8, 128]), op=ALU.mult)),
                          ["tri2", "dtA"], ["Zt"])
                        yield
                        for hg in range(2):
                            if hg:
                                yield
                            b_, bk = bank()
                            mm(b_[:, :], ones32[:], Ztf[:, 512 * hg:512 * hg + 512], True, False, ["ones32", "Zt"], [bk])
                            mm(b_[:, :], identb[:], negmf[d][:], False, True, ["identb", "negm%d" % d], [bk])
                            for hh in range(4):
                                h = 4 * hg + hh
                                A(("activation", _a(out=seg[:, h, :], in_=b_[:, hh * 128:(hh + 1) * 128], func=AF.Exp,
                                                                          bias=ac[:, 1, h:h + 1])), [bk, "ac1"], ["seg"])
                        yield
                        b_, bk = bank()
                        for g in range(2):
                            mm(b_[:, g * 128:(g + 1) * 128], xbcT[:, 4 + g, :], xbcT[:, 6 + g, :], True, True, ["xbcT"], [bk])
                        V(("tensor_copy", _a(out=cb[:], in_=b_[:, 0:256])), [bk], ["cb"])
                        V(("tensor_tensor", _a(out=MT[:].rearrange("p (g r) t -> p g r t", g=2),
                                                    in0=seg[:].rearrange("p (g r) t -> p g r t", g=2),
                                                    in1=cb[:].unsqueeze(2).to_broadcast([128, 2, 4, 128]), op=ALU.mult)),
                          ["seg", "cb"], ["MT"])
                        V(("tensor_tensor", _a(out=xdt[:].rearrange("p (h q) -> p h q", h=8),
                                                    in0=xsB[:, 0:512].rearrange("p (h q) -> p h q", h=8),
                                                    in1=dtt[s2][:].unsqueeze(2).to_broadcast([128, 8, 64]), op=ALU.mult)),
                          ["xsB", dk], ["xdt"])
                        G(("tensor_tensor", _a(out=xdtw[:].rearrange("p (h q) -> p h q", h=8),
                                                    in0=xsB[:, 0:512].rearrange("p (h q) -> p h q", h=8),
                                                    in1=ac[:, 3, :].unsqueeze(2).to_broadcast([128, 8, 64]), op=ALU.mult)),
                          ["xsB", "ac3"], ["xdtw"])
                        yield
                        by_, byk = bank()
                        if useJ:
                            mm(by_[:, :], J32[:], ofs[:, 512:1024], True, False, ["J32", "ofs"], [byk])
                        for h in range(8):
                            mm(by_[:, 64 * h:64 * h + 64], MT[:, h, :], xdt[:, 64 * h:64 * h + 64], (not useJ) and h == 0, h == 7, ["MT", "xdt"], [byk])
                        b_, bk = bank()
                        for g in range(2):
                            mm(b_[:, 256 * g:256 * g + 256], xbcT[:, 6 + g, :], H16[:, 256 * g:256 * g + 256], True, True, ["xbcT", "H16"], [bk])
                        V(("tensor_tensor", _a(out=ytmp[:].rearrange("p (h q) -> p h q", h=8),
                                                    in0=b_[:, :].rearrange("p (h q) -> p h q", h=8),
                                                    in1=ac[:, 2, :].unsqueeze(2).to_broadcast([128, 8, 64]), op=ALU.mult)),
                          [bk, "ac2"], ["ytmp"])
                        V(("tensor_tensor", _a(out=ysum[:], in0=by_[:, :], in1=ytmp[:], op=ALU.add)), [byk, "ytmp"], ["ysum"])
                        yield
                        b_, bk = bank()
                        for g in range(2):
                            mm(b_[:, 256 * g:256 * g + 256], xsB[:, 512 + 128 * g:512 + 128 * g + 128], xdtw[:, 256 * g:256 * g + 256], True, True,
                               ["xsB", "xdtw"], [bk])
                        G(("tensor_tensor", _a(out=ytmp[:].rearrange("p (h q) -> p h q", h=8),
                                                    in0=H32[:].rearrange("p (h q) -> p h q", h=8),
                                                    in1=ac[:, 4, :].unsqueeze(2).to_broadcast([128, 8, 64]), op=ALU.mult)),
                          ["H32", "ac4", "ytmp"], ["ytmp"])
                        V(("tensor_tensor", _a(out=H32[:], in0=b_[:, :], in1=ytmp[:], op=ALU.add)), [bk, "ytmp"], ["H32"])
                        if boundary:
                            V(("tensor_scalar", _a(out=H32[:], in0=H32[:], scalar1=flag[:, 0:1], scalar2=None, op0=ALU.mult)),
                              ["H32", "flag"], ["H32"])
                        A(("copy", _a(out=H16[:], in_=H32[:])), ["H32"], ["H16"])
                        if d == 0:
                            P.dma("gpsimd", ofd[ci * C:(ci + 1) * C, 512:1024], ysum[:], ["ysum"], [])


                    def gla_gen():
                        gT = glaT[s2]
                        b_, bk = bank()
                        mm(b_[:, 0:128], gT[0:16, 2, :], aupg[:], True, False, [gk, "aupg"], [bk])
                        mm(b_[:, 0:128], ones32[0:1, :], abg[:], False, True, ["ones32", "abg"], [bk])
                        A(("activation", _a(out=sg[:], in_=b_[:, 0:128], func=AF.Sigmoid)), [bk], ["sg"])
                        A(("activation", _a(out=nl[:], in_=sg[:], func=AF.Ln)), ["sg"], ["nl"])
                        yield
                        b_, bk = bank()
                        mm(b_[:, 0:128], nl[:], tri2[:, 0, :], True, True, ["nl", "tri2"], [bk])
                        A(("activation", _a(out=epos[:], in_=b_[:, 0:128], func=AF.Exp, scale=1.0 / 16)), [bk], ["epos"])
                        A(("activation", _a(out=eneg[:], in_=b_[:, 0:128], func=AF.Exp, scale=-1.0 / 16)), [bk], ["eneg"])
                        V(("scalar_tensor_tensor", _a(out=qd[:], in0=gT[:, 0, :], scalar=32.0 ** -0.5, in1=epos[:], op0=ALU.mult, op1=ALU.mult)),
                          [gk, "epos"], ["qd"])
                        V(("tensor_tensor", _a(out=kd[:], in0=gT[:, 1, :], in1=eneg[:], op=ALU.mult)), [gk, "eneg"], ["kd"])
                        G(("tensor_scalar", _a(out=kt[:], in0=kd[:], scalar1=epos[:, 127:128], scalar2=None, op0=ALU.mult)), ["kd", "epos"], ["kt"])
                        G(("tensor_tensor", _a(out=Qblk, in0=qd[:].unsqueeze(1).to_broadcast([128, 4, 128]), in1=qmask, op=ALU.mult)),
                          ["qd", "qmask"], ["Qblk"])
                        yield
                        b_, bk = bank()
                        mm(b_[:, :], kd[:], Qblkf[:], True, True, ["kd", "Qblk"], [bk])
                        V(("tensor_tensor", _a(out=Sm[:], in0=b_[:, :].rearrange("p (h t) -> p h t", h=4),
                                                    in1=tri2[:, d:d + 1, :].to_broadcast([128, 4, 128]), op=ALU.mult)), [bk, "tri2"], ["Sm"])
                        yield
                        b_, bk = bank()
                        T(("transpose", _a(bf(b_)[:, 0:128], kt[:], identb[:])), ["kt", "identb"], [bk])
                        A(("copy", _a(out=kttok[:], in_=bf(b_)[:, 0:128])), [bk], ["kttok"])
                        yield
                        if useJ:
                            mm(bo_[:, :], J32[:], ofs[:, 0:512], True, False, ["J32", "ofs"], [bok])
                        mm(bo_[:, 0:256], qd[:], S16[:], not useJ, False, ["qd", "S16"], [bok])
                        for h in range(4):
                            mm(bo_[:, 64 * h:64 * h + 64], Sm[:, h, :], vtok[s2][:, 64 * h:64 * h + 64], False, False, ["Sm", vk], [bok])
                        yield
                        shared["gla_o"] = True
                        b_, bk = bank()
                        mm(b_[:, 0:256], kttok[:], vtok[s2][:], True, True, ["kttok", vk], [bk])
                        V(("tensor_tensor", _a(out=utmp[:], in0=b_[:, 0:256], in1=bm4[:], op=ALU.mult)), [bk, "bm4"], ["utmp"])
                        V(("scalar_tensor_tensor", _a(out=S32[:], in0=S32[:], scalar=epos[:, 127:128], in1=utmp[:], op0=ALU.mult, op1=ALU.add)),
                          ["S32", "epos", "utmp"], ["S32"])
                        if boundary:
                            V(("tensor_scalar", _a(out=S32[:], in0=S32[:], scalar1=flag[:, 0:1], scalar2=None, op0=ALU.mult)),
                              ["S32", "flag"], ["S32"])
                        G(("tensor_copy", _a(out=S16[:], in_=S32[:])), ["S32"], ["S16"])


                    def rwkv_gen():
                        nrb = 9 if d == 1 else 8
                        bsh = []
                        for g0 in range(0, nrb, 4):
                            b_, bk = bank()
                            bsh.append((b_, bk))
                            for j in range(min(4, nrb - g0)):
                                bb = g0 + j
                                m = 64 if bb in (6, 7) else 128
                                rb = 8 + bb
                                mm(b_[0:m, j * 128:(j + 1) * 128], dsh[0:m, bb, 0, 0:m], rw[0:m, rb, 2:130], True, False, ["dsh"] + rr, [bk])
                                mm(b_[0:m, j * 128:(j + 1) * 128], dsh[0:m, bb, 1, 0:m], rw[0:m, rb, 1:129], False, False, ["dsh"] + rr, [bk])
                                mm(b_[0:m, j * 128:(j + 1) * 128], dsh[0:m, bb, 1, 0:m], rw[0:m, rb, 3:131], False, True, ["dsh"] + rr, [bk])
                        A(("copy", _a(out=rkv[:, 0:4, :], in_=bsh[0][0][:, :].rearrange("p (a t) -> p a t", a=4))), [bsh[0][1]], ["rkv"])
                        V(("tensor_copy", _a(out=rkv[:, 4:6, :], in_=bsh[1][0][:, 0:256].rearrange("p (a t) -> p a t", a=2))), [bsh[1][1]], ["rkv"])
                        A(("activation", _a(out=twT[:], in_=bsh[1][0][0:64, 256:384], func=AF.Tanh)), [bsh[1][1]], ["twT"])
                        V(("tensor_copy", _a(out=afT[:], in_=bsh[1][0][0:64, 384:512])), [bsh[1][1]], ["afT"])
                        if d == 1:
                            A(("activation", _a(out=sgd[:], in_=bsh[2][0][:, 0:128], func=AF.Sigmoid)), [bsh[2][1]], ["sgd"])
                        yield
                        b_, bk = bank()
                        mm(b_[:, 0:256], twT[:], wup[:], True, False, ["twT", "wup"], [bk])
                        mm(b_[:, 0:256], ones32[0:1, :], w0r[:], False, True, ["ones32", "w0r"], [bk])
                        A(("activation", _a(out=nlw[:], in_=b_[:, 0:256], func=AF.Sigmoid)), [bk], ["nlw"])
                        yield
                        b_, bk = bank()
                        for bb in range(2):
                            mm(b_[:, 256 * bb:256 * bb + 256], nlw[:, 128 * bb:128 * bb + 128], tri2f[:], True, True, ["nlw", "tri2"], [bk])
                        cw = b_[:, :].rearrange("p (b x t) -> p b x t", b=2, x=2)
                        LW = 0.6065306597126334
                        A(("activation", _a(out=gin[:], in_=cw[:, :, 0, :], func=AF.Exp, scale=-LW)), [bk], ["gin"])
                        A(("activation", _a(out=gex[:], in_=cw[:, :, 1, :], func=AF.Exp, scale=-LW)), [bk], ["gex"])
                        A(("activation", _a(out=ginv[:], in_=cw[:, :, 0, :], func=AF.Exp, scale=LW)), [bk], ["ginv"])
                        yield
                        b_, bk = bank()
                        for bb in range(2):
                            mm(b_[:, 128 * bb:128 * bb + 128], aup[:, 128 * bb:128 * bb + 128], afT[:], True, True, ["aup", "afT"], [bk])
                        for bb in range(2):
                            A(("activation", _a(out=alpha[:, bb, :], in_=b_[:, 128 * bb:128 * bb + 128], func=AF.Sigmoid,
                                                                 bias=a0[:, bb:bb + 1])), [bk, "a0"], ["alpha"])
                        V(("tensor_tensor", _a(out=kkk[:], in0=rkv[:, 2:4, :], in1=kk3[:, 0, :].unsqueeze(2).to_broadcast([128, 2, 128]), op=ALU.mult)),
                          ["rkv", "kk3"], ["kkk"])
                        G(("tensor_tensor", _a(out=sq[:], in0=kkk[:], in1=kkk[:], op=ALU.mult)), ["kkk"], ["sq"])
                        yield
                        b_, bk = bank()
                        for bb in range(2):
                            mm(b_[:, 128 * bb:128 * bb + 128], bm2b[:], sq[:, bb, :], True, True, ["bm2b", "sq"], [bk])
                        A(("activation", _a(out=rinv[:], in_=b_[:, 0:256].rearrange("p (b t) -> p b t", b=2), func=AF.Sqrt, bias=1e-12)),
                          [bk], ["rinv"])
                        V(("reciprocal", _a(out=rinv[:], in_=rinv[:])), ["rinv"], ["rinv"])
                        V(("tensor_tensor", _a(out=kkn[:], in0=kkk[:], in1=rinv[:], op=ALU.mult)), ["kkk", "rinv"], ["kkn"])
                        G(("tensor_tensor", _a(out=t1[:], in0=alpha[:], in1=kk3[:, 1, :].unsqueeze(2).to_broadcast([128, 2, 128]), op=ALU.mult)),
                          ["alpha", "kk3"], ["t1"])
                        G(("tensor_tensor", _a(out=t1[:], in0=t1[:], in1=omka[:].unsqueeze(2).to_broadcast([128, 2, 128]), op=ALU.add)),
                          ["t1", "omka"], ["t1"])
                        G(("tensor_tensor", _a(out=kdir[:], in0=rkv[:, 2:4, :], in1=t1[:], op=ALU.mult)), ["rkv", "t1"], ["kdir"])
                        V(("tensor_tensor", _a(out=bvec[:], in0=kkn[:], in1=alpha[:], op=ALU.mult)), ["kkn", "alpha"], ["bvec"])
                        V(("tensor_tensor", _a(out=RtT[:], in0=rkv[:, 0:2, :], in1=gin[:], op=ALU.mult)), ["rkv", "gin"], ["RtT"])
                        V(("scalar_tensor_tensor", _a(out=AtT[:], in0=kkn[:], scalar=-1.0, in1=gex[:], op0=ALU.mult, op1=ALU.mult)),
                          ["kkn", "gex"], ["AtT"])
                        V(("tensor_tensor", _a(out=BtT[:], in0=bvec[:], in1=ginv[:], op=ALU.mult)), ["bvec", "ginv"], ["BtT"])
                        G(("tensor_tensor", _a(out=KtT[:], in0=kdir[:], in1=ginv[:], op=ALU.mult)), ["kdir", "ginv"], ["KtT"])
                        G(("tensor_copy", _a(out=fmT[:, 0, :, :], in_=AtT[:])), ["AtT"], ["fmT0"])
                        V(("tensor_tensor", _a(out=fmT[:, 1, :, :], in0=BtT[:], in1=gin[:, :, 127:128].to_broadcast([128, 2, 128]), op=ALU.mult)),
                          ["BtT", "gin"], ["fmT1"])
                        G(("tensor_tensor", _a(out=fmT[:, 2, :, :], in0=KtT[:], in1=gin[:, :, 127:128].to_broadcast([128, 2, 128]), op=ALU.mult)),
                          ["KtT", "gin"], ["fmT2"])
                        G(("tensor_copy", _a(out=fmT[:, 3, :, :], in_=rkv[:, 4:6, :])), ["rkv"], ["fmT3"])
                        yield
                        b_, bk = bank()
                        for a in range(4):
                            for bb in range(2):
                                T(("transpose", _a(bf(b_)[:, (2 * a + bb) * 128:(2 * a + bb + 1) * 128], fmT[:, a, bb, :], identb[:])),
                                  ["fmT%d" % a, "identb"], [bk])
                        V(("tensor_copy", _a(out=tok4[:].rearrange("p a c -> p (a c)"), in_=bf(b_)[:, :])), [bk], ["tok4"])
                        yield
                        for bb in range(2):
                            G(("tensor_tensor", _a(out=RAb[:, bb, :, 0, :], in0=AtT[:, bb:bb + 1, :].to_broadcast([128, 2, 128]),
                                                               in1=m2[:].unsqueeze(2).to_broadcast([128, 2, 128]), op=ALU.mult)), ["AtT", "m2"], ["RAb"])
                            V(("tensor_tensor", _a(out=RAb[:, bb, :, 1, :], in0=RtT[:, bb:bb + 1, :].to_broadcast([128, 2, 128]),
                                                               in1=m2[:].unsqueeze(2).to_broadcast([128, 2, 128]), op=ALU.mult)), ["RtT", "m2"], ["RAb"])
                        for bb in range(2):
                            yield
                            b_, bk = bank()
                            mm(b_[:, :], BtT[:, bb, :], RAbf[:, bb, :], True, True, ["BtT", "RAb"], [bk])
                            V(("tensor_tensor", _a(out=SCB[:, bb, :, :, :], in0=b_[:, :].rearrange("p (h x t) -> p h x t", h=2, x=2),
                                                                    in1=mskB[:], op=ALU.mult)), [bk, "mskB"], ["SCB"])
                            b_, bk = bank()
                            mm(b_[:, :], KtT[:, bb, :], RAbf[:, bb, :], True, True, ["KtT", "RAb"], [bk])
                            V(("tensor_tensor", _a(out=SCK[:, bb, :, :, :], in0=b_[:, :].rearrange("p (h x t) -> p h x t", h=2, x=2),
                                                                    in1=mskK[d][:], op=ALU.mult)), [bk, "mskK%d" % d], ["SCK"])
                        yield
                        N0v = SCB[:].rearrange("p b h x t -> p (b h) x t")[:, :, 0, :]
                        G(("tensor_copy", _a(out=Nk[0][:], in_=N0v)), ["SCB"], ["Nk0"])
                        b_, bk = bank()
                        for h in range(4):
                            T(("transpose", _a(bf(b_)[:, h * 128:(h + 1) * 128], Nk[0][:, h, :], identb[:])), ["Nk0", "identb"], [bk])
                        A(("copy", _a(out=Ak[0][:], in_=bf(b_)[:, 0:512].rearrange("p (h t) -> p h t", h=4))), [bk], ["Ak0"])
                        V(("tensor_tensor", _a(out=T32[:], in0=Nk[0][:], in1=ident32[:].unsqueeze(1).to_broadcast([128, 4, 128]), op=ALU.add)),
                          ["Nk0", "ident32"], ["T32"])
                        G(("tensor_copy", _a(out=T16, in_=T32[:])), ["T32"], ["T16"])
                        yield
                        for lev in range(6):
                            if lev:
                                yield
                            c_, n_ = lev % 2, (lev + 1) % 2
                            ba_, bak = bank()
                            bn_, bnk = bank()
                            for h in range(4):
                                mm(ba_[:, h * 128:(h + 1) * 128], Nk[c_][:, h, :], Ak[c_][:, h, :], True, True, ["Nk%d" % c_, "Ak%d" % c_], [bak])
                            for h in range(4):
                                mm(bn_[:, h * 128:(h + 1) * 128], Ak[c_][:, h, :], Nk[c_][:, h, :], True, True, ["Nk%d" % c_, "Ak%d" % c_], [bnk])
                            A(("copy", _a(out=Ak[n_][:], in_=ba_[:, :].rearrange("p (h t) -> p h t", h=4))), [bak], ["Ak%d" % n_])
                            if lev < 5:
                                V(("tensor_copy", _a(out=Nk[n_][:], in_=bn_[:, :].rearrange("p (h t) -> p h t", h=4))), [bnk], ["Nk%d" % n_])
                            bt_, btk = bank()
                            for h in range(4):
                                mm(bt_[:, h * 128:(h + 1) * 128], Ak[n_][:, h, :], T16[:, h, :], True, True, ["Ak%d" % n_, "T16"], [btk])
                            V(("tensor_tensor", _a(out=T32[:], in0=bt_[:, :].rearrange("p (h t) -> p h t", h=4), in1=T32[:], op=ALU.add)),
                              [btk, "T32"], ["T32"])
                            G(("tensor_copy", _a(out=T16, in_=T32[:])), ["T32"], ["T16"])
                        yield
                        for bb in range(2):
                            b_, bk = bank()
                            mm(b_[:, 0:256], tok4[:, 0, 128 * bb:128 * bb + 128], T16f[:, 256 * bb:256 * bb + 256], True, True, ["tok4", "T16"], [bk])
                            V(("tensor_copy", _a(out=WtT[0:64, bb, :], in_=b_[0:64, 0:128])), [bk], ["WtT"])
                            A(("copy", _a(out=WtT[64:128, bb, :], in_=b_[64:128, 128:256])), [bk], ["WtT"])
                        yield
                        b_, bk = bank()
                        for h in range(4):
                            mm(b_[:, 64 * h:64 * h + 64], SCK[:, h // 2, h % 2, 0, :], tok4[:, 3, 64 * h:64 * h + 64], True, True, ["SCK", "tok4"], [bk])
                        V(("tensor_copy", _a(out=X2[:], in_=b_[:, 0:256])), [bk], ["X2"])
                        yield
                        b_, bk = bank()
                        for h in range(4):
                            mm(b_[:, 64 * h:64 * h + 64], T16[:, h, :], X2[:, 64 * h:64 * h + 64], h == 0, False, ["T16", "X2"], [bk])
                        for bb in range(2):
                            mm(b_[:, 128 * bb:128 * bb + 128], WtT[:, bb, :], R16[:, bb, :], False, bb == 1, ["WtT", "R16"], [bk])
                        A(("copy", _a(out=U16[:], in_=b_[:, 0:256])), [bk], ["U16"])
                        yield
                        while not shared["gla_o"]:
                            yield
                        for bb in range(2):
                            mm(bo_[:, 256 + 128 * bb:256 + 128 * bb + 128], RtT[:, bb, :], R16[:, bb, :], False, False, ["RtT", "R16"], [bok])
                        for h in range(4):
                            mm(bo_[:, 256 + 64 * h:256 + 64 * h + 64], SCB[:, h // 2, h % 2, 1, :], U16[:, 64 * h:64 * h + 64], False, False, ["SCB", "U16"], [bok])
                            mm(bo_[:, 256 + 64 * h:256 + 64 * h + 64], SCK[:, h // 2, h % 2, 1, :], tok4[:, 3, 64 * h:64 * h + 64], False, h == 3, ["SCK", "tok4"], [bok])
                        yield
                        b_, bk = bank()
                        for bb in range(2):
                            mm(b_[:, 128 * bb:128 * bb + 128], tok4[:, 1, 128 * bb:128 * bb + 128], U16[:, 128 * bb:128 * bb + 128], True, False, ["tok4", "U16"], [bk])
                            mm(b_[:, 128 * bb:128 * bb + 128], tok4[:, 2, 128 * bb:128 * bb + 128], tok4[:, 3, 128 * bb:128 * bb + 128], False, True, ["tok4"], [bk])
                        V(("tensor_tensor", _a(out=rtmp[:], in0=b_[:, 0:256].rearrange("p (b t) -> p b t", b=2),
                                                          in1=bm2[:].unsqueeze(1).to_broadcast([128, 2, 128]), op=ALU.mult)), [bk, "bm2"], ["rtmp"])
                        G(("tensor_tensor", _a(out=R32[:], in0=R32[:], in1=gin[:, :, 127:128].to_broadcast([128, 2, 128]), op=ALU.mult)),
                          ["R32", "gin"], ["R32"])
                        G(("tensor_tensor", _a(out=R32[:], in0=R32[:], in1=rtmp[:], op=ALU.add)), ["R32", "rtmp"], ["R32"])
                        if boundary:
                            G(("tensor_scalar", _a(out=R32[:], in0=R32[:], scalar1=flag[:, 0:1], scalar2=None, op0=ALU.mult)),
                              ["R32", "flag"], ["R32"])
                        G(("tensor_copy", _a(out=R16[:], in_=R32[:])), ["R32"], ["R16"])


                    KIL = os.environ.get("KIL", "sr")
                    if KIL == "seq":
                        plan = [[ssd_gen()], [gla_gen()], [rwkv_gen()]]
                    elif KIL == "seq2":
                        plan = [[rwkv_gen()], [ssd_gen()], [gla_gen()]]
                    elif KIL == "sg":
                        plan = [[ssd_gen(), gla_gen()], [rwkv_gen()]]
                    elif KIL == "sr":
                        plan = [[gla_gen()], [rwkv_gen(), ssd_gen()]]
                    else:
                        plan = [[rwkv_gen(), ssd_gen(), gla_gen()]]
                    for gens in plan:
                        while gens:
                            for g_ in list(gens):
                                try:
                                    next(g_)
                                except StopIteration:
                                    gens.remove(g_)
                    if d == 0:
                        V(("tensor_copy", _a(out=osb[:], in_=bo_[:, :])), [bok], ["osb"])
                        P.dma("gpsimd", ofd[ci * C:(ci + 1) * C, 0:512], osb[:], ["osb"], [])
                        return
                    if DBG in ("B1", "B1noJ", "B1noDMA"):
                        return
                    A(("copy", _a(out=fa[:, 0:256], in_=bo_[:, 0:256])), [bok], ["fa"])
                    G(("tensor_tensor", _a(out=fb[:, 0:256], in0=fa[:, 0:256], in1=fa[:, 0:256], op=ALU.mult)), ["fa"], ["fb"])
                    V(("tensor_reduce", _a(out=st4[:, 0:4], in_=fb[:, 0:256].rearrange("p (h q) -> p h q", h=4), axis=AX.X, op=ALU.add)),
                      ["fb"], ["st4a"])
                    A(("activation", _a(out=st4[:, 0:4], in_=st4[:, 0:4], func=AF.Sqrt, scale=1.0 / 64, bias=EPS)), ["st4a"], ["st4a"])
                    V(("reciprocal", _a(out=st4[:, 0:4], in_=st4[:, 0:4])), ["st4a"], ["st4a"])
                    V(("tensor_tensor", _a(out=fa[:, 0:256].rearrange("p (h q) -> p h q", h=4), in0=fa[:, 0:256].rearrange("p (h q) -> p h q", h=4),
                                                in1=st4[:, 0:4].unsqueeze(2).to_broadcast([128, 4, 64]), op=ALU.mult)), ["fa", "st4a"], ["fa"])
                    V(("tensor_tensor", _a(out=fa[:, 0:256].rearrange("p (h q) -> p h q", h=4), in0=fa[:, 0:256].rearrange("p (h q) -> p h q", h=4),
                                                in1=glan[:].unsqueeze(1).to_broadcast([128, 4, 64]), op=ALU.mult)), ["fa", "glan"], ["fa"])
                    V(("tensor_tensor", _a(out=mix[:, 0:256], in0=fa[:, 0:256], in1=sgz[s2][:, 0:256], op=ALU.mult)), ["fa", "sgz%d" % s2], ["mixg"])
                    ya = fa[:, 256:512]
                    yb = fb[:, 256:512]
                    A(("copy", _a(out=ya, in_=bo_[:, 256:512])), [bok], ["ya"])
                    V(("tensor_reduce", _a(out=st4[:, 4:8], in_=ya.rearrange("p (h q) -> p h q", h=4), axis=AX.X, op=ALU.add)), ["ya"], ["st4b"])
                    V(("tensor_scalar", _a(out=st4[:, 4:8], in0=st4[:, 4:8], scalar1=1.0 / 64, scalar2=None, op0=ALU.mult)), ["st4b"], ["st4b"])
                    V(("tensor_tensor", _a(out=ya.rearrange("p (h q) -> p h q", h=4), in0=ya.rearrange("p (h q) -> p h q", h=4),
                                                in1=st4[:, 4:8].unsqueeze(2).to_broadcast([128, 4, 64]), op=ALU.subtract)), ["ya", "st4b"], ["ya"])
                    G(("tensor_tensor", _a(out=yb, in0=ya, in1=ya, op=ALU.mult)), ["ya"], ["yb"])
                    V(("tensor_reduce", _a(out=st4[:, 8:12], in_=yb.rearrange("p (h q) -> p h q", h=4), axis=AX.X, op=ALU.add)), ["yb"], ["st4c"])
                    A(("activation", _a(out=st4[:, 8:12], in_=st4[:, 8:12], func=AF.Sqrt, scale=1.0 / 64, bias=64e-5)), ["st4c"], ["st4c"])
                    V(("reciprocal", _a(out=st4[:, 8:12], in_=st4[:, 8:12])), ["st4c"], ["st4c"])
                    V(("tensor_tensor", _a(out=ya.rearrange("p (h q) -> p h q", h=4), in0=ya.rearrange("p (h q) -> p h q", h=4),
                                                in1=st4[:, 8:12].unsqueeze(2).to_broadcast([128, 4, 64]), op=ALU.mult)), ["ya", "st4c"], ["ya"])
                    V(("tensor_tensor", _a(out=ya, in0=ya, in1=lng[:], op=ALU.mult)), ["ya", "lng"], ["ya"])
                    V(("tensor_tensor", _a(out=ya, in0=ya, in1=lnb[:], op=ALU.add)), ["ya", "lnb"], ["ya"])
                    G(("tensor_tensor", _a(out=rtmp[:], in0=rkv[:, 0:2, :], in1=rkv[:, 2:4, :], op=ALU.mult)), ["rkv", "rtmp"], ["rtmp"])
                    G(("tensor_tensor", _a(out=prodT[:], in0=rtmp[:], in1=kk3[:, 2, :].unsqueeze(2).to_broadcast([128, 2, 128]), op=ALU.mult)),
                      ["rtmp", "kk3"], ["prodT"])
                    b_, bk = bank()
                    for bb in range(2):
                        mm(b_[:, 2 * bb:2 * bb + 2], prodT[:, bb, :], hsel[:], True, True, ["prodT", "hsel"], [bk])
                    V(("tensor_copy", _a(out=st4[:, 12:16], in_=b_[:, 0:4])), [bk], ["st4d"])
                    V(("tensor_tensor", _a(out=yb.rearrange("p (h q) -> p h q", h=4), in0=tok4[:, 3, :].rearrange("p (h q) -> p h q", h=4),
                                                in1=st4[:, 12:16].unsqueeze(2).to_broadcast([128, 4, 64]), op=ALU.mult)), ["tok4", "st4d", "yb"], ["yb"])
                    V(("tensor_tensor", _a(out=ya, in0=ya, in1=yb, op=ALU.add)), ["ya", "yb"], ["ya"])
                    b_, bk = bank()
                    mm(b_[:, 0:256], sgd[:], gup[:], True, True, ["sgd", "gup"], [bk])
                    V(("tensor_tensor", _a(out=mix[:, 256:512], in0=ya, in1=b_[:, 0:256], op=ALU.mult)), ["ya", bk], ["mixr"])
                    G(("tensor_tensor", _a(out=ytmp[:].rearrange("p (h q) -> p h q", h=8), in0=xsB[:, 0:512].rearrange("p (h q) -> p h q", h=8),
                                                in1=ssdD[:].unsqueeze(2).to_broadcast([128, 8, 64]), op=ALU.mult)), ["xsB", "ssdD", "ytmp"], ["ytmp"])
                    V(("tensor_tensor", _a(out=ysum[:], in0=ysum[:], in1=ytmp[:], op=ALU.add)), ["ysum", "ytmp"], ["ysum"])
                    V(("tensor_tensor", _a(out=ysum[:], in0=ysum[:], in1=sgz[s2][:, 256:768], op=ALU.mult)), ["ysum", "sgz%d" % s2], ["ysum"])
                    A(("activation", _a(out=ytmp[:], in_=ysum[:], func=AF.Square, accum_out=st8[:, 3:4])), ["ysum", "ytmp"], ["ytmp", "st8e"])
                    A(("activation", _a(out=st8[:, 3:4], in_=st8[:, 3:4], func=AF.Sqrt, scale=1.0 / 512, bias=EPS)), ["st8e"], ["st8e"])
                    V(("reciprocal", _a(out=st8[:, 3:4], in_=st8[:, 3:4])), ["st8e"], ["st8e"])
                    V(("scalar_tensor_tensor", _a(out=mix[:, 512:1024], in0=ysum[:], scalar=st8[:, 3:4], in1=ssdn[:], op0=ALU.mult, op1=ALU.mult)),
                      ["ysum", "st8e", "ssdn"], ["mixs"])
                    if DBG != "B2":
                        P.dma("sync", mixd[ci * C:(ci + 1) * C, :], mix[:], ["mixg", "mixr", "mixs"], [])

                if DBG == "consts" or DBG3 == "setup":
                    continue
                stageA(0)
                for i in range(NCH):
                    if i + 1 < NCH:
                        stageA(i + 1)
                    if DBG != "A" and DBG2 != "A":
                        stageB(i)
                if DBG in ("A", "ssd", "gla", "F"):
                    break
                if DBG == "Bonly":
                    break

        if DBG in ("A", "ssd", "gla", "F", "consts", "B", "B1", "B2", "B1noJ", "B1noDMA", "FF", "Bonly"):
            break
        P.barrier()
        with ExitStack() as ss:
            sb = lambda name, shape, dt: P.sb("L%dC_%s" % (l, name), shape, dt, ss)
            cst = mk_consts(sb, False)
            identb, Jb = cst["identb"], cst["Jb"]
            set_stg(sb, 1024)
            wout = sb("wout", [128, 8, D], BF16)
            wg = sb("wg", [128, 8, DFF], BF16)
            wu = sb("wu", [128, 8, DFF], BF16)
            wd = sb("wd", [128, NFC, D], BF16)
            load_w(wout, "wout", W["wout", l], D, D)
            load_w(wg, "wg", W["wg", l], D, DFF)
            load_w(wu, "wu", W["wu", l], D, DFF)
            load_w(wd, "wd", W["wd", l], DFF, D)
            gffn = sb("gffn", [128, 8], F32)
            P.dma("sync", gffn[:], W["gffn", l][:, :], [], ["gffn"])
            last = (l == DEPTH - 1)
            if last:
                gfin = sb("gfin", [128, D], F32)
                P.dma("sync", gfin[:], W["gfin"][:, :], [], ["gfin"])
            TT = min(int(os.environ.get("KTT", "2")), NCH)
            xc = [sb("xc%d" % i, [128, D], F32) for i in range(TT)]
            mixc = [sb("mixc%d" % i, [128, D], BF16) for i in range(2)]
            mixT = sb("mixT", [128, 8, 128], BF16)
            hTc = sb("hTc", [128, 8, TT * 128], BF16)
            actT = sb("actT", [128, NFC, TT * 128], BF16)
            sgc = [sb("sgc%d" % i, [128, TT * 128], BF16) for i in range(2)]
            hn = sb("hn", [128, D], BF16)
            st8 = sb("st8", [128, 8], F32)
            xo = sb("xo", [128, D], F32)
            dst = y_out if last else xnext
            ntile = (NCH + TT - 1) // TT
            for ti in range(ntile):
                nsub = min(TT, NCH - ti * TT)
                NTK = nsub * 128
                for sub in range(nsub):
                    ci = ti * TT + sub
                    xk = "xc%d" % sub
                    mk_ = "mixc%d" % (ci % 2)
                    mc = mixc[ci % 2]
                    P.dma("sync", xc[sub][:], xsrc[ci * C:(ci + 1) * C, :], [], [xk])
                    P.dma("gpsimd", mc[:], mixd[ci * C:(ci + 1) * C, :], [], [mk_])
                    b_, bk = bank()
                    for kc in range(8):
                        T(("transpose", _a(bf(b_)[:, kc * 128:(kc + 1) * 128], mc[:, kc * 128:(kc + 1) * 128], Jb[:])),
                          [mk_, "Jb"], [bk])
                    A(("copy", _a(out=mixT[:].rearrange("p a t -> p (a t)"), in_=bf(b_)[:, :])), [bk], ["mixT"])
                    for n in range(2):
                        b_, bk = bank()
                        for kc in range(8):
                            mm(b_[:, :], mixT[:, kc, :], wout[:, kc, 512 * n:512 * n + 512], kc == 0, kc == 7, ["mixT", "wout"], [bk])
                        V(("tensor_tensor", _a(out=xc[sub][:, 512 * n:512 * n + 512], in0=b_[:, :], in1=xc[sub][:, 512 * n:512 * n + 512], op=ALU.add)),
                          [bk, xk], [xk])
                    A(("activation", _a(out=hn[:], in_=xc[sub][:], func=AF.Square, accum_out=st8[:, 0:1])), [xk], ["hn", "st8a"])
                    A(("activation", _a(out=st8[:, 1:2], in_=st8[:, 0:1], func=AF.Sqrt, scale=1.0 / D, bias=EPS)), ["st8a"], ["st8b"])
                    V(("reciprocal", _a(out=st8[:, 2:3], in_=st8[:, 1:2])), ["st8b"], ["st8c"])
                    V(("tensor_scalar", _a(out=hn[:], in0=xc[sub][:], scalar1=st8[:, 2:3], scalar2=None, op0=ALU.mult)), [xk, "st8c"], ["hn"])
                    b_, bk = bank()
                    for kc in range(8):
                        T(("transpose", _a(bf(b_)[:, kc * 128:(kc + 1) * 128], hn[:, kc * 128:(kc + 1) * 128], identb[:])), ["hn", "identb"], [bk])
                    V(("tensor_tensor", _a(out=hTc[:, :, sub * 128:(sub + 1) * 128], in0=bf(b_).rearrange("p (a b) -> p a b", a=8),
                                                               in1=gffn[:].unsqueeze(2).to_broadcast([128, 8, 128]), op=ALU.mult)), [bk, "gffn"], ["hTc"])
                for fc in range(NFC):
                    bg_, bgk = bank()
                    bu_, buk = bank()
                    for kc in range(8):
                        mm(bg_[:, 0:NTK], wg[:, kc, fc * 128:(fc + 1) * 128], hTc[:, kc, 0:NTK], kc == 0, kc == 7, ["wg", "hTc"], [bgk])
                    for kc in range(8):
                        mm(bu_[:, 0:NTK], wu[:, kc, fc * 128:(fc + 1) * 128], hTc[:, kc, 0:NTK], kc == 0, kc == 7, ["wu", "hTc"], [buk])
                    sgb = sgc[fc % 2]
                    sgk = "sgc%d" % (fc % 2)
                    A(("activation", _a(out=sgb[:, 0:NTK], in_=bg_[:, 0:NTK], func=AF.Silu)), [bgk], [sgk])
                    V(("tensor_tensor", _a(out=actT[:, fc, 0:NTK], in0=bu_[:, 0:NTK], in1=sgb[:, 0:NTK], op=ALU.mult)),
                      [buk, sgk], ["actT"])
                for sub in range(nsub):
                    ci = ti * TT + sub
                    xk = "xc%d" % sub
                    for n in range(2):
                        b_, bk = bank()
                        for fc in range(NFC):
                            mm(b_[:, :], actT[:, fc, sub * 128:(sub + 1) * 128], wd[:, fc, 512 * n:512 * n + 512], fc == 0, fc == NFC - 1, ["actT", "wd"], [bk])
                        V(("tensor_tensor", _a(out=xo[:, 512 * n:512 * n + 512], in0=b_[:, :], in1=xc[sub][:, 512 * n:512 * n + 512], op=ALU.add)),
                          [bk, xk], ["xo"])
                    if last:
                        A(("activation", _a(out=hn[:], in_=xo[:], func=AF.Square, accum_out=st8[:, 4:5])), ["xo", "hn"], ["hn", "st8f"])
                        A(("activation", _a(out=st8[:, 4:5], in_=st8[:, 4:5], func=AF.Sqrt, scale=1.0 / D, bias=EPS)), ["st8f"], ["st8f"])
                        V(("reciprocal", _a(out=st8[:, 4:5], in_=st8[:, 4:5])), ["st8f"], ["st8f"])
                        V(("scalar_tensor_tensor", _a(out=xo[:], in0=xo[:], scalar=st8[:, 4:5], in1=gfin[:], op0=ALU.mult, op1=ALU.mult)),
                          ["xo", "st8f", "gfin"], ["xo"])
                    P.dma("sync", dst[ci * C:(ci + 1) * C, :], xo[:], ["xo"], [])
    P.emit()
    P.stack.close()
    return nc


def make_weight_maps(prm, DEPTH=2):
    f = lambda a: np.ascontiguousarray(np.asarray(a, dtype=np.float32))
    rep = lambda v, n=128: f(np.broadcast_to(np.asarray(v, np.float32).reshape(1, -1), (n, np.asarray(v).size)))
    fmaj = lambda v, nb: f(np.asarray(v, np.float32).reshape(nb, 128).T)
    m = {}
    for l in range(DEPTH):
        win = np.asarray(prm["w_in"][l], np.float32)
        for d in (0, 1):
            m["wfm_%d_%d" % (l, d)] = f(win[:, col_index(fm_blocks(d))])
            m["wtm_%d_%d" % (l, d)] = f(win[:, col_index(tm_cols(d))])
            m["gla_aup_%d_%d" % (l, d)] = f(prm["gla_a_up"][l][d])
            m["gla_ab_%d_%d" % (l, d)] = f(prm["gla_a_bias"][l][d].reshape(1, 128))
            mu = np.asarray(prm["rwkv_mu"][l], np.float32)
            mub = np.zeros((128, 9), np.float32)
            for b in range(6):
                mub[:, b] = mu[128 * b:128 * b + 128]
            mub[0:64, 6] = mu[768 + 64 * d:768 + 64 * d + 64]
            mub[0:64, 7] = mu[896 + 64 * d:896 + 64 * d + 64]
            mub[:, 8] = mu[1024:1152]
            m["mu_%d_%d" % (l, d)] = mub
            m["w0_%d_%d" % (l, d)] = f(prm["rwkv_w0"][l][d].reshape(1, 256))
            m["wup_%d_%d" % (l, d)] = f(prm["rwkv_w_up"][l][d])
            m["a0_%d_%d" % (l, d)] = fmaj(prm["rwkv_a0"][l][d], 2)
            m["aup_%d_%d" % (l, d)] = f(prm["rwkv_a_up"][l][d])
            m["dtb_%d_%d" % (l, d)] = rep(prm["ssd_dt_bias"][l][d])
            m["alog_%d_%d" % (l, d)] = rep(prm["ssd_A_log"][l][d])
        m["wout_%d" % l] = f(prm["w_out"][l])
        m["wg_%d" % l] = f(prm["ffn_gate"][l])
        m["wu_%d" % l] = f(prm["ffn_up"][l])
        m["wd_%d" % l] = f(prm["ffn_down"][l])
        m["gmix_%d" % l] = fmaj(prm["norm_mix"][l], 8)
        m["gffn_%d" % l] = fmaj(prm["norm_ffn"][l], 8)
        m["gla_norm_%d" % l] = rep(prm["gla_norm"][l])
        m["gup_%d" % l] = f(prm["rwkv_g_up"][l])
        kk3 = np.stack([fmaj(prm["rwkv_k_k"][l], 2), fmaj(prm["rwkv_k_a"][l], 2), fmaj(prm["rwkv_r_k"][l], 2)], axis=1)
        m["kk3_%d" % l] = f(kk3.reshape(128, 6))
        m["lng_%d" % l] = rep(prm["rwkv_ln_g"][l])
        m["lnb_%d" % l] = rep(prm["rwkv_ln_b"][l])
        cw = np.asarray(prm["ssd_conv_w"][l], np.float32)
        m["convw_%d" % l] = f(cw.reshape(5, 8, 128).transpose(2, 1, 0).reshape(128, 40))
        m["convb_%d" % l] = fmaj(prm["ssd_conv_b"][l], 8)
        m["ssdD_%d" % l] = rep(prm["ssd_D"][l])
        m["ssdn_%d" % l] = rep(prm["ssd_norm"][l])
    m["gfin"] = rep(prm["final_norm"])
    return m


_CACHE = {}


def run_cores(core_x, core_flag, prm, NSLOT, SLOT, DEPTH=2, runner=None):
    key = (NSLOT, SLOT, DEPTH)
    nc = build_program(NSLOT, SLOT, DEPTH)
    wm = make_weight_maps(prm, DEPTH)
    in_maps = []
    for x, fl in zip(core_x, core_flag):
        mp = dict(wm)
        mp["x_in"] = np.ascontiguousarray(x, dtype=np.float32)
        mp["flag"] = np.full((128, 1), fl, np.float32)
        in_maps.append(mp)
    if runner is None:
        res = run_bass_kernel_spmd(nc, in_maps, core_ids=list(range(len(in_maps))))
        return [r["y_out"] for r in res.results]
    return [r["y_out"] for r in runner(nc, in_maps)]


def kernel(**inputs):
    xp = np.asarray(inputs["x_prompt"], np.float32)
    xs = np.asarray(inputs["x_sample"], np.float32)
    prm = {k: np.asarray(v) for k, v in inputs.items() if k not in ("x_prompt", "x_sample")}
    NSLOT, SLOT = 8, 2048
    NT = NSLOT * SLOT
    assign = [[], []] + [[] for _ in range(6)]
    for b in range(xp.shape[0]):
        assign[2 + b % 6].append(b)
    core_x, core_flag = [], []
    for c in range(8):
        if c < 2:
            core_x.append(xs[c])
            core_flag.append(1.0)
        else:
            buf = np.empty((NT, D), np.float32)
            for s in range(NSLOT):
                b = assign[c][s % len(assign[c])]
                buf[s * SLOT:(s + 1) * SLOT] = xp[b]
            core_x.append(buf)
            core_flag.append(0.0)
    outs = run_cores(core_x, core_flag, prm, NSLOT, SLOT)
    yp = np.zeros_like(xp)
    ys = np.zeros_like(xs)
    for c in range(8):
        if c < 2:
            ys[c] = outs[c]
        else:
            for s, b in enumerate(assign[c]):
                yp[b] = outs[c][s * SLOT:(s + 1) * SLOT]
    return (yp, ys)
```

```python
import numpy as np
import concourse.bass as bass
import concourse.mybir as mybir
from concourse.bass_utils import run_bass_kernel_spmd
from contextlib import ExitStack

F32 = mybir.dt.float32
BF16 = mybir.dt.bfloat16
ALU = mybir.AluOpType
AF = mybir.ActivationFunctionType
AX = mybir.AxisListType

ENGS = ("tensor", "vector", "scalar", "gpsimd", "sync")
DMA_RING = 6
import os
DBG = os.environ.get("KDBG", "")
DBG2 = os.environ.get("KDBG2", "")
DBG3 = os.environ.get("KDBG3", "")
C = 128
D = 1024
DFF = 2816
NFC = DFF // 128
EPS = 1e-6


def _a(*args, **kw):
    return (args, kw)


def _call(fn, e):
    if isinstance(fn, tuple):
        return getattr(e, fn[0])(*fn[1][0], **fn[1][1])
    return fn(e)


class Op:
    __slots__ = ("eng", "fn", "reads", "writes", "dma", "idx", "deps", "sig", "ring", "ringn")


class Prog:
    def __init__(self, nc):
        self.nc = nc
        self.ops = []
        self.stack = ExitStack()
        self.last_w = {}
        self.readers = {}
        self.ndma = {e: 0 for e in ENGS}
        self.last_eng = {}
        self.ring_last = {}
        self.bar_deps = []
        self.bar_seen = {e: True for e in ENGS}

    def sb(self, name, shape, dt, stack=None):
        return (stack or self.stack).enter_context(self.nc.sbuf_tensor(name, list(shape), dt))

    def ps(self, name, shape, dt):
        return self.stack.enter_context(self.nc.psum_tensor(name, list(shape), dt))

    def barrier(self):
        deps = [v for v in self.last_eng.values()] + [v for v in self.ring_last.values()]
        self.bar_deps = sorted(set(deps))
        self.bar_seen = {e: False for e in ENGS}
        self.last_w = {}
        self.readers = {}

    def op(self, eng, fn, reads=(), writes=(), dma=False):
        o = Op()
        o.eng, o.fn, o.reads, o.writes, o.dma = eng, fn, tuple(reads), tuple(writes), dma
        o.sig = o.ring = o.ringn = None
        o.idx = len(self.ops)
        deps = set()
        if not self.bar_seen[eng]:
            deps.update(self.bar_deps)
            self.bar_seen[eng] = True
        for k in o.reads:
            w = self.last_w.get(k)
            if w is not None:
                deps.add(w)
        for k in o.writes:
            w = self.last_w.get(k)
            if w is not None:
                deps.add(w)
            deps.update(self.readers.get(k, ()))
        for k in o.reads:
            self.readers.setdefault(k, []).append(o.idx)
        for k in o.writes:
            self.last_w[k] = o.idx
            self.readers[k] = []
        deps.discard(o.idx)
        o.deps = sorted(deps)
        if dma:
            n = self.ndma[eng]
            o.ring = n % DMA_RING
            o.ringn = n // DMA_RING + 1
            self.ndma[eng] = n + 1
            self.ring_last[(eng, o.ring)] = o.idx
        else:
            self.last_eng[eng] = o.idx
        self.ops.append(o)
        return o

    def dma(self, eng, out, in_, reads=(), writes=()):
        return self.op(eng, lambda e: e.dma_start(out=out, in_=in_), reads, writes, dma=True)

    def emit(self):
        nc, ops = self.nc, self.ops
        needed = set()
        for o in ops:
            for d in o.deps:
                p = ops[d]
                if p.dma:
                    continue
                if p.eng == "tensor" and o.eng == "tensor" and not o.dma:
                    continue
                needed.add(d)
        cnt = {e: 0 for e in ENGS}
        for o in ops:
            if (not o.dma) and o.idx in needed:
                cnt[o.eng] += 1
                o.sig = cnt[o.eng]
        per = {e: [o for o in ops if o.eng == e] for e in ENGS}
        st = self.stack
        csem = {e: st.enter_context(nc.semaphore("c_" + e)) for e in ("tensor", "vector", "scalar", "gpsimd")}
        dsem = {}
        for e in ENGS:
            if self.ndma[e] > 0:
                dsem[e] = [st.enter_context(nc.semaphore("d_%s_%d" % (e, i))) for i in range(DMA_RING)]
        block = st.enter_context(nc.Block())
        ndma = self.ndma

        def run(ename, eobj):
            waited = {}

            def wait(sem, val):
                key = id(sem)
                if waited.get(key, 0) >= val:
                    return
                waited[key] = val
                eobj.wait_ge(sem, val)

            for o in per[ename]:
                for d in o.deps:
                    p = ops[d]
                    if p.dma:
                        wait(dsem[p.eng][p.ring], 16 * p.ringn)
                    else:
                        if p.eng == "tensor" and ename == "tensor" and not o.dma:
                            continue
                        wait(csem[p.eng], p.sig)
                if o.dma:
                    if o.ringn > 1:
                        wait(dsem[ename][o.ring], 16 * (o.ringn - 1))
                    _call(o.fn, eobj).then_inc(dsem[ename][o.ring], 16)
                else:
                    ins = _call(o.fn, eobj)
                    if o.sig is not None:
                        ins.then_inc(csem[ename], 1)
            if ename == "sync":
                for qe, sems in dsem.items():
                    n = ndma[qe]
                    for r in range(DMA_RING):
                        if n > r:
                            wait(sems[r], 16 * ((n - r + DMA_RING - 1) // DMA_RING))

        for en in ("tensor", "vector", "scalar", "gpsimd", "sync"):
            if per[en] or en == "sync":
                getattr(block, en)(lambda e, en=en: run(en, e))


G0, R0, S0 = 0, 800, 1952


def fm_blocks(d):
    bl = [(S0 + 512 + 128 * b, 128) for b in range(8)]
    bl += [(R0 + 128 * b, 128) for b in range(6)]
    bl += [(R0 + 768 + 64 * d, 64), (R0 + 896 + 64 * d, 64)]
    if d == 1:
        bl += [(R0 + 1024, 128)]
    bl += [(G0, 128), (G0 + 128, 128), (G0 + 768 + 16 * d, 16)]
    return bl


def tm_cols(d):
    cols = [(G0 + 256, 256)]
    if d == 1:
        cols += [(G0 + 512, 256), (S0, 512)]
    cols += [(S0 + 1536 + 8 * d, 8)]
    return cols


def col_index(blocks):
    return np.concatenate([np.arange(s, s + w) for s, w in blocks])


def build_program(NSLOT, SLOT, DEPTH=2):
    NT = NSLOT * SLOT
    NCH = NT // C
    CPS = SLOT // C
    nc = bass.Bass("TRN2", target_bir_lowering=False)
    P = Prog(nc)

    def din(name, shape):
        return nc.dram_tensor(name, list(shape), F32, kind="ExternalInput").ap()

    x_in = din("x_in", [NT, D])
    y_out = nc.dram_tensor("y_out", [NT, D], F32, kind="ExternalOutput").ap()
    mixd = nc.dram_tensor("mixd", [NT, D], BF16, kind="Internal").ap()
    xnext = nc.dram_tensor("xnext", [NT, D], F32, kind="Internal").ap()
    ofd = nc.dram_tensor("ofd", [NT, D], F32, kind="Internal").ap()
    flag_d = din("flag", [128, 1])
    NFM = [sum(w for _, w in fm_blocks(d)) for d in (0, 1)]
    NTM = [sum(w for _, w in tm_cols(d)) for d in (0, 1)]
    W = {}
    for l in range(DEPTH):
        for d in (0, 1):
            W["wfm", l, d] = din("wfm_%d_%d" % (l, d), [D, NFM[d]])
            W["wtm", l, d] = din("wtm_%d_%d" % (l, d), [D, NTM[d]])
            W["gla_aup", l, d] = din("gla_aup_%d_%d" % (l, d), [16, 128])
            W["gla_ab", l, d] = din("gla_ab_%d_%d" % (l, d), [1, 128])
            W["mu", l, d] = din("mu_%d_%d" % (l, d), [128, 9])
            W["w0", l, d] = din("w0_%d_%d" % (l, d), [1, 256])
            W["wup", l, d] = din("wup_%d_%d" % (l, d), [64, 256])
            W["a0", l, d] = din("a0_%d_%d" % (l, d), [128, 2])
            W["aup", l, d] = din("aup_%d_%d" % (l, d), [64, 256])
            W["dtb", l, d] = din("dtb_%d_%d" % (l, d), [128, 8])
            W["alog", l, d] = din("alog_%d_%d" % (l, d), [128, 8])
        W["wout", l] = din("wout_%d" % l, [D, D])
        W["wg", l] = din("wg_%d" % l, [D, DFF])
        W["wu", l] = din("wu_%d" % l, [D, DFF])
        W["wd", l] = din("wd_%d" % l, [DFF, D])
        W["gmix", l] = din("gmix_%d" % l, [128, 8])
        W["gffn", l] = din("gffn_%d" % l, [128, 8])
        W["gla_norm", l] = din("gla_norm_%d" % l, [128, 64])
        W["gup", l] = din("gup_%d" % l, [128, 256])
        W["kk3", l] = din("kk3_%d" % l, [128, 6])
        W["lng", l] = din("lng_%d" % l, [128, 256])
        W["lnb", l] = din("lnb_%d" % l, [128, 256])
        W["convw", l] = din("convw_%d" % l, [128, 40])
        W["convb", l] = din("convb_%d" % l, [128, 8])
        W["ssdD", l] = din("ssdD_%d" % l, [128, 8])
        W["ssdn", l] = din("ssdn_%d" % l, [128, 512])
    W["gfin"] = din("gfin", [128, D])

    V = lambda fn, r=(), w=(): P.op("vector", fn, r, w)
    A = lambda fn, r=(), w=(): P.op("scalar", fn, r, w)
    G = lambda fn, r=(), w=(): P.op("gpsimd", fn, r, w)
    T = lambda fn, r=(), w=(): P.op("tensor", fn, r, w)

    def mm(out, lhsT, rhs, start, stop, r, w):
        T(lambda e: e.matmul(out, lhsT=lhsT, rhs=rhs, start=start, stop=stop), r, w)

    banks = [P.ps("bank%d" % i, [128, 512], F32) for i in range(8)]
    bstate = {"i": 0}

    def bank():
        i = bstate["i"]
        bstate["i"] = (i + 1) % 7
        return banks[i], "bank%d" % i

    def bf(b):
        return b[:].bitcast(BF16)

    def asel(out, in_, pattern, op, fill, base, cm, key):
        G(("affine_select", _a(out=out, in_=in_, pattern=pattern, compare_op=op, fill=fill, base=base,
                                    channel_multiplier=cm)), [key], [key])

    def mk_consts(sbf, full):
        c = {}
        ident32 = sbf("ident32", [128, 128], F32)
        J32 = sbf("J32", [128, 128], F32)
        identb = sbf("identb", [128, 128], BF16)
        Jb = sbf("Jb", [128, 128], BF16)
        G(("memset", _a(ident32[:], 1.0)), [], ["ident32"])
        asel(ident32[:], ident32[:], [[-1, 128]], ALU.is_equal, 0.0, 0, 1, "ident32")
        G(("memset", _a(J32[:], 1.0)), [], ["J32"])
        asel(J32[:], J32[:], [[1, 128]], ALU.is_equal, 0.0, -127, 1, "J32")
        V(("tensor_copy", _a(out=identb[:], in_=ident32[:])), ["ident32"], ["identb"])
        V(("tensor_copy", _a(out=Jb[:], in_=J32[:])), ["J32"], ["Jb"])
        c.update(ident32=ident32, J32=J32, identb=identb, Jb=Jb)
        if not full:
            return c
        tri2f = sbf("tri2", [128, 256], F32)
        tri2 = tri2f[:].rearrange("p (x t) -> p x t", x=2)
        ones32 = sbf("ones32", [128, 128], F32)
        negmf = [sbf("negm%d" % d, [128, 512], BF16) for d in (0, 1)]
        negm = [t_[:].rearrange("p (h t) -> p h t", h=4) for t_ in negmf]
        qmaskf = sbf("qmask", [128, 512], BF16)
        qmask = qmaskf[:].rearrange("p (h t) -> p h t", h=4)
        bm4 = sbf("bm4", [128, 256], F32)
        bm2 = sbf("bm2", [128, 128], F32)
        bm2b = sbf("bm2b", [128, 128], BF16)
        hsel = sbf("hsel", [128, 2], BF16)
        m2 = sbf("m2", [128, 2], F32)
        flag = sbf("flagsb", [128, 1], F32)
        G(("memset", _a(ones32[:], 1.0)), [], ["ones32"])
        G(("memset", _a(tri2f[:], 1.0)), [], ["tri2"])
        asel(tri2[:, 0, :], tri2[:, 0, :], [[1, 128]], ALU.is_ge, 0.0, 0, -1, "tri2")
        asel(tri2[:, 1, :], tri2[:, 1, :], [[1, 128]], ALU.is_gt, 0.0, 0, -1, "tri2")
        for d in (0, 1):
            G(("memset", _a(negmf[d][:], 0.0)), [], ["negm%d" % d])
            for h in range(4):
                asel(negm[d][:, h, :], negm[d][:, h, :], [[1, 128]], ALU.is_ge if d == 0 else ALU.is_gt, -30000.0, 0, -1,
                     "negm%d" % d)
        G(("memset", _a(qmaskf[:], 1.0)), [], ["qmask"])
        G(("memset", _a(bm4[:], 1.0)), [], ["bm4"])
        for h in range(4):
            asel(qmask[:, h, :], qmask[:, h, :], [[0, 128]], ALU.is_ge, 0.0, -32 * h, 1, "qmask")
            asel(qmask[:, h, :], qmask[:, h, :], [[0, 128]], ALU.is_ge, 0.0, 32 * h + 31, -1, "qmask")
            asel(bm4[:, 64 * h:64 * h + 64], bm4[:, 64 * h:64 * h + 64], [[0, 64]], ALU.is_ge, 0.0, -32 * h, 1, "bm4")
            asel(bm4[:, 64 * h:64 * h + 64], bm4[:, 64 * h:64 * h + 64], [[0, 64]], ALU.is_ge, 0.0, 32 * h + 31, -1, "bm4")
        G(("memset", _a(bm2[:], 1.0)), [], ["bm2"])
        asel(bm2[:, 0:64], bm2[:, 0:64], [[0, 64]], ALU.is_ge, 0.0, 63, -1, "bm2")
        asel(bm2[:, 64:128], bm2[:, 64:128], [[0, 64]], ALU.is_ge, 0.0, -64, 1, "bm2")
        V(("tensor_copy", _a(out=bm2b[:], in_=bm2[:])), ["bm2"], ["bm2b"])
        G(("memset", _a(m2[:], 1.0)), [], ["m2"])
        asel(m2[:, 0:1], m2[:, 0:1], [[0, 1]], ALU.is_ge, 0.0, 63, -1, "m2")
        asel(m2[:, 1:2], m2[:, 1:2], [[0, 1]], ALU.is_ge, 0.0, -64, 1, "m2")
        V(("tensor_copy", _a(out=hsel[:], in_=m2[:])), ["m2"], ["hsel"])
        P.dma("sync", flag[:], flag_d[:, :], [], ["flag"])
        mskB = sbf("mskB", [128, 2, 2, 128], BF16)
        mskK = [sbf("mskK%d" % d, [128, 2, 2, 128], BF16) for d in (0, 1)]
        for hh in range(2):
            V(("tensor_copy", _a(out=mskB[:, hh, 0, :], in_=tri2[:, 1, :])), ["tri2"], ["mskB"])
            V(("tensor_copy", _a(out=mskB[:, hh, 1, :], in_=tri2[:, 0, :])), ["tri2"], ["mskB"])
            for d in (0, 1):
                V(("tensor_copy", _a(out=mskK[d][:, hh, 0, :], in_=tri2[:, 1, :])), ["tri2"], ["mskK%d" % d])
                V(("tensor_copy", _a(out=mskK[d][:, hh, 1, :], in_=tri2[:, d, :])), ["tri2"], ["mskK%d" % d])
        c.update(tri2f=tri2f, tri2=tri2, ones32=ones32, negmf=negmf, negm=negm, qmaskf=qmaskf, qmask=qmask, bm4=bm4,
                 bm2=bm2, bm2b=bm2b, hsel=hsel, m2=m2, flag=flag, mskB=mskB, mskK=mskK)
        return c

    ld = {"i": 0, "stg": None, "w": 1024}

    def set_stg(sbf, width):
        ld["stg"] = [sbf("stg%d" % i, [128, width], F32) for i in range(2)]
        ld["w"] = width

    def load_cast(dst_ap, src_ap, ncols, dkey, np_=128):
        i = ld["i"]
        ld["i"] += 1
        s = ld["stg"][i % 2]
        sk = "stg%d" % (i % 2)
        P.dma("sync" if i % 2 == 0 else "gpsimd", s[0:np_, 0:ncols], src_ap, [], [sk])
        eng = ("vector", "gpsimd", "scalar")[i % 3]
        if eng == "scalar":
            A(("copy", _a(out=dst_ap, in_=s[0:np_, 0:ncols])), [sk], [dkey])
        else:
            P.op(eng, ("tensor_copy", _a(out=dst_ap, in_=s[0:np_, 0:ncols])), [sk], [dkey])

    def load_w(dst, dkey, src, K, N):
        wdt = ld["w"]
        for kc in range(K // 128):
            for n0 in range(0, N, wdt):
                n1 = min(N, n0 + wdt)
                load_cast(dst[:, kc, n0:n1], src[kc * 128:(kc + 1) * 128, n0:n1], n1 - n0, dkey)

    for l in range(DEPTH):
        xsrc = x_in if l == 0 else xnext
        for d in ((0, 0) if DBG == "FF" else (0, 1)):
            if DBG == "Bonly" and d == 0:
                continue
            P.barrier()
            with ExitStack() as ss:
                sb = lambda name, shape, dt: P.sb("L%dD%d_%s_%d" % (l, d, name, len(P.ops)), shape, dt, ss)
                cst = mk_consts(sb, True)
                ident32, J32, identb, Jb = cst["ident32"], cst["J32"], cst["identb"], cst["Jb"]
                tri2f, tri2, ones32, negmf, negm = cst["tri2f"], cst["tri2"], cst["ones32"], cst["negmf"], cst["negm"]
                qmaskf, qmask, bm4, bm2, bm2b = cst["qmaskf"], cst["qmask"], cst["bm4"], cst["bm2"], cst["bm2b"]
                hsel, m2, flag, mskB, mskK = cst["hsel"], cst["m2"], cst["flag"], cst["mskB"], cst["mskK"]
                set_stg(sb, 1024)
                NB = 17 if d == 1 else 16
                fmb = fm_blocks(d)
                fmoff = np.concatenate([[0], np.cumsum([w for _, w in fmb])]).tolist()
                tmc = tm_cols(d)
                ntm = NTM[d]
                wfm = sb("wfm", [128, 8, NFM[d]], BF16)
                wtm = sb("wtm", [128, 8, ntm], BF16)
                load_w(wfm, "wfm", W["wfm", l, d], D, NFM[d])
                load_w(wtm, "wtm", W["wtm", l, d], D, ntm)
                gmix = sb("gmix", [128, 8], F32)
                P.dma("sync", gmix[:], W["gmix", l][:, :], [], ["gmix"])
                aupg = sb("aupg", [16, 128], BF16)
                load_cast(aupg[:], W["gla_aup", l, d][:, :], 128, "aupg", 16)
                abg = sb("abg", [1, 128], F32)
                P.dma("sync", abg[:], W["gla_ab", l, d][:, :], [], ["abg"])
                mu = sb("mu", [128, 9], F32)
                P.dma("sync", mu[:], W["mu", l, d][:, :], [], ["mu"])
                w0r = sb("w0r", [1, 256], F32)
                P.dma("sync", w0r[:], W["w0", l, d][:, :], [], ["w0r"])
                wup = sb("wup", [64, 256], BF16)
                load_cast(wup[:], W["wup", l, d][:, :], 256, "wup", 64)
                a0 = sb("a0", [128, 2], F32)
                P.dma("sync", a0[:], W["a0", l, d][:, :], [], ["a0"])
                aup = sb("aup", [64, 256], BF16)
                load_cast(aup[:], W["aup", l, d][:, :], 256, "aup", 64)
                dtb = sb("dtb", [128, 8], F32)
                P.dma("sync", dtb[:], W["dtb", l, d][:, :], [], ["dtb"])
                alog = sb("alog", [128, 8], F32)
                P.dma("sync", alog[:], W["alog", l, d][:, :], [], ["alog"])
                expA = sb("expA", [128, 8], F32)
                A(("activation", _a(out=expA[:], in_=alog[:], func=AF.Exp)), ["alog"], ["expA"])
                kk3 = sb("kk3", [128, 3, 2], F32)
                P.dma("sync", kk3[:], W["kk3", l].rearrange("p (a b) -> p a b", a=3), [], ["kk3"])
                omka = sb("omka", [128, 2], F32)
                V(("tensor_scalar", _a(out=omka[:], in0=kk3[:, 1, :], scalar1=-1.0, scalar2=1.0, op0=ALU.mult,
                                            op1=ALU.add)), ["kk3"], ["omka"])
                convw = sb("convw", [128, 8, 5], F32)
                P.dma("sync", convw[:], W["convw", l].rearrange("p (b j) -> p b j", b=8), [], ["convw"])
                convb = sb("convb", [128, 8], F32)
                P.dma("sync", convb[:], W["convb", l][:, :], [], ["convb"])
                dconv = sb("dconv", [128, 8, 5, 128], BF16)
                for b in range(8):
                    for j in range(5):
                        jj = j if d == 0 else 4 - j
                        V(("tensor_scalar", _a(out=dconv[:, b, j, :], in0=ident32[:],
                                                                      scalar1=convw[:, b, jj:jj + 1], scalar2=None,
                                                                      op0=ALU.mult)), ["ident32", "convw"], ["dconv"])
                omu = sb("omu", [128, 9], F32)
                hmu = sb("hmu", [128, 9], F32)
                V(("tensor_scalar", _a(out=omu[:], in0=mu[:], scalar1=-1.0, scalar2=1.0, op0=ALU.mult, op1=ALU.add)),
                  ["mu"], ["omu"])
                V(("tensor_scalar", _a(out=hmu[:], in0=mu[:], scalar1=0.5, scalar2=None, op0=ALU.mult)), ["mu"], ["hmu"])
                dsh = sb("dsh", [128, 9, 2, 128], BF16)
                for b in range(9):
                    V(("tensor_scalar", _a(out=dsh[:, b, 0, :], in0=ident32[:], scalar1=omu[:, b:b + 1],
                                                     scalar2=None, op0=ALU.mult)), ["ident32", "omu"], ["dsh"])
                    V(("tensor_scalar", _a(out=dsh[:, b, 1, :], in0=ident32[:], scalar1=hmu[:, b:b + 1],
                                                     scalar2=None, op0=ALU.mult)), ["ident32", "hmu"], ["dsh"])
                if d == 1:
                    glan = sb("glan", [128, 64], F32)
                    P.dma("sync", glan[:], W["gla_norm", l][:, :], [], ["glan"])
                    gup = sb("gup", [128, 256], BF16)
                    load_cast(gup[:], W["gup", l][:, :], 256, "gup")
                    lng = sb("lng", [128, 256], F32)
                    P.dma("sync", lng[:], W["lng", l][:, :], [], ["lng"])
                    lnb = sb("lnb", [128, 256], F32)
                    P.dma("sync", lnb[:], W["lnb", l][:, :], [], ["lnb"])
                    ssdD = sb("ssdD", [128, 8], F32)
                    P.dma("sync", ssdD[:], W["ssdD", l][:, :], [], ["ssdD"])
                    ssdn = sb("ssdn", [128, 512], F32)
                    P.dma("sync", ssdn[:], W["ssdn", l][:, :], [], ["ssdn"])
                xt = [sb("xt%d" % i, [128, D], F32) for i in range(2)]
                raw = [sb("raw%d" % i, [128, NB, 132], BF16) for i in range(2)]
                glaT = [sb("glaT%d" % i, [128, 3, 128], BF16) for i in range(2)]
                vtok = [sb("vtok%d" % i, [128, 256], BF16) for i in range(2)]
                dtt = [sb("dtt%d" % i, [128, 8], F32) for i in range(2)]
                if d == 1:
                    sgz = [sb("sgz%d" % i, [128, 768], F32) for i in range(3)]
                hn = sb("hn", [128, D], BF16)
                hT = sb("hT", [128, 8, 128], BF16)
                st8 = sb("st8", [128, 8], F32)
                dtmp = sb("dtmp", [128, 8], F32)
                S32 = sb("S32", [128, 256], F32)
                S16 = sb("S16", [128, 256], BF16)
                H32 = sb("H32", [128, 512], F32)
                H16 = sb("H16", [128, 512], BF16)
                R32 = sb("R32", [128, 2, 128], F32)
                R16 = sb("R16", [128, 2, 128], BF16)
                for t_, k_ in ((S32, "S32"), (S16, "S16"), (H32, "H32"), (H16, "H16"), (R32, "R32"), (R16, "R16")):
                    G(("memset", _a(t_[:], 0.0)), [], [k_])
                for i in range(2):
                    G(("memset", _a(raw[i][:], 0.0)), [], ["raw%d" % i])
                xbcT = sb("xbcT", [128, 8, 128], BF16)
                xsB = sb("xsB", [128, 768], BF16)
                dtA = sb("dtA", [128, 8], F32)
                ac = sb("ac", [128, 5, 8], F32)
                Ztf = sb("Zt", [128, 1024], F32)
                Zt = Ztf[:].rearrange("p (h t) -> p h t", h=8)
                seg = sb("seg", [128, 8, 128], BF16)
                cb = sb("cb", [128, 2, 128], BF16)
                MT = sb("MT", [128, 8, 128], BF16)
                xdt = sb("xdt", [128, 512], BF16)
                xdtw = sb("xdtw", [128, 512], BF16)
                ytmp = sb("ytmp", [128, 512], F32)
                ysum = sb("ysum", [128, 512], F32)
                sg = sb("sg", [128, 128], F32)
                nl = sb("nl", [128, 128], F32)
                epos = sb("epos", [128, 128], F32)
                eneg = sb("eneg", [128, 128], F32)
                qd = sb("qd", [128, 128], BF16)
                kd = sb("kd", [128, 128], BF16)
                kt = sb("kt", [128, 128], BF16)
                Qblkf = sb("Qblk", [128, 512], BF16)
                Qblk = Qblkf[:].rearrange("p (h t) -> p h t", h=4)
                Sm = sb("Sm", [128, 4, 128], BF16)
                kttok = sb("kttok", [128, 128], BF16)
                utmp = sb("utmp", [128, 256], F32)
                rkv = sb("rkv", [128, 6, 128], F32)
                twT = sb("twT", [64, 128], BF16)
                afT = sb("afT", [64, 128], BF16)
                sgd = sb("sgd", [128, 128], BF16)
                nlw = sb("nlw", [128, 256], F32)
                gin = sb("gin", [128, 2, 128], F32)
                gex = sb("gex", [128, 2, 128], F32)
                ginv = sb("ginv", [128, 2, 128], F32)
                alpha = sb("alpha", [128, 2, 128], F32)
                kkk = sb("kkk", [128, 2, 128], F32)
                sq = sb("sq", [128, 2, 128], BF16)
                rinv = sb("rinv", [128, 2, 128], F32)
                kkn = sb("kkn", [128, 2, 128], F32)
                t1 = sb("t1", [128, 2, 128], F32)
                kdir = sb("kdir", [128, 2, 128], F32)
                bvec = sb("bvec", [128, 2, 128], F32)
                RtT = sb("RtT", [128, 2, 128], BF16)
                AtT = sb("AtT", [128, 2, 128], BF16)
                BtT = sb("BtT", [128, 2, 128], BF16)
                KtT = sb("KtT", [128, 2, 128], BF16)
                fmT = sb("fmT", [128, 4, 2, 128], BF16)
                tok4 = sb("tok4", [128, 4, 256], BF16)
                RAbf = sb("RAb", [128, 2, 512], BF16)
                RAb = RAbf[:].rearrange("p b (h x t) -> p b h x t", h=2, x=2)
                SCB = sb("SCB", [128, 2, 2, 2, 128], BF16)
                SCK = sb("SCK", [128, 2, 2, 2, 128], BF16)
                Ak = [sb("Ak%d" % i, [128, 4, 128], BF16) for i in range(2)]
                Nk = [sb("Nk%d" % i, [128, 4, 128], BF16) for i in range(2)]
                T32 = sb("T32", [128, 4, 128], F32)
                T16f = sb("T16", [128, 512], BF16)
                T16 = T16f[:].rearrange("p (h t) -> p h t", h=4)
                WtT = sb("WtT", [128, 2, 128], BF16)
                X2 = sb("X2", [128, 256], BF16)
                U16 = sb("U16", [128, 256], BF16)
                rtmp = sb("rtmp", [128, 2, 128], F32)
                if d == 0:
                    osb = sb("osb", [128, 512], F32)
                if d == 1:
                    ofs = sb("ofs", [128, D], F32)
                    mix = sb("mix", [128, D], BF16)
                    fa = sb("fa", [128, 512], F32)
                    fb = sb("fb", [128, 512], F32)
                    st4 = sb("st4", [128, 16], F32)
                    prodT = sb("prodT", [128, 2, 128], BF16)

                identX = identb if (d == 0 or DBG3 == "noJ") else Jb
                if os.environ.get("KMEM"):
                    print("SBUF remaining after sweep allocs l=%d d=%d:" % (l, d), nc.sbuf_bytes_remaining)
                order = list(range(NCH)) if d == 0 else list(range(NCH - 1, -1, -1))

                def stageA(i):
                    ci = order[i]
                    s3, s2 = i % 2, i % 2
                    xk, rk_, gk, vk, dk = "xt%d" % s3, "raw%d" % s3, "glaT%d" % s2, "vtok%d" % s2, "dtt%d" % s2
                    P.dma("sync", xt[s3][:], xsrc[ci * C:(ci + 1) * C, :], [], [xk])
                    A(("activation", _a(out=hn[:], in_=xt[s3][:], func=AF.Square, accum_out=st8[:, 0:1])),
                      [xk], ["hn", "st8a"])
                    A(("activation", _a(out=st8[:, 1:2], in_=st8[:, 0:1], func=AF.Sqrt, scale=1.0 / D, bias=EPS)),
                      ["st8a"], ["st8b"])
                    V(("reciprocal", _a(out=st8[:, 2:3], in_=st8[:, 1:2])), ["st8b"], ["st8c"])
                    V(("tensor_scalar", _a(out=hn[:], in0=xt[s3][:], scalar1=st8[:, 2:3], scalar2=None, op0=ALU.mult)),
                      [xk, "st8c"], ["hn"])
                    b_, bk = bank()
                    for kc in range(8):
                        T(("transpose", _a(bf(b_)[:, kc * 128:(kc + 1) * 128], hn[:, kc * 128:(kc + 1) * 128],
                                                       identX[:])), ["hn", "identb", "Jb"], [bk])
                    V(("tensor_tensor", _a(out=hT[:], in0=bf(b_).rearrange("p (a b) -> p a b", a=8),
                                                in1=gmix[:].unsqueeze(2).to_broadcast([128, 8, 128]), op=ALU.mult)),
                      [bk, "gmix"], ["hT"])
                    yield
                    nblk = len(fmb)
                    if DBG3 == "noFM4":
                        nblk = 16
                    if DBG3 == "noFM":
                        nblk = 0
                    fgroups = [list(range(g0, min(NB, g0 + 4))) for g0 in range(0, NB, 4)] + [list(range(NB, len(fmb)))]
                    if nblk < len(fmb):
                        fgroups = [g_ for g_ in fgroups if g_ and g_[-1] < nblk]
                    for grp in fgroups:
                        b_, bk = bank()
                        for j, bi in enumerate(grp):
                            wdt = fmb[bi][1]
                            for kc in range(8):
                                mm(b_[0:wdt, j * 128:(j + 1) * 128], wfm[:, kc, fmoff[bi]:fmoff[bi] + wdt], hT[:, kc, :],
                                   kc == 0, kc == 7, ["wfm", "hT"], [bk])
                        for j, bi in enumerate(grp):
                            wdt = fmb[bi][1]
                            if bi < NB:
                                A(("copy", _a(out=raw[s3][0:wdt, bi, 2:130],
                                                                        in_=b_[0:wdt, j * 128:(j + 1) * 128])), [bk], [rk_])
                            else:
                                V(("tensor_copy", _a(out=glaT[s2][0:wdt, bi - NB, :],
                                                                                in_=b_[0:wdt, j * 128:(j + 1) * 128])),
                                  [bk], [gk])
                        yield
                    off = 0
                    groups = [[(0, 256), (256 if d == 0 else 1024, 8)]] if d == 0 else \
                        [[(0, 256), (256, 256)], [(512, 512)], [(1024, 8)]]
                    if DBG3 == "noTM":
                        groups = groups[:1]
                    if DBG3 == "noTM0":
                        groups = []
                    for grp in groups:
                        b_, bk = bank()
                        o = 0
                        for (c0, wdt) in grp:
                            for kc in range(8):
                                mm(b_[:, o:o + wdt], hT[:, kc, :], wtm[:, kc, c0:c0 + wdt], kc == 0, kc == 7, ["hT", "wtm"], [bk])
                            if c0 == 0:
                                V(("tensor_copy", _a(out=vtok[s2][:], in_=b_[:, o:o + 256])), [bk], [vk])
                            elif wdt == 8:
                                V(("tensor_tensor", _a(out=dtmp[:], in0=b_[:, o:o + 8], in1=dtb[:], op=ALU.add)),
                                  [bk, "dtb"], ["dtmp"])
                                A(("activation", _a(out=dtmp[:], in_=dtmp[:], func=AF.Exp)), ["dtmp"], ["dtmp"])
                                A(("activation", _a(out=dtt[s2][:], in_=dtmp[:], func=AF.Ln, bias=1.0)), ["dtmp"], [dk])
                            elif c0 == 256:
                                A(("activation", _a(out=sgz[i % 3][:, 0:256], in_=b_[:, o:o + 256], func=AF.Silu)),
                                  [bk], ["sgz%d" % (i % 3)])
                            else:
                                A(("activation", _a(out=sgz[i % 3][:, 256:768], in_=b_[:, o:o + 512], func=AF.Silu)),
                                  [bk], ["sgz%d" % (i % 3)])
                            o += wdt
                        yield
                    if i > 0:
                        p3 = (i - 1) % 2
                        pk = "raw%d" % p3
                        if i % CPS == 0:
                            G(("tensor_scalar", _a(out=raw[s3][:, :, 0:2], in0=raw[p3][:, :, 128:130], scalar1=flag[:, 0:1],
                                                        scalar2=None, op0=ALU.mult)), [pk, rk_, "flag"], [rk_ + "h"])
                            G(("tensor_scalar", _a(out=raw[p3][:, :, 130:132], in0=raw[s3][:, :, 2:4], scalar1=flag[:, 0:1],
                                                        scalar2=None, op0=ALU.mult)), [pk, rk_, "flag"], [pk + "h"])
                        else:
                            G(("tensor_copy", _a(out=raw[s3][:, :, 0:2], in_=raw[p3][:, :, 128:130])), [pk, rk_], [rk_ + "h"])
                            G(("tensor_copy", _a(out=raw[p3][:, :, 130:132], in_=raw[s3][:, :, 2:4])), [pk, rk_], [pk + "h"])
                    else:
                        G(("memset", _a(raw[s3][:, :, 0:2], 0.0)), [rk_], [rk_ + "h"])
                    if i == NCH - 1:
                        G(("memset", _a(raw[s3][:, :, 130:132], 0.0)), [rk_], [rk_ + "h"])

                def stageB(i):
                    ci = order[i]
                    s3, s2 = i % 2, i % 2
                    xk, rk_, gk, vk, dk = "xt%d" % s3, "raw%d" % s3, "glaT%d" % s2, "vtok%d" % s2, "dtt%d" % s2
                    rw = raw[s3]
                    rr = [rk_, rk_ + "h"]
                    boundary = (i % CPS == CPS - 1)
                    useJ = (d == 1) and DBG not in ("B1noJ",)
                    if d == 1 and DBG != "B1noDMA":
                        P.dma("gpsimd", ofs[:], ofd[ci * C:(ci + 1) * C, :], [], ["ofs"])
                    bo_, bok = banks[7], "bank7"
                    shared = {"gla_o": False}
                    def ssd_gen():
                        for g0 in (0, 4):
                            if g0:
                                yield
                            b_, bk = bank()
                            for j in range(4):
                                b = g0 + j
                                for tp in range(5):
                                    mm(b_[:, j * 128:(j + 1) * 128], dconv[:, b, tp, :], rw[:, b, tp:tp + 128], tp == 0, tp == 4,
                                       ["dconv"] + rr, [bk])
                            for j in range(4):
                                b = g0 + j
                                A(("activation", _a(out=xbcT[:, b, :], in_=b_[:, j * 128:(j + 1) * 128], func=AF.Silu,
                                                                   bias=convb[:, b:b + 1])), [bk, "convb"], ["xbcT"])
                        yield
                        b_, bk = bank()
                        for b in range(6):
                            T(("transpose", _a(bf(b_)[:, b * 128:(b + 1) * 128], xbcT[:, b, :], identb[:])), ["xbcT", "identb"], [bk])
                        V(("tensor_copy", _a(out=xsB[:], in_=bf(b_)[:, 0:768])), [bk], ["xsB"])
                        V(("scalar_tensor_tensor", _a(out=dtA[:], in0=dtt[s2][:], scalar=-1.0, in1=expA[:], op0=ALU.mult, op1=ALU.mult)),
                          [dk, "expA"], ["dtA"])
                        yield
                        b_, bk = bank()
                        mm(b_[:, 0:8], tri2[:, 0, :], dtA[:], True, True, ["tri2", "dtA"], [bk])
                        mm(b_[:, 8:16], ones32[:], dtA[:], True, True, ["ones32", "dtA"], [bk])
                        V(("tensor_copy", _a(out=ac[:, 0, :], in_=b_[:, 0:8])), [bk], ["ac0"])
                        V(("tensor_scalar", _a(out=ac[:, 1, :], in0=b_[:, 0:8], scalar1=-1.0, scalar2=None, op0=ALU.mult)), [bk], ["ac1"])
                        A(("activation", _a(out=ac[:, 2, :], in_=b_[:, 0:8], func=AF.Exp)), [bk], ["ac2"])
                        A(("activation", _a(out=ac[:, 4, :], in_=b_[:, 8:16], func=AF.Exp)), [bk], ["ac4"])
                        V(("tensor_tensor", _a(out=ac[:, 3, :], in0=b_[:, 8:16], in1=ac[:, 0, :], op=ALU.subtract)), [bk, "ac0"], ["ac3"])
                        A(("activation", _a(out=ac[:, 3, :], in_=ac[:, 3, :], func=AF.Exp)), ["ac3"], ["ac3"])
                        V(("tensor_tensor", _a(out=ac[:, 3, :], in0=ac[:, 3, :], in1=dtt[s2][:], op=ALU.mult)), ["ac3", dk], ["ac3"])
                        G(("tensor_tensor", _a(out=Zt, in0=tri2[:, 0:1, :].to_broadcast([128, 8, 128]),
                                                    in1=dtA[:].unsqueeze(2).to_broadcast([128, 8, 128]), op=ALU.mult)),
                          ["tri2", "dtA"], ["Zt"])
                        yield
                        for hg in range(2):
                            if hg:
                                yield
                            b_, bk = bank()
                            mm(b_[:, :], ones32[:], Ztf[:, 512 * hg:512 * hg + 512], True, False, ["ones32", "Zt"], [bk])
                            mm(b_[:, :], identb[:], negmf[d][:], False, True, ["identb", "negm%d" % d], [bk])
                            for hh in range(4):
                                h = 4 * hg + hh
                                A(("activation", _a(out=seg[:, h, :], in_=b_[:, hh * 128:(hh + 1) * 128], func=AF.Exp,
                                                                          bias=ac[:, 1, h:h + 1])), [bk, "ac1"], ["seg"])
                        yield
                        b_, bk = bank()
                        for g in range(2):
                            mm(b_[:, g * 128:(g + 1) * 128], xbcT[:, 4 + g, :], xbcT[:, 6 + g, :], True, True, ["xbcT"], [bk])
                        V(("tensor_copy", _a(out=cb[:], in_=b_[:, 0:256])), [bk], ["cb"])
                        V(("tensor_tensor", _a(out=MT[:].rearrange("p (g r) t -> p g r t", g=2),
                                                    in0=seg[:].rearrange("p (g r) t -> p g r t", g=2),
                                                    in1=cb[:].unsqueeze(2).to_broadcast([128, 2, 4, 128]), op=ALU.mult)),
                          ["seg", "cb"], ["MT"])
                        V(("tensor_tensor", _a(out=xdt[:].rearrange("p (h q) -> p h q", h=8),
                                                    in0=xsB[:, 0:512].rearrange("p (h q) -> p h q", h=8),
                                                    in1=dtt[s2][:].unsqueeze(2).to_broadcast([128, 8, 64]), op=ALU.mult)),
                          ["xsB", dk], ["xdt"])
                        G(("tensor_tensor", _a(out=xdtw[:].rearrange("p (h q) -> p h q", h=8),
                                                    in0=xsB[:, 0:512].rearrange("p (h q) -> p h q", h=8),
                                                    in1=ac[:, 3, :].unsqueeze(2).to_broadcast([128, 8, 64]), op=ALU.mult)),
                          ["xsB", "ac3"], ["xdtw"])
                        yield
                        by_, byk = bank()
                        if useJ:
                            mm(by_[:, :], J32[:], ofs[:, 512:1024], True, False, ["J32", "ofs"], [byk])
                        for h in range(8):
                            mm(by_[:, 64 * h:64 * h + 64], MT[:, h, :], xdt[:, 64 * h:64 * h + 64], (not useJ) and h == 0, h == 7, ["MT", "xdt"], [byk])
                        b_, bk = bank()
                        for g in range(2):
                            mm(b_[:, 256 * g:256 * g + 256], xbcT[:, 6 + g, :], H16[:, 256 * g:256 * g + 256], True, True, ["xbcT", "H16"], [bk])
                        V(("tensor_tensor", _a(out=ytmp[:].rearrange("p (h q) -> p h q", h=8),
                                                    in0=b_[:, :].rearrange("p (h q) -> p h q", h=8),
                                                    in1=ac[:, 2, :].unsqueeze(2).to_broadcast([128, 8, 64]), op=ALU.mult)),
                          [bk, "ac2"], ["ytmp"])
                        V(("tensor_tensor", _a(out=ysum[:], in0=by_[:, :], in1=ytmp[:], op=ALU.add)), [byk, "ytmp"], ["ysum"])
                        yield
                        b_, bk = bank()
                        for g in range(2):
                            mm(b_[:, 256 * g:256 * g + 256], xsB[:, 512 + 128 * g:512 + 128 * g + 128], xdtw[:, 256 * g:256 * g + 256], True, True,
                               ["xsB", "xdtw"], [bk])
                        G(("tensor_tensor", _a(out=ytmp[:].rearrange("p (h q) -> p h q", h=8),
                                                    in0=H32[:].rearrange("p (h q) -> p h q", h=8),
                                                    in1=ac[:, 4, :].unsqueeze(2).to_broadcast([128, 8, 64]), op=ALU.mult)),
                          ["H32", "ac4", "ytmp"], ["ytmp"])
                        V(("tensor_tensor", _a(out=H32[:], in0=b_[:, :], in1=ytmp[:], op=ALU.add)), [bk, "ytmp"], ["H32"])
                        if boundary:
                            V(("tensor_scalar", _a(out=H32[:], in0=H32[:], scalar1=flag[:, 0:1], scalar2=None, op0=ALU.mult)),
                              ["H32", "flag"], ["H32"])
                        A(("copy", _a(out=H16[:], in_=H32[:])), ["H32"], ["H16"])
                        if d == 0:
                            P.dma("gpsimd", ofd[ci * C:(ci + 1) * C, 512:1024], ysum[:], ["ysum"], [])


                    def gla_gen():
                        gT = glaT[s2]
                        b_, bk = bank()
                        mm(b_[:, 0:128], gT[0:16, 2, :], aupg[:], True, False, [gk, "aupg"], [bk])
                        mm(b_[:, 0:128], ones32[0:1, :], abg[:], False, True, ["ones32", "abg"], [bk])
                        A(("activation", _a(out=sg[:], in_=b_[:, 0:128], func=AF.Sigmoid)), [bk], ["sg"])
                        A(("activation", _a(out=nl[:], in_=sg[:], func=AF.Ln)), ["sg"], ["nl"])
                        yield
                        b_, bk = bank()
                        mm(b_[:, 0:128], nl[:], tri2[:, 0, :], True, True, ["nl", "tri2"], [bk])
                        A(("activation", _a(out=epos[:], in_=b_[:, 0:128], func=AF.Exp, scale=1.0 / 16)), [bk], ["epos"])
                        A(("activation", _a(out=eneg[:], in_=b_[:, 0:128], func=AF.Exp, scale=-1.0 / 16)), [bk], ["eneg"])
                        V(("scalar_tensor_tensor", _a(out=qd[:], in0=gT[:, 0, :], scalar=32.0 ** -0.5, in1=epos[:], op0=ALU.mult, op1=ALU.mult)),
                          [gk, "epos"], ["qd"])
                        V(("tensor_tensor", _a(out=kd[:], in0=gT[:, 1, :], in1=eneg[:], op=ALU.mult)), [gk, "eneg"], ["kd"])
                        G(("tensor_scalar", _a(out=kt[:], in0=kd[:], scalar1=epos[:, 127:128], scalar2=None, op0=ALU.mult)), ["kd", "epos"], ["kt"])
                        G(("tensor_tensor", _a(out=Qblk, in0=qd[:].unsqueeze(1).to_broadcast([128, 4, 128]), in1=qmask, op=ALU.mult)),
                          ["qd", "qmask"], ["Qblk"])
                        yield
                        b_, bk = bank()
                        mm(b_[:, :], kd[:], Qblkf[:], True, True, ["kd", "Qblk"], [bk])
                        V(("tensor_tensor", _a(out=Sm[:], in0=b_[:, :].rearrange("p (h t) -> p h t", h=4),
                                                    in1=tri2[:, d:d + 1, :].to_broadcast([128, 4, 128]), op=ALU.mult)), [bk, "tri2"], ["Sm"])
                        yield
                        b_, bk = bank()
                        T(("transpose", _a(bf(b_)[:, 0:128], kt[:], identb[:])), ["kt", "identb"], [bk])
                        A(("copy", _a(out=kttok[:], in_=bf(b_)[:, 0:128])), [bk], ["kttok"])
                        yield
                        if useJ:
                            mm(bo_[:, :], J32[:], ofs[:, 0:512], True, False, ["J32", "ofs"], [bok])
                        mm(bo_[:, 0:256], qd[:], S16[:], not useJ, False, ["qd", "S16"], [bok])
                        for h in range(4):
                            mm(bo_[:, 64 * h:64 * h + 64], Sm[:, h, :], vtok[s2][:, 64 * h:64 * h + 64], False, False, ["Sm", vk], [bok])
                        yield
                        shared["gla_o"] = True
                        b_, bk = bank()
                        mm(b_[:, 0:256], kttok[:], vtok[s2][:], True, True, ["kttok", vk], [bk])
                        V(("tensor_tensor", _a(out=utmp[:], in0=b_[:, 0:256], in1=bm4[:], op=ALU.mult)), [bk, "bm4"], ["utmp"])
                        V(("scalar_tensor_tensor", _a(out=S32[:], in0=S32[:], scalar=epos[:, 127:128], in1=utmp[:], op0=ALU.mult, op1=ALU.add)),
                          ["S32", "epos", "utmp"], ["S32"])
                        if boundary:
                            V(("tensor_scalar", _a(out=S32[:], in0=S32[:], scalar1=flag[:, 0:1], scalar2=None, op0=ALU.mult)),
                              ["S32", "flag"], ["S32"])
                        G(("tensor_copy", _a(out=S16[:], in_=S32[:])), ["S32"], ["S16"])


                    def rwkv_gen():
                        nrb = 9 if d == 1 else 8
                        bsh = []
                        for g0 in range(0, nrb, 4):
                            b_, bk = bank()
                            bsh.append((b_, bk))
                            for j in range(min(4, nrb - g0)):
                                bb = g0 + j
                                m = 64 if bb in (6, 7) else 128
                                rb = 8 + bb
                                mm(b_[0:m, j * 128:(j + 1) * 128], dsh[0:m, bb, 0, 0:m], rw[0:m, rb, 2:130], True, False, ["dsh"] + rr, [bk])
                                mm(b_[0:m, j * 128:(j + 1) * 128], dsh[0:m, bb, 1, 0:m], rw[0:m, rb, 1:129], False, False, ["dsh"] + rr, [bk])
                                mm(b_[0:m, j * 128:(j + 1) * 128], dsh[0:m, bb, 1, 0:m], rw[0:m, rb, 3:131], False, True, ["dsh"] + rr, [bk])
                        A(("copy", _a(out=rkv[:, 0:4, :], in_=bsh[0][0][:, :].rearrange("p (a t) -> p a t", a=4))), [bsh[0][1]], ["rkv"])
                        V(("tensor_copy", _a(out=rkv[:, 4:6, :], in_=bsh[1][0][:, 0:256].rearrange("p (a t) -> p a t", a=2))), [bsh[1][1]], ["rkv"])
                        A(("activation", _a(out=twT[:], in_=bsh[1][0][0:64, 256:384], func=AF.Tanh)), [bsh[1][1]], ["twT"])
                        V(("tensor_copy", _a(out=afT[:], in_=bsh[1][0][0:64, 384:512])), [bsh[1][1]], ["afT"])
                        if d == 1:
                            A(("activation", _a(out=sgd[:], in_=bsh[2][0][:, 0:128], func=AF.Sigmoid)), [bsh[2][1]], ["sgd"])
                        yield
                        b_, bk = bank()
                        mm(b_[:, 0:256], twT[:], wup[:], True, False, ["twT", "wup"], [bk])
                        mm(b_[:, 0:256], ones32[0:1, :], w0r[:], False, True, ["ones32", "w0r"], [bk])
                        A(("activation", _a(out=nlw[:], in_=b_[:, 0:256], func=AF.Sigmoid)), [bk], ["nlw"])
                        yield
                        b_, bk = bank()
                        for bb in range(2):
                            mm(b_[:, 256 * bb:256 * bb + 256], nlw[:, 128 * bb:128 * bb + 128], tri2f[:], True, True, ["nlw", "tri2"], [bk])
                        cw = b_[:, :].rearrange("p (b x t) -> p b x t", b=2, x=2)
                        LW = 0.6065306597126334
                        A(("activation", _a(out=gin[:], in_=cw[:, :, 0, :], func=AF.Exp, scale=-LW)), [bk], ["gin"])
                        A(("activation", _a(out=gex[:], in_=cw[:, :, 1, :], func=AF.Exp, scale=-LW)), [bk], ["gex"])
                        A(("activation", _a(out=ginv[:], in_=cw[:, :, 0, :], func=AF.Exp, scale=LW)), [bk], ["ginv"])
                        yield
                        b_, bk = bank()
                        for bb in range(2):
                            mm(b_[:, 128 * bb:128 * bb + 128], aup[:, 128 * bb:128 * bb + 128], afT[:], True, True, ["aup", "afT"], [bk])
                        for bb in range(2):
                            A(("activation", _a(out=alpha[:, bb, :], in_=b_[:, 128 * bb:128 * bb + 128], func=AF.Sigmoid,
                                                                 bias=a0[:, bb:bb + 1])), [bk, "a0"], ["alpha"])
                        V(("tensor_tensor", _a(out=kkk[:], in0=rkv[:, 2:4, :], in1=kk3[:, 0, :].unsqueeze(2).to_broadcast([128, 2, 128]), op=ALU.mult)),
                          ["rkv", "kk3"], ["kkk"])
                        G(("tensor_tensor", _a(out=sq[:], in0=kkk[:], in1=kkk[:], op=ALU.mult)), ["kkk"], ["sq"])
                        yield
                        b_, bk = bank()
                        for bb in range(2):
                            mm(b_[:, 128 * bb:128 * bb + 128], bm2b[:], sq[:, bb, :], True, True, ["bm2b", "sq"], [bk])
                        A(("activation", _a(out=rinv[:], in_=b_[:, 0:256].rearrange("p (b t) -> p b t", b=2), func=AF.Sqrt, bias=1e-12)),
                          [bk], ["rinv"])
                        V(("reciprocal", _a(out=rinv[:], in_=rinv[:])), ["rinv"], ["rinv"])
                        V(("tensor_tensor", _a(out=kkn[:], in0=kkk[:], in1=rinv[:], op=ALU.mult)), ["kkk", "rinv"], ["kkn"])
                        G(("tensor_tensor", _a(out=t1[:], in0=alpha[:], in1=kk3[:, 1, :].unsqueeze(2).to_broadcast([128, 2, 128]), op=ALU.mult)),
                          ["alpha", "kk3"], ["t1"])
                        G(("tensor_tensor", _a(out=t1[:], in0=t1[:], in1=omka[:].unsqueeze(2).to_broadcast([128, 2, 128]), op=ALU.add)),
                          ["t1", "omka"], ["t1"])
                        G(("tensor_tensor", _a(out=kdir[:], in0=rkv[:, 2:4, :], in1=t1[:], op=ALU.mult)), ["rkv", "t1"], ["kdir"])
                        V(("tensor_tensor", _a(out=bvec[:], in0=kkn[:], in1=alpha[:], op=ALU.mult)), ["kkn", "alpha"], ["bvec"])
                        V(("tensor_tensor", _a(out=RtT[:], in0=rkv[:, 0:2, :], in1=gin[:], op=ALU.mult)), ["rkv", "gin"], ["RtT"])
                        V(("scalar_tensor_tensor", _a(out=AtT[:], in0=kkn[:], scalar=-1.0, in1=gex[:], op0=ALU.mult, op1=ALU.mult)),
                          ["kkn", "gex"], ["AtT"])
                        V(("tensor_tensor", _a(out=BtT[:], in0=bvec[:], in1=ginv[:], op=ALU.mult)), ["bvec", "ginv"], ["BtT"])
                        G(("tensor_tensor", _a(out=KtT[:], in0=kdir[:], in1=ginv[:], op=ALU.mult)), ["kdir", "ginv"], ["KtT"])
                        G(("tensor_copy", _a(out=fmT[:, 0, :, :], in_=AtT[:])), ["AtT"], ["fmT0"])
                        V(("tensor_tensor", _a(out=fmT[:, 1, :, :], in0=BtT[:], in1=gin[:, :, 127:128].to_broadcast([128, 2, 128]), op=ALU.mult)),
                          ["BtT", "gin"], ["fmT1"])
                        G(("tensor_tensor", _a(out=fmT[:, 2, :, :], in0=KtT[:], in1=gin[:, :, 127:128].to_broadcast([128, 2, 128]), op=ALU.mult)),
                          ["KtT", "gin"], ["fmT2"])
                        G(("tensor_copy", _a(out=fmT[:, 3, :, :], in_=rkv[:, 4:6, :])), ["rkv"], ["fmT3"])
                        yield
                        b_, bk = bank()
                        for a in range(4):
                            for bb in range(2):
                                T(("transpose", _a(bf(b_)[:, (2 * a + bb) * 128:(2 * a + bb + 1) * 128], fmT[:, a, bb, :], identb[:])),
                                  ["fmT%d" % a, "identb"], [bk])
                        V(("tensor_copy", _a(out=tok4[:].rearrange("p a c -> p (a c)"), in_=bf(b_)[:, :])), [bk], ["tok4"])
                        yield
                        for bb in range(2):
                            G(("tensor_tensor", _a(out=RAb[:, bb, :, 0, :], in0=AtT[:, bb:bb + 1, :].to_broadcast([128, 2, 128]),
                                                               in1=m2[:].unsqueeze(2).to_broadcast([128, 2, 128]), op=ALU.mult)), ["AtT", "m2"], ["RAb"])
                            V(("tensor_tensor", _a(out=RAb[:, bb, :, 1, :], in0=RtT[:, bb:bb + 1, :].to_broadcast([128, 2, 128]),
                                                               in1=m2[:].unsqueeze(2).to_broadcast([128, 2, 128]), op=ALU.mult)), ["RtT", "m2"], ["RAb"])
                        for bb in range(2):
                            yield
                            b_, bk = bank()
                            mm(b_[:, :], BtT[:, bb, :], RAbf[:, bb, :], True, True, ["BtT", "RAb"], [bk])
                            V(("tensor_tensor", _a(out=SCB[:, bb, :, :, :], in0=b_[:, :].rearrange("p (h x t) -> p h x t", h=2, x=2),
                                                                    in1=mskB[:], op=ALU.mult)), [bk, "mskB"], ["SCB"])
                            b_, bk = bank()
                            mm(b_[:, :], KtT[:, bb, :], RAbf[:, bb, :], True, True, ["KtT", "RAb"], [bk])
                            V(("tensor_tensor", _a(out=SCK[:, bb, :, :, :], in0=b_[:, :].rearrange("p (h x t) -> p h x t", h=2, x=2),
                                                                    in1=mskK[d][:], op=ALU.mult)), [bk, "mskK%d" % d], ["SCK"])
                        yield
                        N0v = SCB[:].rearrange("p b h x t -> p (b h) x t")[:, :, 0, :]
                        G(("tensor_copy", _a(out=Nk[0][:], in_=N0v)), ["SCB"], ["Nk0"])
                        b_, bk = bank()
                        for h in range(4):
                            T(("transpose", _a(bf(b_)[:, h * 128:(h + 1) * 128], Nk[0][:, h, :], identb[:])), ["Nk0", "identb"], [bk])
                        A(("copy", _a(out=Ak[0][:], in_=bf(b_)[:, 0:512].rearrange("p (h t) -> p h t", h=4))), [bk], ["Ak0"])
                        V(("tensor_tensor", _a(out=T32[:], in0=Nk[0][:], in1=ident32[:].unsqueeze(1).to_broadcast([128, 4, 128]), op=ALU.add)),
                          ["Nk0", "ident32"], ["T32"])
                        G(("tensor_copy", _a(out=T16, in_=T32[:])), ["T32"], ["T16"])
                        yield
                        for lev in range(6):
                            if lev:
                                yield
                            c_, n_ = lev % 2, (lev + 1) % 2
                            ba_, bak = bank()
                            bn_, bnk = bank()
                            for h in range(4):
                                mm(ba_[:, h * 128:(h + 1) * 128], Nk[c_][:, h, :], Ak[c_][:, h, :], True, True, ["Nk%d" % c_, "Ak%d" % c_], [bak])
                            for h in range(4):
                                mm(bn_[:, h * 128:(h + 1) * 128], Ak[c_][:, h, :], Nk[c_][:, h, :], True, True, ["Nk%d" % c_, "Ak%d" % c_], [bnk])
                            A(("copy", _a(out=Ak[n_][:], in_=ba_[:, :].rearrange("p (h t) -> p h t", h=4))), [bak], ["Ak%d" % n_])
                            if lev < 5:
                                V(("tensor_copy", _a(out=Nk[n_][:], in_=bn_[:, :].rearrange("p (h t) -> p h t", h=4))), [bnk], ["Nk%d" % n_])
                            bt_, btk = bank()
                            for h in range(4):
                                mm(bt_[:, h * 128:(h + 1) * 128], Ak[n_][:, h, :], T16[:, h, :], True, True, ["Ak%d" % n_, "T16"], [btk])
                            V(("tensor_tensor", _a(out=T32[:], in0=bt_[:, :].rearrange("p (h t) -> p h t", h=4), in1=T32[:], op=ALU.add)),
                              [btk, "T32"], ["T32"])
                            G(("tensor_copy", _a(out=T16, in_=T32[:])), ["T32"], ["T16"])
                        yield
                        for bb in range(2):
                            b_, bk = bank()
                            mm(b_[:, 0:256], tok4[:, 0, 128 * bb:128 * bb + 128], T16f[:, 256 * bb:256 * bb + 256], True, True, ["tok4", "T16"], [bk])
                            V(("tensor_copy", _a(out=WtT[0:64, bb, :], in_=b_[0:64, 0:128])), [bk], ["WtT"])
                            A(("copy", _a(out=WtT[64:128, bb, :], in_=b_[64:128, 128:256])), [bk], ["WtT"])
                        yield
                        b_, bk = bank()
                        for h in range(4):
                            mm(b_[:, 64 * h:64 * h + 64], SCK[:, h // 2, h % 2, 0, :], tok4[:, 3, 64 * h:64 * h + 64], True, True, ["SCK", "tok4"], [bk])
                        V(("tensor_copy", _a(out=X2[:], in_=b_[:, 0:256])), [bk], ["X2"])
                        yield
                        b_, bk = bank()
                        for h in range(4):
                            mm(b_[:, 64 * h:64 * h + 64], T16[:, h, :], X2[:, 64 * h:64 * h + 64], h == 0, False, ["T16", "X2"], [bk])
                        for bb in range(2):
                            mm(b_[:, 128 * bb:128 * bb + 128], WtT[:, bb, :], R16[:, bb, :], False, bb == 1, ["WtT", "R16"], [bk])
                        A(("copy", _a(out=U16[:], in_=b_[:, 0:256])), [bk], ["U16"])
                        yield
                        while not shared["gla_o"]:
                            yield
                        for bb in range(2):
                            mm(bo_[:, 256 + 128 * bb:256 + 128 * bb + 128], RtT[:, bb, :], R16[:, bb, :], False, False, ["RtT", "R16"], [bok])
                        for h in range(4):
                            mm(bo_[:, 256 + 64 * h:256 + 64 * h + 64], SCB[:, h // 2, h % 2, 1, :], U16[:, 64 * h:64 * h + 64], False, False, ["SCB", "U16"], [bok])
                            mm(bo_[:, 256 + 64 * h:256 + 64 * h + 64], SCK[:, h // 2, h % 2, 1, :], tok4[:, 3, 64 * h:64 * h + 64], False, h == 3, ["SCK", "tok4"], [bok])
                        yield
                        b_, bk = bank()
                        for bb in range(2):
                            mm(b_[:, 128 * bb:128 * bb + 128], tok4[:, 1, 128 * bb:128 * bb + 128], U16[:, 128 * bb:128 * bb + 128], True, False, ["tok4", "U16"], [bk])
                            mm(b_[:, 128 * bb:128 * bb + 128], tok4[:, 2, 128 * bb:128 * bb + 128], tok4[:, 3, 128 * bb:128 * bb + 128], False, True, ["tok4"], [bk])
                        V(("tensor_tensor", _a(out=rtmp[:], in0=b_[:, 0:256].rearrange("p (b t) -> p b t", b=2),
                                                          in1=bm2[:].unsqueeze(1).to_broadcast([128, 2, 128]), op=ALU.mult)), [bk, "bm2"], ["rtmp"])
                        G(("tensor_tensor", _a(out=R32[:], in0=R32[:], in1=gin[:, :, 127:128].to_broadcast([128, 2, 128]), op=ALU.mult)),
                          ["R32", "gin"], ["R32"])
                        G(("tensor_tensor", _a(out=R32[:], in0=R32[:], in1=rtmp[:], op=ALU.add)), ["R32", "rtmp"], ["R32"])
                        if boundary:
                            G(("tensor_scalar", _a(out=R32[:], in0=R32[:], scalar1=flag[:, 0:1], scalar2=None, op0=ALU.mult)),
                              ["R32", "flag"], ["R32"])
                        G(("tensor_copy", _a(out=R16[:], in_=R32[:])), ["R32"], ["R16"])


                    KIL = os.environ.get("KIL", "sr")
                    if KIL == "seq":
                        plan = [[ssd_gen()], [gla_gen()], [rwkv_gen()]]
                    elif KIL == "seq2":
                        plan = [[rwkv_gen()], [ssd_gen()], [gla_gen()]]
                    elif KIL == "sg":
                        plan = [[ssd_gen(), gla_gen()], [rwkv_gen()]]
                    elif KIL == "sr":
                        plan = [[gla_gen()], [rwkv_gen(), ssd_gen()]]
                    else:
                        plan = [[rwkv_gen(), ssd_gen(), gla_gen()]]
                    for gens in plan:
                        while gens:
                            for g_ in list(gens):
                                try:
                                    next(g_)
                                except StopIteration:
                                    gens.remove(g_)
                    yield "mixdone"
                    if d == 0:
                        V(("tensor_copy", _a(out=osb[:], in_=bo_[:, :])), [bok], ["osb"])
                        P.dma("gpsimd", ofd[ci * C:(ci + 1) * C, 0:512], osb[:], ["osb"], [])
                        return
                    if DBG in ("B1", "B1noJ", "B1noDMA"):
                        return
                    A(("copy", _a(out=fa[:, 0:256], in_=bo_[:, 0:256])), [bok], ["fa"])
                    G(("tensor_tensor", _a(out=fb[:, 0:256], in0=fa[:, 0:256], in1=fa[:, 0:256], op=ALU.mult)), ["fa"], ["fb"])
                    V(("tensor_reduce", _a(out=st4[:, 0:4], in_=fb[:, 0:256].rearrange("p (h q) -> p h q", h=4), axis=AX.X, op=ALU.add)),
                      ["fb"], ["st4a"])
                    A(("activation", _a(out=st4[:, 0:4], in_=st4[:, 0:4], func=AF.Sqrt, scale=1.0 / 64, bias=EPS)), ["st4a"], ["st4a"])
                    V(("reciprocal", _a(out=st4[:, 0:4], in_=st4[:, 0:4])), ["st4a"], ["st4a"])
                    V(("tensor_tensor", _a(out=fa[:, 0:256].rearrange("p (h q) -> p h q", h=4), in0=fa[:, 0:256].rearrange("p (h q) -> p h q", h=4),
                                                in1=st4[:, 0:4].unsqueeze(2).to_broadcast([128, 4, 64]), op=ALU.mult)), ["fa", "st4a"], ["fa"])
                    V(("tensor_tensor", _a(out=fa[:, 0:256].rearrange("p (h q) -> p h q", h=4), in0=fa[:, 0:256].rearrange("p (h q) -> p h q", h=4),
                                                in1=glan[:].unsqueeze(1).to_broadcast([128, 4, 64]), op=ALU.mult)), ["fa", "glan"], ["fa"])
                    V(("tensor_tensor", _a(out=mix[:, 0:256], in0=fa[:, 0:256], in1=sgz[i % 3][:, 0:256], op=ALU.mult)), ["fa", "sgz%d" % (i % 3)], ["mixg"])
                    yield
                    ya = fa[:, 256:512]
                    yb = fb[:, 256:512]
                    A(("copy", _a(out=ya, in_=bo_[:, 256:512])), [bok], ["ya"])
                    V(("tensor_reduce", _a(out=st4[:, 4:8], in_=ya.rearrange("p (h q) -> p h q", h=4), axis=AX.X, op=ALU.add)), ["ya"], ["st4b"])
                    V(("tensor_scalar", _a(out=st4[:, 4:8], in0=st4[:, 4:8], scalar1=1.0 / 64, scalar2=None, op0=ALU.mult)), ["st4b"], ["st4b"])
                    V(("tensor_tensor", _a(out=ya.rearrange("p (h q) -> p h q", h=4), in0=ya.rearrange("p (h q) -> p h q", h=4),
                                                in1=st4[:, 4:8].unsqueeze(2).to_broadcast([128, 4, 64]), op=ALU.subtract)), ["ya", "st4b"], ["ya"])
                    G(("tensor_tensor", _a(out=yb, in0=ya, in1=ya, op=ALU.mult)), ["ya"], ["yb"])
                    V(("tensor_reduce", _a(out=st4[:, 8:12], in_=yb.rearrange("p (h q) -> p h q", h=4), axis=AX.X, op=ALU.add)), ["yb"], ["st4c"])
                    A(("activation", _a(out=st4[:, 8:12], in_=st4[:, 8:12], func=AF.Sqrt, scale=1.0 / 64, bias=64e-5)), ["st4c"], ["st4c"])
                    V(("reciprocal", _a(out=st4[:, 8:12], in_=st4[:, 8:12])), ["st4c"], ["st4c"])
                    V(("tensor_tensor", _a(out=ya.rearrange("p (h q) -> p h q", h=4), in0=ya.rearrange("p (h q) -> p h q", h=4),
                                                in1=st4[:, 8:12].unsqueeze(2).to_broadcast([128, 4, 64]), op=ALU.mult)), ["ya", "st4c"], ["ya"])
                    V(("tensor_tensor", _a(out=ya, in0=ya, in1=lng[:], op=ALU.mult)), ["ya", "lng"], ["ya"])
                    V(("tensor_tensor", _a(out=ya, in0=ya, in1=lnb[:], op=ALU.add)), ["ya", "lnb"], ["ya"])
                    yield
                    G(("tensor_tensor", _a(out=rtmp[:], in0=rkv[:, 0:2, :], in1=rkv[:, 2:4, :], op=ALU.mult)), ["rkv", "rtmp"], ["rtmp"])
                    G(("tensor_tensor", _a(out=prodT[:], in0=rtmp[:], in1=kk3[:, 2, :].unsqueeze(2).to_broadcast([128, 2, 128]), op=ALU.mult)),
                      ["rtmp", "kk3"], ["prodT"])
                    b_, bk = bank()
                    for bb in range(2):
                        mm(b_[:, 2 * bb:2 * bb + 2], prodT[:, bb, :], hsel[:], True, True, ["prodT", "hsel"], [bk])
                    V(("tensor_copy", _a(out=st4[:, 12:16], in_=b_[:, 0:4])), [bk], ["st4d"])
                    V(("tensor_tensor", _a(out=yb.rearrange("p (h q) -> p h q", h=4), in0=tok4[:, 3, :].rearrange("p (h q) -> p h q", h=4),
                                                in1=st4[:, 12:16].unsqueeze(2).to_broadcast([128, 4, 64]), op=ALU.mult)), ["tok4", "st4d", "yb"], ["yb"])
                    V(("tensor_tensor", _a(out=ya, in0=ya, in1=yb, op=ALU.add)), ["ya", "yb"], ["ya"])
                    yield
                    b_, bk = bank()
                    mm(b_[:, 0:256], sgd[:], gup[:], True, True, ["sgd", "gup"], [bk])
                    V(("tensor_tensor", _a(out=mix[:, 256:512], in0=ya, in1=b_[:, 0:256], op=ALU.mult)), ["ya", bk], ["mixr"])
                    yield
                    G(("tensor_tensor", _a(out=ytmp[:].rearrange("p (h q) -> p h q", h=8), in0=xsB[:, 0:512].rearrange("p (h q) -> p h q", h=8),
                                                in1=ssdD[:].unsqueeze(2).to_broadcast([128, 8, 64]), op=ALU.mult)), ["xsB", "ssdD", "ytmp"], ["ytmp"])
                    V(("tensor_tensor", _a(out=ysum[:], in0=ysum[:], in1=ytmp[:], op=ALU.add)), ["ysum", "ytmp"], ["ysum"])
                    V(("tensor_tensor", _a(out=ysum[:], in0=ysum[:], in1=sgz[i % 3][:, 256:768], op=ALU.mult)), ["ysum", "sgz%d" % (i % 3)], ["ysum"])
                    A(("activation", _a(out=ytmp[:], in_=ysum[:], func=AF.Square, accum_out=st8[:, 3:4])), ["ysum", "ytmp"], ["ytmp", "st8e"])
                    A(("activation", _a(out=st8[:, 3:4], in_=st8[:, 3:4], func=AF.Sqrt, scale=1.0 / 512, bias=EPS)), ["st8e"], ["st8e"])
                    V(("reciprocal", _a(out=st8[:, 3:4], in_=st8[:, 3:4])), ["st8e"], ["st8e"])
                    V(("scalar_tensor_tensor", _a(out=mix[:, 512:1024], in0=ysum[:], scalar=st8[:, 3:4], in1=ssdn[:], op0=ALU.mult, op1=ALU.mult)),
                      ["ysum", "st8e", "ssdn"], ["mixs"])
                    yield
                    if DBG != "B2":
                        P.dma("sync", mixd[ci * C:(ci + 1) * C, :], mix[:], ["mixg", "mixr", "mixs"], [])

                if DBG == "consts" or DBG3 == "setup":
                    continue
                def drain(g_):
                    for _ in g_:
                        pass

                drain(stageA(0))
                if NCH > 1:
                    drain(stageA(1))
                for i in range(NCH):
                    gb = stageB(i)
                    for tok_ in gb:
                        if tok_ == "mixdone":
                            break
                    gens = [gb] + ([stageA(i + 2)] if i + 2 < NCH else [])
                    if os.environ.get("KSEQ"):
                        for g_ in gens:
                            drain(g_)
                        continue
                    while gens:
                        for g_ in list(gens):
                            try:
                                next(g_)
                            except StopIteration:
                                gens.remove(g_)
                if DBG in ("A", "ssd", "gla", "F"):
                    break
                if DBG == "Bonly":
                    break

        if DBG in ("A", "ssd", "gla", "F", "consts", "B", "B1", "B2", "B1noJ", "B1noDMA", "FF", "Bonly"):
            break
        P.barrier()
        with ExitStack() as ss:
            sb = lambda name, shape, dt: P.sb("L%dC_%s" % (l, name), shape, dt, ss)
            cst = mk_consts(sb, False)
            identb, Jb = cst["identb"], cst["Jb"]
            set_stg(sb, 1024)
            wout = sb("wout", [128, 8, D], BF16)
            wg = sb("wg", [128, 8, DFF], BF16)
            wu = sb("wu", [128, 8, DFF], BF16)
            wd = sb("wd", [128, NFC, D], BF16)
            load_w(wout, "wout", W["wout", l], D, D)
            load_w(wg, "wg", W["wg", l], D, DFF)
            load_w(wu, "wu", W["wu", l], D, DFF)
            load_w(wd, "wd", W["wd", l], DFF, D)
            gffn = sb("gffn", [128, 8], F32)
            P.dma("sync", gffn[:], W["gffn", l][:, :], [], ["gffn"])
            last = (l == DEPTH - 1)
            if last:
                gfin = sb("gfin", [128, D], F32)
                P.dma("sync", gfin[:], W["gfin"][:, :], [], ["gfin"])
            TT = min(int(os.environ.get("KTT", "2")), NCH)
            xc = [sb("xc%d" % i, [128, D], F32) for i in range(TT)]
            mixc = [sb("mixc%d" % i, [128, D], BF16) for i in range(2)]
            mixT = sb("mixT", [128, 8, 128], BF16)
            hTc = sb("hTc", [128, 8, TT * 128], BF16)
            actT = sb("actT", [128, NFC, TT * 128], BF16)
            sgc = [sb("sgc%d" % i, [128, TT * 128], BF16) for i in range(2)]
            hn = sb("hn", [128, D], BF16)
            st8 = sb("st8", [128, 8], F32)
            xo = sb("xo", [128, D], F32)
            dst = y_out if last else xnext
            ntile = (NCH + TT - 1) // TT
            for ti in range(ntile):
                nsub = min(TT, NCH - ti * TT)
                NTK = nsub * 128
                for sub in range(nsub):
                    ci = ti * TT + sub
                    xk = "xc%d" % sub
                    mk_ = "mixc%d" % (ci % 2)
                    mc = mixc[ci % 2]
                    P.dma("sync", xc[sub][:], xsrc[ci * C:(ci + 1) * C, :], [], [xk])
                    P.dma("gpsimd", mc[:], mixd[ci * C:(ci + 1) * C, :], [], [mk_])
                    b_, bk = bank()
                    for kc in range(8):
                        T(("transpose", _a(bf(b_)[:, kc * 128:(kc + 1) * 128], mc[:, kc * 128:(kc + 1) * 128], Jb[:])),
                          [mk_, "Jb"], [bk])
                    A(("copy", _a(out=mixT[:].rearrange("p a t -> p (a t)"), in_=bf(b_)[:, :])), [bk], ["mixT"])
                    for n in range(2):
                        b_, bk = bank()
                        for kc in range(8):
                            mm(b_[:, :], mixT[:, kc, :], wout[:, kc, 512 * n:512 * n + 512], kc == 0, kc == 7, ["mixT", "wout"], [bk])
                        V(("tensor_tensor", _a(out=xc[sub][:, 512 * n:512 * n + 512], in0=b_[:, :], in1=xc[sub][:, 512 * n:512 * n + 512], op=ALU.add)),
                          [bk, xk], [xk])
                    A(("activation", _a(out=hn[:], in_=xc[sub][:], func=AF.Square, accum_out=st8[:, 0:1])), [xk], ["hn", "st8a"])
                    A(("activation", _a(out=st8[:, 1:2], in_=st8[:, 0:1], func=AF.Sqrt, scale=1.0 / D, bias=EPS)), ["st8a"], ["st8b"])
                    V(("reciprocal", _a(out=st8[:, 2:3], in_=st8[:, 1:2])), ["st8b"], ["st8c"])
                    V(("tensor_scalar", _a(out=hn[:], in0=xc[sub][:], scalar1=st8[:, 2:3], scalar2=None, op0=ALU.mult)), [xk, "st8c"], ["hn"])
                    b_, bk = bank()
                    for kc in range(8):
                        T(("transpose", _a(bf(b_)[:, kc * 128:(kc + 1) * 128], hn[:, kc * 128:(kc + 1) * 128], identb[:])), ["hn", "identb"], [bk])
                    V(("tensor_tensor", _a(out=hTc[:, :, sub * 128:(sub + 1) * 128], in0=bf(b_).rearrange("p (a b) -> p a b", a=8),
                                                               in1=gffn[:].unsqueeze(2).to_broadcast([128, 8, 128]), op=ALU.mult)), [bk, "gffn"], ["hTc"])
                for fc in range(NFC):
                    bg_, bgk = bank()
                    bu_, buk = bank()
                    for kc in range(8):
                        mm(bg_[:, 0:NTK], wg[:, kc, fc * 128:(fc + 1) * 128], hTc[:, kc, 0:NTK], kc == 0, kc == 7, ["wg", "hTc"], [bgk])
                    for kc in range(8):
                        mm(bu_[:, 0:NTK], wu[:, kc, fc * 128:(fc + 1) * 128], hTc[:, kc, 0:NTK], kc == 0, kc == 7, ["wu", "hTc"], [buk])
                    sgb = sgc[fc % 2]
                    sgk = "sgc%d" % (fc % 2)
                    A(("activation", _a(out=sgb[:, 0:NTK], in_=bg_[:, 0:NTK], func=AF.Silu)), [bgk], [sgk])
                    V(("tensor_tensor", _a(out=actT[:, fc, 0:NTK], in0=bu_[:, 0:NTK], in1=sgb[:, 0:NTK], op=ALU.mult)),
                      [buk, sgk], ["actT"])
                for sub in range(nsub):
                    ci = ti * TT + sub
                    xk = "xc%d" % sub
                    for n in range(2):
                        b_, bk = bank()
                        for fc in range(NFC):
                            mm(b_[:, :], actT[:, fc, sub * 128:(sub + 1) * 128], wd[:, fc, 512 * n:512 * n + 512], fc == 0, fc == NFC - 1, ["actT", "wd"], [bk])
                        V(("tensor_tensor", _a(out=xo[:, 512 * n:512 * n + 512], in0=b_[:, :], in1=xc[sub][:, 512 * n:512 * n + 512], op=ALU.add)),
                          [bk, xk], ["xo"])
                    if last:
                        A(("activation", _a(out=hn[:], in_=xo[:], func=AF.Square, accum_out=st8[:, 4:5])), ["xo", "hn"], ["hn", "st8f"])
                        A(("activation", _a(out=st8[:, 4:5], in_=st8[:, 4:5], func=AF.Sqrt, scale=1.0 / D, bias=EPS)), ["st8f"], ["st8f"])
                        V(("reciprocal", _a(out=st8[:, 4:5], in_=st8[:, 4:5])), ["st8f"], ["st8f"])
                        V(("scalar_tensor_tensor", _a(out=xo[:], in0=xo[:], scalar=st8[:, 4:5], in1=gfin[:], op0=ALU.mult, op1=ALU.mult)),
                          ["xo", "st8f", "gfin"], ["xo"])
                    P.dma("sync", dst[ci * C:(ci + 1) * C, :], xo[:], ["xo"], [])
    P.emit()
    P.stack.close()
    return nc


def make_weight_maps(prm, DEPTH=2):
    f = lambda a: np.ascontiguousarray(np.asarray(a, dtype=np.float32))
    rep = lambda v, n=128: f(np.broadcast_to(np.asarray(v, np.float32).reshape(1, -1), (n, np.asarray(v).size)))
    fmaj = lambda v, nb: f(np.asarray(v, np.float32).reshape(nb, 128).T)
    m = {}
    for l in range(DEPTH):
        win = np.asarray(prm["w_in"][l], np.float32)
        for d in (0, 1):
            m["wfm_%d_%d" % (l, d)] = f(win[:, col_index(fm_blocks(d))])
            m["wtm_%d_%d" % (l, d)] = f(win[:, col_index(tm_cols(d))])
            m["gla_aup_%d_%d" % (l, d)] = f(prm["gla_a_up"][l][d])
            m["gla_ab_%d_%d" % (l, d)] = f(prm["gla_a_bias"][l][d].reshape(1, 128))
            mu = np.asarray(prm["rwkv_mu"][l], np.float32)
            mub = np.zeros((128, 9), np.float32)
            for b in range(6):
                mub[:, b] = mu[128 * b:128 * b + 128]
            mub[0:64, 6] = mu[768 + 64 * d:768 + 64 * d + 64]
            mub[0:64, 7] = mu[896 + 64 * d:896 + 64 * d + 64]
            mub[:, 8] = mu[1024:1152]
            m["mu_%d_%d" % (l, d)] = mub
            m["w0_%d_%d" % (l, d)] = f(prm["rwkv_w0"][l][d].reshape(1, 256))
            m["wup_%d_%d" % (l, d)] = f(prm["rwkv_w_up"][l][d])
            m["a0_%d_%d" % (l, d)] = fmaj(prm["rwkv_a0"][l][d], 2)
            m["aup_%d_%d" % (l, d)] = f(prm["rwkv_a_up"][l][d])
            m["dtb_%d_%d" % (l, d)] = rep(prm["ssd_dt_bias"][l][d])
            m["alog_%d_%d" % (l, d)] = rep(prm["ssd_A_log"][l][d])
        m["wout_%d" % l] = f(prm["w_out"][l])
        m["wg_%d" % l] = f(prm["ffn_gate"][l])
        m["wu_%d" % l] = f(prm["ffn_up"][l])
        m["wd_%d" % l] = f(prm["ffn_down"][l])
        m["gmix_%d" % l] = fmaj(prm["norm_mix"][l], 8)
        m["gffn_%d" % l] = fmaj(prm["norm_ffn"][l], 8)
        m["gla_norm_%d" % l] = rep(prm["gla_norm"][l])
        m["gup_%d" % l] = f(prm["rwkv_g_up"][l])
        kk3 = np.stack([fmaj(prm["rwkv_k_k"][l], 2), fmaj(prm["rwkv_k_a"][l], 2), fmaj(prm["rwkv_r_k"][l], 2)], axis=1)
        m["kk3_%d" % l] = f(kk3.reshape(128, 6))
        m["lng_%d" % l] = rep(prm["rwkv_ln_g"][l])
        m["lnb_%d" % l] = rep(prm["rwkv_ln_b"][l])
        cw = np.asarray(prm["ssd_conv_w"][l], np.float32)
        m["convw_%d" % l] = f(cw.reshape(5, 8, 128).transpose(2, 1, 0).reshape(128, 40))
        m["convb_%d" % l] = fmaj(prm["ssd_conv_b"][l], 8)
        m["ssdD_%d" % l] = rep(prm["ssd_D"][l])
        m["ssdn_%d" % l] = rep(prm["ssd_norm"][l])
    m["gfin"] = rep(prm["final_norm"])
    return m


_CACHE = {}


def run_cores(core_x, core_flag, prm, NSLOT, SLOT, DEPTH=2, runner=None):
    key = (NSLOT, SLOT, DEPTH)
    nc = build_program(NSLOT, SLOT, DEPTH)
    wm = make_weight_maps(prm, DEPTH)
    in_maps = []
    for x, fl in zip(core_x, core_flag):
        mp = dict(wm)
        mp["x_in"] = np.ascontiguousarray(x, dtype=np.float32)
        mp["flag"] = np.full((128, 1), fl, np.float32)
        in_maps.append(mp)
    if runner is None:
        res = run_bass_kernel_spmd(nc, in_maps, core_ids=list(range(len(in_maps))))
        return [r["y_out"] for r in res.results]
    return [r["y_out"] for r in runner(nc, in_maps)]


def kernel(**inputs):
    xp = np.asarray(inputs["x_prompt"], np.float32)
    xs = np.asarray(inputs["x_sample"], np.float32)
    prm = {k: np.asarray(v) for k, v in inputs.items() if k not in ("x_prompt", "x_sample")}
    NSLOT, SLOT = 8, 2048
    NT = NSLOT * SLOT
    assign = [[], []] + [[] for _ in range(6)]
    for b in range(xp.shape[0]):
        assign[2 + b % 6].append(b)
    core_x, core_flag = [], []
    for c in range(8):
        if c < 2:
            core_x.append(xs[c])
            core_flag.append(1.0)
        else:
            buf = np.empty((NT, D), np.float32)
            for s in range(NSLOT):
                b = assign[c][s % len(assign[c])]
                buf[s * SLOT:(s + 1) * SLOT] = xp[b]
            core_x.append(buf)
            core_flag.append(0.0)
    outs = run_cores(core_x, core_flag, prm, NSLOT, SLOT)
    yp = np.zeros_like(xp)
    ys = np.zeros_like(xs)
    for c in range(8):
        if c < 2:
            ys[c] = outs[c]
        else:
            for s, b in enumerate(assign[c]):
                yp[b] = outs[c][s * SLOT:(s + 1) * SLOT]
    return (yp, ys)
```

```python
import numpy as np
import concourse.bass as bass
import concourse.mybir as mybir
from concourse.bass_utils import run_bass_kernel_spmd
from contextlib import ExitStack

F32 = mybir.dt.float32
BF16 = mybir.dt.bfloat16
ALU = mybir.AluOpType
AF = mybir.ActivationFunctionType
AX = mybir.AxisListType

ENGS = ("tensor", "vector", "scalar", "gpsimd", "sync")
DMA_RING = 6
import os
DBG = os.environ.get("KDBG", "")
DBG2 = os.environ.get("KDBG2", "")
DBG3 = os.environ.get("KDBG3", "")
C = 128
D = 1024
DFF = 2816
NFC = DFF // 128
EPS = 1e-6


def _a(*args, **kw):
    return (args, kw)


def _call(fn, e):
    if isinstance(fn, tuple):
        return getattr(e, fn[0])(*fn[1][0], **fn[1][1])
    return fn(e)


class Op:
    __slots__ = ("eng", "fn", "reads", "writes", "dma", "idx", "deps", "sig", "ring", "ringn")


class Prog:
    def __init__(self, nc):
        self.nc = nc
        self.ops = []
        self.stack = ExitStack()
        self.last_w = {}
        self.readers = {}
        self.ndma = {e: 0 for e in ENGS}
        self.last_eng = {}
        self.ring_last = {}
        self.bar_deps = []
        self.bar_seen = {e: True for e in ENGS}

    def sb(self, name, shape, dt, stack=None):
        return (stack or self.stack).enter_context(self.nc.sbuf_tensor(name, list(shape), dt))

    def ps(self, name, shape, dt):
        return self.stack.enter_context(self.nc.psum_tensor(name, list(shape), dt))

    def barrier(self):
        deps = [v for v in self.last_eng.values()] + [v for v in self.ring_last.values()]
        self.bar_deps = sorted(set(deps))
        self.bar_seen = {e: False for e in ENGS}
        self.last_w = {}
        self.readers = {}

    def op(self, eng, fn, reads=(), writes=(), dma=False):
        o = Op()
        o.eng, o.fn, o.reads, o.writes, o.dma = eng, fn, tuple(reads), tuple(writes), dma
        o.sig = o.ring = o.ringn = None
        o.idx = len(self.ops)
        deps = set()
        if not self.bar_seen[eng]:
            deps.update(self.bar_deps)
            self.bar_seen[eng] = True
        for k in o.reads:
            w = self.last_w.get(k)
            if w is not None:
                deps.add(w)
        for k in o.writes:
            w = self.last_w.get(k)
            if w is not None:
                deps.add(w)
            deps.update(self.readers.get(k, ()))
        for k in o.reads:
            self.readers.setdefault(k, []).append(o.idx)
        for k in o.writes:
            self.last_w[k] = o.idx
            self.readers[k] = []
        deps.discard(o.idx)
        o.deps = sorted(deps)
        if dma:
            n = self.ndma[eng]
            o.ring = n % DMA_RING
            o.ringn = n // DMA_RING + 1
            self.ndma[eng] = n + 1
            self.ring_last[(eng, o.ring)] = o.idx
        else:
            self.last_eng[eng] = o.idx
        self.ops.append(o)
        return o

    def dma(self, eng, out, in_, reads=(), writes=()):
        return self.op(eng, lambda e: e.dma_start(out=out, in_=in_), reads, writes, dma=True)

    def emit(self):
        nc, ops = self.nc, self.ops
        needed = set()
        for o in ops:
            for d in o.deps:
                p = ops[d]
                if p.dma:
                    continue
                if p.eng == "tensor" and o.eng == "tensor" and not o.dma:
                    continue
                needed.add(d)
        cnt = {e: 0 for e in ENGS}
        for o in ops:
            if (not o.dma) and o.idx in needed:
                cnt[o.eng] += 1
                o.sig = cnt[o.eng]
        per = {e: [o for o in ops if o.eng == e] for e in ENGS}
        st = self.stack
        csem = {e: st.enter_context(nc.semaphore("c_" + e)) for e in ("tensor", "vector", "scalar", "gpsimd")}
        dsem = {}
        for e in ENGS:
            if self.ndma[e] > 0:
                dsem[e] = [st.enter_context(nc.semaphore("d_%s_%d" % (e, i))) for i in range(DMA_RING)]
        block = st.enter_context(nc.Block())
        ndma = self.ndma

        def run(ename, eobj):
            waited = {}

            def wait(sem, val):
                key = id(sem)
                if waited.get(key, 0) >= val:
                    return
                waited[key] = val
                eobj.wait_ge(sem, val)

            for o in per[ename]:
                for d in o.deps:
                    p = ops[d]
                    if p.dma:
                        wait(dsem[p.eng][p.ring], 16 * p.ringn)
                    else:
                        if p.eng == "tensor" and ename == "tensor" and not o.dma:
                            continue
                        wait(csem[p.eng], p.sig)
                if o.dma:
                    if o.ringn > 1:
                        wait(dsem[ename][o.ring], 16 * (o.ringn - 1))
                    _call(o.fn, eobj).then_inc(dsem[ename][o.ring], 16)
                else:
                    ins = _call(o.fn, eobj)
                    if o.sig is not None:
                        ins.then_inc(csem[ename], 1)
            if ename == "sync":
                for qe, sems in dsem.items():
                    n = ndma[qe]
                    for r in range(DMA_RING):
                        if n > r:
                            wait(sems[r], 16 * ((n - r + DMA_RING - 1) // DMA_RING))

        for en in ("tensor", "vector", "scalar", "gpsimd", "sync"):
            if per[en] or en == "sync":
                getattr(block, en)(lambda e, en=en: run(en, e))


G0, R0, S0 = 0, 800, 1952


def fm_blocks(d):
    bl = [(S0 + 512 + 128 * b, 128) for b in range(8)]
    bl += [(R0 + 128 * b, 128) for b in range(6)]
    bl += [(R0 + 768 + 64 * d, 64), (R0 + 896 + 64 * d, 64)]
    if d == 1:
        bl += [(R0 + 1024, 128)]
    bl += [(G0, 128), (G0 + 128, 128), (G0 + 768 + 16 * d, 16)]
    return bl


def tm_cols(d):
    cols = [(G0 + 256, 256)]
    if d == 1:
        cols += [(G0 + 512, 256), (S0, 512)]
    cols += [(S0 + 1536 + 8 * d, 8)]
    return cols


def col_index(blocks):
    return np.concatenate([np.arange(s, s + w) for s, w in blocks])


def build_program(NSLOT, SLOT, DEPTH=2):
    NT = NSLOT * SLOT
    NCH = NT // C
    CPS = SLOT // C
    nc = bass.Bass("TRN2", target_bir_lowering=False)
    P = Prog(nc)

    def din(name, shape):
        return nc.dram_tensor(name, list(shape), F32, kind="ExternalInput").ap()

    x_in = din("x_in", [NT, D])
    y_out = nc.dram_tensor("y_out", [NT, D], F32, kind="ExternalOutput").ap()
    mixd = nc.dram_tensor("mixd", [NT, D], BF16, kind="Internal").ap()
    xnext = nc.dram_tensor("xnext", [NT, D], F32, kind="Internal").ap()
    ofd = nc.dram_tensor("ofd", [NT, D], F32, kind="Internal").ap()
    flag_d = din("flag", [128, 1])
    NFM = [sum(w for _, w in fm_blocks(d)) for d in (0, 1)]
    NTM = [sum(w for _, w in tm_cols(d)) for d in (0, 1)]
    W = {}
    for l in range(DEPTH):
        for d in (0, 1):
            W["wfm", l, d] = din("wfm_%d_%d" % (l, d), [D, NFM[d]])
            W["wtm", l, d] = din("wtm_%d_%d" % (l, d), [D, NTM[d]])
            W["gla_aup", l, d] = din("gla_aup_%d_%d" % (l, d), [16, 128])
            W["gla_ab", l, d] = din("gla_ab_%d_%d" % (l, d), [1, 128])
            W["mu", l, d] = din("mu_%d_%d" % (l, d), [128, 9])
            W["w0", l, d] = din("w0_%d_%d" % (l, d), [1, 256])
            W["wup", l, d] = din("wup_%d_%d" % (l, d), [64, 256])
            W["a0", l, d] = din("a0_%d_%d" % (l, d), [128, 2])
            W["aup", l, d] = din("aup_%d_%d" % (l, d), [64, 256])
            W["dtb", l, d] = din("dtb_%d_%d" % (l, d), [128, 8])
            W["alog", l, d] = din("alog_%d_%d" % (l, d), [128, 8])
        W["wout", l] = din("wout_%d" % l, [D, D])
        W["wg", l] = din("wg_%d" % l, [D, DFF])
        W["wu", l] = din("wu_%d" % l, [D, DFF])
        W["wd", l] = din("wd_%d" % l, [DFF, D])
        W["gmix", l] = din("gmix_%d" % l, [128, 8])
        W["gffn", l] = din("gffn_%d" % l, [128, 8])
        W["gla_norm", l] = din("gla_norm_%d" % l, [128, 64])
        W["gup", l] = din("gup_%d" % l, [128, 256])
        W["kk3", l] = din("kk3_%d" % l, [128, 6])
        W["lng", l] = din("lng_%d" % l, [128, 256])
        W["lnb", l] = din("lnb_%d" % l, [128, 256])
        W["convw", l] = din("convw_%d" % l, [128, 40])
        W["convb", l] = din("convb_%d" % l, [128, 8])
        W["ssdD", l] = din("ssdD_%d" % l, [128, 8])
        W["ssdn", l] = din("ssdn_%d" % l, [128, 512])
    W["gfin"] = din("gfin", [128, D])

    V = lambda fn, r=(), w=(): P.op("vector", fn, r, w)
    A = lambda fn, r=(), w=(): P.op("scalar", fn, r, w)
    G = lambda fn, r=(), w=(): P.op("gpsimd", fn, r, w)
    T = lambda fn, r=(), w=(): P.op("tensor", fn, r, w)

    def mm(out, lhsT, rhs, start, stop, r, w):
        T(lambda e: e.matmul(out, lhsT=lhsT, rhs=rhs, start=start, stop=stop), r, w)

    banks = [P.ps("bank%d" % i, [128, 512], F32) for i in range(8)]
    bstate = {"i": 0}

    def bank():
        i = bstate["i"]
        bstate["i"] = (i + 1) % 7
        return banks[i], "bank%d" % i

    def bf(b):
        return b[:].bitcast(BF16)

    def asel(out, in_, pattern, op, fill, base, cm, key):
        G(("affine_select", _a(out=out, in_=in_, pattern=pattern, compare_op=op, fill=fill, base=base,
                                    channel_multiplier=cm)), [key], [key])

    def mk_consts(sbf, full):
        c = {}
        ident32 = sbf("ident32", [128, 128], F32)
        J32 = sbf("J32", [128, 128], F32)
        identb = sbf("identb", [128, 128], BF16)
        Jb = sbf("Jb", [128, 128], BF16)
        G(("memset", _a(ident32[:], 1.0)), [], ["ident32"])
        asel(ident32[:], ident32[:], [[-1, 128]], ALU.is_equal, 0.0, 0, 1, "ident32")
        G(("memset", _a(J32[:], 1.0)), [], ["J32"])
        asel(J32[:], J32[:], [[1, 128]], ALU.is_equal, 0.0, -127, 1, "J32")
        V(("tensor_copy", _a(out=identb[:], in_=ident32[:])), ["ident32"], ["identb"])
        V(("tensor_copy", _a(out=Jb[:], in_=J32[:])), ["J32"], ["Jb"])
        c.update(ident32=ident32, J32=J32, identb=identb, Jb=Jb)
        if not full:
            return c
        tri2f = sbf("tri2", [128, 256], F32)
        tri2 = tri2f[:].rearrange("p (x t) -> p x t", x=2)
        ones32 = sbf("ones32", [128, 128], F32)
        negmf = [sbf("negm%d" % d, [128, 512], BF16) for d in (0, 1)]
        negm = [t_[:].rearrange("p (h t) -> p h t", h=4) for t_ in negmf]
        qmaskf = sbf("qmask", [128, 512], BF16)
        qmask = qmaskf[:].rearrange("p (h t) -> p h t", h=4)
        bm4 = sbf("bm4", [128, 256], F32)
        bm2 = sbf("bm2", [128, 128], F32)
        bm2b = sbf("bm2b", [128, 128], BF16)
        hsel = sbf("hsel", [128, 2], BF16)
        m2 = sbf("m2", [128, 2], F32)
        flag = sbf("flagsb", [128, 1], F32)
        G(("memset", _a(ones32[:], 1.0)), [], ["ones32"])
        G(("memset", _a(tri2f[:], 1.0)), [], ["tri2"])
        asel(tri2[:, 0, :], tri2[:, 0, :], [[1, 128]], ALU.is_ge, 0.0, 0, -1, "tri2")
        asel(tri2[:, 1, :], tri2[:, 1, :], [[1, 128]], ALU.is_gt, 0.0, 0, -1, "tri2")
        for d in (0, 1):
            G(("memset", _a(negmf[d][:], 0.0)), [], ["negm%d" % d])
            for h in range(4):
                asel(negm[d][:, h, :], negm[d][:, h, :], [[1, 128]], ALU.is_ge if d == 0 else ALU.is_gt, -30000.0, 0, -1,
                     "negm%d" % d)
        G(("memset", _a(qmaskf[:], 1.0)), [], ["qmask"])
        G(("memset", _a(bm4[:], 1.0)), [], ["bm4"])
        for h in range(4):
            asel(qmask[:, h, :], qmask[:, h, :], [[0, 128]], ALU.is_ge, 0.0, -32 * h, 1, "qmask")
            asel(qmask[:, h, :], qmask[:, h, :], [[0, 128]], ALU.is_ge, 0.0, 32 * h + 31, -1, "qmask")
            asel(bm4[:, 64 * h:64 * h + 64], bm4[:, 64 * h:64 * h + 64], [[0, 64]], ALU.is_ge, 0.0, -32 * h, 1, "bm4")
            asel(bm4[:, 64 * h:64 * h + 64], bm4[:, 64 * h:64 * h + 64], [[0, 64]], ALU.is_ge, 0.0, 32 * h + 31, -1, "bm4")
        G(("memset", _a(bm2[:], 1.0)), [], ["bm2"])
        asel(bm2[:, 0:64], bm2[:, 0:64], [[0, 64]], ALU.is_ge, 0.0, 63, -1, "bm2")
        asel(bm2[:, 64:128], bm2[:, 64:128], [[0, 64]], ALU.is_ge, 0.0, -64, 1, "bm2")
        V(("tensor_copy", _a(out=bm2b[:], in_=bm2[:])), ["bm2"], ["bm2b"])
        G(("memset", _a(m2[:], 1.0)), [], ["m2"])
        asel(m2[:, 0:1], m2[:, 0:1], [[0, 1]], ALU.is_ge, 0.0, 63, -1, "m2")
        asel(m2[:, 1:2], m2[:, 1:2], [[0, 1]], ALU.is_ge, 0.0, -64, 1, "m2")
        V(("tensor_copy", _a(out=hsel[:], in_=m2[:])), ["m2"], ["hsel"])
        P.dma("sync", flag[:], flag_d[:, :], [], ["flag"])
        mskB = sbf("mskB", [128, 2, 2, 128], BF16)
        mskK = [sbf("mskK%d" % d, [128, 2, 2, 128], BF16) for d in (0, 1)]
        for hh in range(2):
            V(("tensor_copy", _a(out=mskB[:, hh, 0, :], in_=tri2[:, 1, :])), ["tri2"], ["mskB"])
            V(("tensor_copy", _a(out=mskB[:, hh, 1, :], in_=tri2[:, 0, :])), ["tri2"], ["mskB"])
            for d in (0, 1):
                V(("tensor_copy", _a(out=mskK[d][:, hh, 0, :], in_=tri2[:, 1, :])), ["tri2"], ["mskK%d" % d])
                V(("tensor_copy", _a(out=mskK[d][:, hh, 1, :], in_=tri2[:, d, :])), ["tri2"], ["mskK%d" % d])
        c.update(tri2f=tri2f, tri2=tri2, ones32=ones32, negmf=negmf, negm=negm, qmaskf=qmaskf, qmask=qmask, bm4=bm4,
                 bm2=bm2, bm2b=bm2b, hsel=hsel, m2=m2, flag=flag, mskB=mskB, mskK=mskK)
        return c

    ld = {"i": 0, "stg": None, "w": 1024}

    def set_stg(sbf, width):
        ld["stg"] = [sbf("stg%d" % i, [128, width], F32) for i in range(2)]
        ld["w"] = width

    def load_cast(dst_ap, src_ap, ncols, dkey, np_=128):
        i = ld["i"]
        ld["i"] += 1
        s = ld["stg"][i % 2]
        sk = "stg%d" % (i % 2)
        P.dma("sync" if i % 2 == 0 else "gpsimd", s[0:np_, 0:ncols], src_ap, [], [sk])
        eng = ("vector", "gpsimd", "scalar")[i % 3]
        if eng == "scalar":
            A(("copy", _a(out=dst_ap, in_=s[0:np_, 0:ncols])), [sk], [dkey])
        else:
            P.op(eng, ("tensor_copy", _a(out=dst_ap, in_=s[0:np_, 0:ncols])), [sk], [dkey])

    def load_w(dst, dkey, src, K, N):
        wdt = ld["w"]
        for kc in range(K // 128):
            for n0 in range(0, N, wdt):
                n1 = min(N, n0 + wdt)
                load_cast(dst[:, kc, n0:n1], src[kc * 128:(kc + 1) * 128, n0:n1], n1 - n0, dkey)

    for l in range(DEPTH):
        xsrc = x_in if l == 0 else xnext
        for d in ((0, 0) if DBG == "FF" else (0, 1)):
            if DBG == "Bonly" and d == 0:
                continue
            P.barrier()
            with ExitStack() as ss:
                sb = lambda name, shape, dt: P.sb("L%dD%d_%s_%d" % (l, d, name, len(P.ops)), shape, dt, ss)
                cst = mk_consts(sb, True)
                ident32, J32, identb, Jb = cst["ident32"], cst["J32"], cst["identb"], cst["Jb"]
                tri2f, tri2, ones32, negmf, negm = cst["tri2f"], cst["tri2"], cst["ones32"], cst["negmf"], cst["negm"]
                qmaskf, qmask, bm4, bm2, bm2b = cst["qmaskf"], cst["qmask"], cst["bm4"], cst["bm2"], cst["bm2b"]
                hsel, m2, flag, mskB, mskK = cst["hsel"], cst["m2"], cst["flag"], cst["mskB"], cst["mskK"]
                set_stg(sb, 1024)
                NB = 17 if d == 1 else 16
                fmb = fm_blocks(d)
                fmoff = np.concatenate([[0], np.cumsum([w for _, w in fmb])]).tolist()
                tmc = tm_cols(d)
                ntm = NTM[d]
                wfm = sb("wfm", [128, 8, NFM[d]], BF16)
                wtm = sb("wtm", [128, 8, ntm], BF16)
                load_w(wfm, "wfm", W["wfm", l, d], D, NFM[d])
                load_w(wtm, "wtm", W["wtm", l, d], D, ntm)
                gmix = sb("gmix", [128, 8], F32)
                P.dma("sync", gmix[:], W["gmix", l][:, :], [], ["gmix"])
                aupg = sb("aupg", [16, 128], BF16)
                load_cast(aupg[:], W["gla_aup", l, d][:, :], 128, "aupg", 16)
                abg = sb("abg", [1, 128], F32)
                P.dma("sync", abg[:], W["gla_ab", l, d][:, :], [], ["abg"])
                mu = sb("mu", [128, 9], F32)
                P.dma("sync", mu[:], W["mu", l, d][:, :], [], ["mu"])
                w0r = sb("w0r", [1, 256], F32)
                P.dma("sync", w0r[:], W["w0", l, d][:, :], [], ["w0r"])
                wup = sb("wup", [64, 256], BF16)
                load_cast(wup[:], W["wup", l, d][:, :], 256, "wup", 64)
                a0 = sb("a0", [128, 2], F32)
                P.dma("sync", a0[:], W["a0", l, d][:, :], [], ["a0"])
                aup = sb("aup", [64, 256], BF16)
                load_cast(aup[:], W["aup", l, d][:, :], 256, "aup", 64)
                dtb = sb("dtb", [128, 8], F32)
                P.dma("sync", dtb[:], W["dtb", l, d][:, :], [], ["dtb"])
                alog = sb("alog", [128, 8], F32)
                P.dma("sync", alog[:], W["alog", l, d][:, :], [], ["alog"])
                expA = sb("expA", [128, 8], F32)
                A(("activation", _a(out=expA[:], in_=alog[:], func=AF.Exp)), ["alog"], ["expA"])
                kk3 = sb("kk3", [128, 3, 2], F32)
                P.dma("sync", kk3[:], W["kk3", l].rearrange("p (a b) -> p a b", a=3), [], ["kk3"])
                omka = sb("omka", [128, 2], F32)
                V(("tensor_scalar", _a(out=omka[:], in0=kk3[:, 1, :], scalar1=-1.0, scalar2=1.0, op0=ALU.mult,
                                            op1=ALU.add)), ["kk3"], ["omka"])
                convw = sb("convw", [128, 8, 5], F32)
                P.dma("sync", convw[:], W["convw", l].rearrange("p (b j) -> p b j", b=8), [], ["convw"])
                convb = sb("convb", [128, 8], F32)
                P.dma("sync", convb[:], W["convb", l][:, :], [], ["convb"])
                dconv = sb("dconv", [128, 8, 5, 128], BF16)
                for b in range(8):
                    for j in range(5):
                        jj = j if d == 0 else 4 - j
                        V(("tensor_scalar", _a(out=dconv[:, b, j, :], in0=ident32[:],
                                                                      scalar1=convw[:, b, jj:jj + 1], scalar2=None,
                                                                      op0=ALU.mult)), ["ident32", "convw"], ["dconv"])
                omu = sb("omu", [128, 9], F32)
                hmu = sb("hmu", [128, 9], F32)
                V(("tensor_scalar", _a(out=omu[:], in0=mu[:], scalar1=-1.0, scalar2=1.0, op0=ALU.mult, op1=ALU.add)),
                  ["mu"], ["omu"])
                V(("tensor_scalar", _a(out=hmu[:], in0=mu[:], scalar1=0.5, scalar2=None, op0=ALU.mult)), ["mu"], ["hmu"])
                dsh = sb("dsh", [128, 9, 2, 128], BF16)
                for b in range(9):
                    V(("tensor_scalar", _a(out=dsh[:, b, 0, :], in0=ident32[:], scalar1=omu[:, b:b + 1],
                                                     scalar2=None, op0=ALU.mult)), ["ident32", "omu"], ["dsh"])
                    V(("tensor_scalar", _a(out=dsh[:, b, 1, :], in0=ident32[:], scalar1=hmu[:, b:b + 1],
                                                     scalar2=None, op0=ALU.mult)), ["ident32", "hmu"], ["dsh"])
                if d == 1:
                    glan = sb("glan", [128, 64], F32)
                    P.dma("sync", glan[:], W["gla_norm", l][:, :], [], ["glan"])
                    gup = sb("gup", [128, 256], BF16)
                    load_cast(gup[:], W["gup", l][:, :], 256, "gup")
                    lng = sb("lng", [128, 256], F32)
                    P.dma("sync", lng[:], W["lng", l][:, :], [], ["lng"])
                    lnb = sb("lnb", [128, 256], F32)
                    P.dma("sync", lnb[:], W["lnb", l][:, :], [], ["lnb"])
                    ssdD = sb("ssdD", [128, 8], F32)
                    P.dma("sync", ssdD[:], W["ssdD", l][:, :], [], ["ssdD"])
                    ssdn = sb("ssdn", [128, 512], F32)
                    P.dma("sync", ssdn[:], W["ssdn", l][:, :], [], ["ssdn"])
                xt = [sb("xt%d" % i, [128, D], F32) for i in range(2)]
                raw = [sb("raw%d" % i, [128, NB, 132], BF16) for i in range(2)]
                glaT = [sb("glaT%d" % i, [128, 3, 128], BF16) for i in range(2)]
                vtok = [sb("vtok%d" % i, [128, 256], BF16) for i in range(2)]
                dtt = [sb("dtt%d" % i, [128, 8], F32) for i in range(2)]
                if d == 1:
                    sgz = [sb("sgz%d" % i, [128, 768], F32) for i in range(3)]
                hn = sb("hn", [128, D], BF16)
                hT = sb("hT", [128, 8, 128], BF16)
                st8 = sb("st8", [128, 8], F32)
                dtmp = sb("dtmp", [128, 8], F32)
                S32 = sb("S32", [128, 256], F32)
                S16 = sb("S16", [128, 256], BF16)
                H32 = sb("H32", [128, 512], F32)
                H16 = sb("H16", [128, 512], BF16)
                R32 = sb("R32", [128, 2, 128], F32)
                R16 = sb("R16", [128, 2, 128], BF16)
                for t_, k_ in ((S32, "S32"), (S16, "S16"), (H32, "H32"), (H16, "H16"), (R32, "R32"), (R16, "R16")):
                    G(("memset", _a(t_[:], 0.0)), [], [k_])
                for i in range(2):
                    G(("memset", _a(raw[i][:], 0.0)), [], ["raw%d" % i])
                xbcT = sb("xbcT", [128, 8, 128], BF16)
                xsB = sb("xsB", [128, 768], BF16)
                dtA = sb("dtA", [128, 8], F32)
                ac = sb("ac", [128, 5, 8], F32)
                Ztf = sb("Zt", [128, 1024], F32)
                Zt = Ztf[:].rearrange("p (h t) -> p h t", h=8)
                seg = sb("seg", [128, 8, 128], BF16)
                cb = sb("cb", [128, 2, 128], BF16)
                MT = sb("MT", [128, 8, 128], BF16)
                xdt = sb("xdt", [128, 512], BF16)
                xdtw = sb("xdtw", [128, 512], BF16)
                ytmp = sb("ytmp", [128, 512], F32)
                ysum = sb("ysum", [128, 512], F32)
                sg = sb("sg", [128, 128], F32)
                nl = sb("nl", [128, 128], F32)
                epos = sb("epos", [128, 128], F32)
                eneg = sb("eneg", [128, 128], F32)
                qd = sb("qd", [128, 128], BF16)
                kd = sb("kd", [128, 128], BF16)
                kt = sb("kt", [128, 128], BF16)
                Qblkf = sb("Qblk", [128, 512], BF16)
                Qblk = Qblkf[:].rearrange("p (h t) -> p h t", h=4)
                Sm = sb("Sm", [128, 4, 128], BF16)
                kttok = sb("kttok", [128, 128], BF16)
                utmp = sb("utmp", [128, 256], F32)
                rkv = sb("rkv", [128, 6, 128], F32)
                twT = sb("twT", [64, 128], BF16)
                afT = sb("afT", [64, 128], BF16)
                sgd = sb("sgd", [128, 128], BF16)
                nlw = sb("nlw", [128, 256], F32)
                gin = sb("gin", [128, 2, 128], F32)
                gex = sb("gex", [128, 2, 128], F32)
                ginv = sb("ginv", [128, 2, 128], F32)
                alpha = sb("alpha", [128, 2, 128], F32)
                kkk = sb("kkk", [128, 2, 128], F32)
                sq = sb("sq", [128, 2, 128], BF16)
                rinv = sb("rinv", [128, 2, 128], F32)
                kkn = sb("kkn", [128, 2, 128], F32)
                t1 = sb("t1", [128, 2, 128], F32)
                kdir = sb("kdir", [128, 2, 128], F32)
                bvec = sb("bvec", [128, 2, 128], F32)
                RtT = sb("RtT", [128, 2, 128], BF16)
                AtT = sb("AtT", [128, 2, 128], BF16)
                BtT = sb("BtT", [128, 2, 128], BF16)
                KtT = sb("KtT", [128, 2, 128], BF16)
                fmT = sb("fmT", [128, 4, 2, 128], BF16)
                tok4 = sb("tok4", [128, 4, 256], BF16)
                RAbf = sb("RAb", [128, 2, 512], BF16)
                RAb = RAbf[:].rearrange("p b (h x t) -> p b h x t", h=2, x=2)
                SCB = sb("SCB", [128, 2, 2, 2, 128], BF16)
                SCK = sb("SCK", [128, 2, 2, 2, 128], BF16)
                Ak = [sb("Ak%d" % i, [128, 4, 128], BF16) for i in range(2)]
                Nk = [sb("Nk%d" % i, [128, 4, 128], BF16) for i in range(2)]
                T32 = sb("T32", [128, 4, 128], F32)
                T16f = sb("T16", [128, 512], BF16)
                T16 = T16f[:].rearrange("p (h t) -> p h t", h=4)
                WtT = sb("WtT", [128, 2, 128], BF16)
                X2 = sb("X2", [128, 256], BF16)
                U16 = sb("U16", [128, 256], BF16)
                rtmp = sb("rtmp", [128, 2, 128], F32)
                if d == 0:
                    osb = sb("osb", [128, 512], F32)
                if d == 1:
                    ofs = sb("ofs", [128, D], F32)
                    mix = sb("mix", [128, D], BF16)
                    fa = sb("fa", [128, 512], F32)
                    fb = sb("fb", [128, 512], F32)
                    st4 = sb("st4", [128, 16], F32)
                    prodT = sb("prodT", [128, 2, 128], BF16)

                identX = identb if (d == 0 or DBG3 == "noJ") else Jb
                if os.environ.get("KMEM"):
                    print("SBUF remaining after sweep allocs l=%d d=%d:" % (l, d), nc.sbuf_bytes_remaining)
                order = list(range(NCH)) if d == 0 else list(range(NCH - 1, -1, -1))

                def stageA(i):
                    ci = order[i]
                    s3, s2 = i % 2, i % 2
                    xk, rk_, gk, vk, dk = "xt%d" % s3, "raw%d" % s3, "glaT%d" % s2, "vtok%d" % s2, "dtt%d" % s2
                    P.dma("sync", xt[s3][:], xsrc[ci * C:(ci + 1) * C, :], [], [xk])
                    A(("activation", _a(out=hn[:], in_=xt[s3][:], func=AF.Square, accum_out=st8[:, 0:1])),
                      [xk], ["hn", "st8a"])
                    A(("activation", _a(out=st8[:, 1:2], in_=st8[:, 0:1], func=AF.Sqrt, scale=1.0 / D, bias=EPS)),
                      ["st8a"], ["st8b"])
                    V(("reciprocal", _a(out=st8[:, 2:3], in_=st8[:, 1:2])), ["st8b"], ["st8c"])
                    V(("tensor_scalar", _a(out=hn[:], in0=xt[s3][:], scalar1=st8[:, 2:3], scalar2=None, op0=ALU.mult)),
                      [xk, "st8c"], ["hn"])
                    b_, bk = bank()
                    for kc in range(8):
                        T(("transpose", _a(bf(b_)[:, kc * 128:(kc + 1) * 128], hn[:, kc * 128:(kc + 1) * 128],
                                                       identX[:])), ["hn", "identb", "Jb"], [bk])
                    V(("tensor_tensor", _a(out=hT[:], in0=bf(b_).rearrange("p (a b) -> p a b", a=8),
                                                in1=gmix[:].unsqueeze(2).to_broadcast([128, 8, 128]), op=ALU.mult)),
                      [bk, "gmix"], ["hT"])
                    yield
                    nblk = len(fmb)
                    if DBG3 == "noFM4":
                        nblk = 16
                    if DBG3 == "noFM":
                        nblk = 0
                    fgroups = [list(range(g0, min(NB, g0 + 4))) for g0 in range(0, NB, 4)] + [list(range(NB, len(fmb)))]
                    if nblk < len(fmb):
                        fgroups = [g_ for g_ in fgroups if g_ and g_[-1] < nblk]
                    for grp in fgroups:
                        b_, bk = bank()
                        for j, bi in enumerate(grp):
                            wdt = fmb[bi][1]
                            for kc in range(8):
                                mm(b_[0:wdt, j * 128:(j + 1) * 128], wfm[:, kc, fmoff[bi]:fmoff[bi] + wdt], hT[:, kc, :],
                                   kc == 0, kc == 7, ["wfm", "hT"], [bk])
                        for j, bi in enumerate(grp):
                            wdt = fmb[bi][1]
                            if bi < NB:
                                A(("copy", _a(out=raw[s3][0:wdt, bi, 2:130],
                                                                        in_=b_[0:wdt, j * 128:(j + 1) * 128])), [bk], [rk_])
                            else:
                                V(("tensor_copy", _a(out=glaT[s2][0:wdt, bi - NB, :],
                                                                                in_=b_[0:wdt, j * 128:(j + 1) * 128])),
                                  [bk], [gk])
                        yield
                    off = 0
                    groups = [[(0, 256), (256 if d == 0 else 1024, 8)]] if d == 0 else \
                        [[(0, 256), (256, 256)], [(512, 512)], [(1024, 8)]]
                    if DBG3 == "noTM":
                        groups = groups[:1]
                    if DBG3 == "noTM0":
                        groups = []
                    for grp in groups:
                        b_, bk = bank()
                        o = 0
                        for (c0, wdt) in grp:
                            for kc in range(8):
                                mm(b_[:, o:o + wdt], hT[:, kc, :], wtm[:, kc, c0:c0 + wdt], kc == 0, kc == 7, ["hT", "wtm"], [bk])
                            if c0 == 0:
                                V(("tensor_copy", _a(out=vtok[s2][:], in_=b_[:, o:o + 256])), [bk], [vk])
                            elif wdt == 8:
                                V(("tensor_tensor", _a(out=dtmp[:], in0=b_[:, o:o + 8], in1=dtb[:], op=ALU.add)),
                                  [bk, "dtb"], ["dtmp"])
                                A(("activation", _a(out=dtmp[:], in_=dtmp[:], func=AF.Exp)), ["dtmp"], ["dtmp"])
                                A(("activation", _a(out=dtt[s2][:], in_=dtmp[:], func=AF.Ln, bias=1.0)), ["dtmp"], [dk])
                            elif c0 == 256:
                                A(("activation", _a(out=sgz[i % 3][:, 0:256], in_=b_[:, o:o + 256], func=AF.Silu)),
                                  [bk], ["sgz%d" % (i % 3)])
                            else:
                                A(("activation", _a(out=sgz[i % 3][:, 256:768], in_=b_[:, o:o + 512], func=AF.Silu)),
                                  [bk], ["sgz%d" % (i % 3)])
                            o += wdt
                        yield
                    if i > 0:
                        p3 = (i - 1) % 2
                        pk = "raw%d" % p3
                        if i % CPS == 0:
                            G(("tensor_scalar", _a(out=raw[s3][:, :, 0:2], in0=raw[p3][:, :, 128:130], scalar1=flag[:, 0:1],
                                                        scalar2=None, op0=ALU.mult)), [pk, rk_, "flag"], [rk_ + "h"])
                            G(("tensor_scalar", _a(out=raw[p3][:, :, 130:132], in0=raw[s3][:, :, 2:4], scalar1=flag[:, 0:1],
                                                        scalar2=None, op0=ALU.mult)), [pk, rk_, "flag"], [pk + "h"])
                        else:
                            G(("tensor_copy", _a(out=raw[s3][:, :, 0:2], in_=raw[p3][:, :, 128:130])), [pk, rk_], [rk_ + "h"])
                            G(("tensor_copy", _a(out=raw[p3][:, :, 130:132], in_=raw[s3][:, :, 2:4])), [pk, rk_], [pk + "h"])
                    else:
                        G(("memset", _a(raw[s3][:, :, 0:2], 0.0)), [rk_], [rk_ + "h"])
                    if i == NCH - 1:
                        G(("memset", _a(raw[s3][:, :, 130:132], 0.0)), [rk_], [rk_ + "h"])

                def stageB(i):
                    ci = order[i]
                    s3, s2 = i % 2, i % 2
                    xk, rk_, gk, vk, dk = "xt%d" % s3, "raw%d" % s3, "glaT%d" % s2, "vtok%d" % s2, "dtt%d" % s2
                    rw = raw[s3]
                    rr = [rk_, rk_ + "h"]
                    boundary = (i % CPS == CPS - 1)
                    useJ = (d == 1) and DBG not in ("B1noJ",)
                    if d == 1 and DBG != "B1noDMA":
                        P.dma("gpsimd", ofs[:], ofd[ci * C:(ci + 1) * C, :], [], ["ofs"])
                    bo_, bok = banks[7], "bank7"
                    shared = {"gla_o": False}
                    def ssd_gen():
                        for g0 in (0, 4):
                            if g0:
                                yield
                            b_, bk = bank()
                            for j in range(4):
                                b = g0 + j
                                for tp in range(5):
                                    mm(b_[:, j * 128:(j + 1) * 128], dconv[:, b, tp, :], rw[:, b, tp:tp + 128], tp == 0, tp == 4,
                                       ["dconv"] + rr, [bk])
                            for j in range(4):
                                b = g0 + j
                                A(("activation", _a(out=xbcT[:, b, :], in_=b_[:, j * 128:(j + 1) * 128], func=AF.Silu,
                                                                   bias=convb[:, b:b + 1])), [bk, "convb"], ["xbcT"])
                        yield
                        b_, bk = bank()
                        for b in range(6):
                            T(("transpose", _a(bf(b_)[:, b * 128:(b + 1) * 128], xbcT[:, b, :], identb[:])), ["xbcT", "identb"], [bk])
                        V(("tensor_copy", _a(out=xsB[:], in_=bf(b_)[:, 0:768])), [bk], ["xsB"])
                        V(("scalar_tensor_tensor", _a(out=dtA[:], in0=dtt[s2][:], scalar=-1.0, in1=expA[:], op0=ALU.mult, op1=ALU.mult)),
                          [dk, "expA"], ["dtA"])
                        yield
                        b_, bk = bank()
                        mm(b_[:, 0:8], tri2[:, 0, :], dtA[:], True, True, ["tri2", "dtA"], [bk])
                        mm(b_[:, 8:16], ones32[:], dtA[:], True, True, ["ones32", "dtA"], [bk])
                        V(("tensor_copy", _a(out=ac[:, 0, :], in_=b_[:, 0:8])), [bk], ["ac0"])
                        V(("tensor_scalar", _a(out=ac[:, 1, :], in0=b_[:, 0:8], scalar1=-1.0, scalar2=None, op0=ALU.mult)), [bk], ["ac1"])
                        A(("activation", _a(out=ac[:, 2, :], in_=b_[:, 0:8], func=AF.Exp)), [bk], ["ac2"])
                        A(("activation", _a(out=ac[:, 4, :], in_=b_[:, 8:16], func=AF.Exp)), [bk], ["ac4"])
                        V(("tensor_tensor", _a(out=ac[:, 3, :], in0=b_[:, 8:16], in1=ac[:, 0, :], op=ALU.subtract)), [bk, "ac0"], ["ac3"])
                        A(("activation", _a(out=ac[:, 3, :], in_=ac[:, 3, :], func=AF.Exp)), ["ac3"], ["ac3"])
                        V(("tensor_tensor", _a(out=ac[:, 3, :], in0=ac[:, 3, :], in1=dtt[s2][:], op=ALU.mult)), ["ac3", dk], ["ac3"])
                        G(("tensor_tensor", _a(out=Zt, in0=tri2[:, 0:1, :].to_broadcast([128, 8, 128]),
                                                    in1=dtA[:].unsqueeze(2).to_broadcast([128, 8, 128]), op=ALU.mult)),
                          ["tri2", "dtA"], ["Zt"])
                        yield
                        for hg in range(2):
                            if hg:
                                yield
                            b_, bk = bank()
                            mm(b_[:, :], ones32[:], Ztf[:, 512 * hg:512 * hg + 512], True, False, ["ones32", "Zt"], [bk])
                            mm(b_[:, :], identb[:], negmf[d][:], False, True, ["identb", "negm%d" % d], [bk])
                            for hh in range(4):
                                h = 4 * hg + hh
                                A(("activation", _a(out=seg[:, h, :], in_=b_[:, hh * 128:(hh + 1) * 128], func=AF.Exp,
                                                                          bias=ac[:, 1, h:h + 1])), [bk, "ac1"], ["seg"])
                        yield
                        b_, bk = bank()
                        for g in range(2):
                            mm(b_[:, g * 128:(g + 1) * 128], xbcT[:, 4 + g, :], xbcT[:, 6 + g, :], True, True, ["xbcT"], [bk])
                        V(("tensor_copy", _a(out=cb[:], in_=b_[:, 0:256])), [bk], ["cb"])
                        V(("tensor_tensor", _a(out=MT[:].rearrange("p (g r) t -> p g r t", g=2),
                                                    in0=seg[:].rearrange("p (g r) t -> p g r t", g=2),
                                                    in1=cb[:].unsqueeze(2).to_broadcast([128, 2, 4, 128]), op=ALU.mult)),
                          ["seg", "cb"], ["MT"])
                        V(("tensor_tensor", _a(out=xdt[:].rearrange("p (h q) -> p h q", h=8),
                                                    in0=xsB[:, 0:512].rearrange("p (h q) -> p h q", h=8),
                                                    in1=dtt[s2][:].unsqueeze(2).to_broadcast([128, 8, 64]), op=ALU.mult)),
                          ["xsB", dk], ["xdt"])
                        G(("tensor_tensor", _a(out=xdtw[:].rearrange("p (h q) -> p h q", h=8),
                                                    in0=xsB[:, 0:512].rearrange("p (h q) -> p h q", h=8),
                                                    in1=ac[:, 3, :].unsqueeze(2).to_broadcast([128, 8, 64]), op=ALU.mult)),
                          ["xsB", "ac3"], ["xdtw"])
                        yield
                        by_, byk = bank()
                        if useJ:
                            mm(by_[:, :], J32[:], ofs[:, 512:1024], True, False, ["J32", "ofs"], [byk])
                        for h in range(8):
                            mm(by_[:, 64 * h:64 * h + 64], MT[:, h, :], xdt[:, 64 * h:64 * h + 64], (not useJ) and h == 0, h == 7, ["MT", "xdt"], [byk])
                        b_, bk = bank()
                        for g in range(2):
                            mm(b_[:, 256 * g:256 * g + 256], xbcT[:, 6 + g, :], H16[:, 256 * g:256 * g + 256], True, True, ["xbcT", "H16"], [bk])
                        V(("tensor_tensor", _a(out=ytmp[:].rearrange("p (h q) -> p h q", h=8),
                                                    in0=b_[:, :].rearrange("p (h q) -> p h q", h=8),
                                                    in1=ac[:, 2, :].unsqueeze(2).to_broadcast([128, 8, 64]), op=ALU.mult)),
                          [bk, "ac2"], ["ytmp"])
                        V(("tensor_tensor", _a(out=ysum[:], in0=by_[:, :], in1=ytmp[:], op=ALU.add)), [byk, "ytmp"], ["ysum"])
                        yield
                        b_, bk = bank()
                        for g in range(2):
                            mm(b_[:, 256 * g:256 * g + 256], xsB[:, 512 + 128 * g:512 + 128 * g + 128], xdtw[:, 256 * g:256 * g + 256], True, True,
                               ["xsB", "xdtw"], [bk])
                        G(("tensor_tensor", _a(out=ytmp[:].rearrange("p (h q) -> p h q", h=8),
                                                    in0=H32[:].rearrange("p (h q) -> p h q", h=8),
                                                    in1=ac[:, 4, :].unsqueeze(2).to_broadcast([128, 8, 64]), op=ALU.mult)),
                          ["H32", "ac4", "ytmp"], ["ytmp"])
                        V(("tensor_tensor", _a(out=H32[:], in0=b_[:, :], in1=ytmp[:], op=ALU.add)), [bk, "ytmp"], ["H32"])
                        if boundary:
                            V(("tensor_scalar", _a(out=H32[:], in0=H32[:], scalar1=flag[:, 0:1], scalar2=None, op0=ALU.mult)),
                              ["H32", "flag"], ["H32"])
                        A(("copy", _a(out=H16[:], in_=H32[:])), ["H32"], ["H16"])
                        if d == 0:
                            P.dma("gpsimd", ofd[ci * C:(ci + 1) * C, 512:1024], ysum[:], ["ysum"], [])


                    def gla_gen():
                        gT = glaT[s2]
                        b_, bk = bank()
                        mm(b_[:, 0:128], gT[0:16, 2, :], aupg[:], True, False, [gk, "aupg"], [bk])
                        mm(b_[:, 0:128], ones32[0:1, :], abg[:], False, True, ["ones32", "abg"], [bk])
                        A(("activation", _a(out=sg[:], in_=b_[:, 0:128], func=AF.Sigmoid)), [bk], ["sg"])
                        A(("activation", _a(out=nl[:], in_=sg[:], func=AF.Ln)), ["sg"], ["nl"])
                        yield
                        b_, bk = bank()
                        mm(b_[:, 0:128], nl[:], tri2[:, 0, :], True, True, ["nl", "tri2"], [bk])
                        A(("activation", _a(out=epos[:], in_=b_[:, 0:128], func=AF.Exp, scale=1.0 / 16)), [bk], ["epos"])
                        A(("activation", _a(out=eneg[:], in_=b_[:, 0:128], func=AF.Exp, scale=-1.0 / 16)), [bk], ["eneg"])
                        V(("scalar_tensor_tensor", _a(out=qd[:], in0=gT[:, 0, :], scalar=32.0 ** -0.5, in1=epos[:], op0=ALU.mult, op1=ALU.mult)),
                          [gk, "epos"], ["qd"])
                        V(("tensor_tensor", _a(out=kd[:], in0=gT[:, 1, :], in1=eneg[:], op=ALU.mult)), [gk, "eneg"], ["kd"])
                        G(("tensor_scalar", _a(out=kt[:], in0=kd[:], scalar1=epos[:, 127:128], scalar2=None, op0=ALU.mult)), ["kd", "epos"], ["kt"])
                        G(("tensor_tensor", _a(out=Qblk, in0=qd[:].unsqueeze(1).to_broadcast([128, 4, 128]), in1=qmask, op=ALU.mult)),
                          ["qd", "qmask"], ["Qblk"])
                        yield
                        b_, bk = bank()
                        mm(b_[:, :], kd[:], Qblkf[:], True, True, ["kd", "Qblk"], [bk])
                        V(("tensor_tensor", _a(out=Sm[:], in0=b_[:, :].rearrange("p (h t) -> p h t", h=4),
                                                    in1=tri2[:, d:d + 1, :].to_broadcast([128, 4, 128]), op=ALU.mult)), [bk, "tri2"], ["Sm"])
                        yield
                        b_, bk = bank()
                        T(("transpose", _a(bf(b_)[:, 0:128], kt[:], identb[:])), ["kt", "identb"], [bk])
                        A(("copy", _a(out=kttok[:], in_=bf(b_)[:, 0:128])), [bk], ["kttok"])
                        yield
                        if useJ:
                            mm(bo_[:, :], J32[:], ofs[:, 0:512], True, False, ["J32", "ofs"], [bok])
                        mm(bo_[:, 0:256], qd[:], S16[:], not useJ, False, ["qd", "S16"], [bok])
                        for h in range(4):
                            mm(bo_[:, 64 * h:64 * h + 64], Sm[:, h, :], vtok[s2][:, 64 * h:64 * h + 64], False, False, ["Sm", vk], [bok])
                        yield
                        shared["gla_o"] = True
                        b_, bk = bank()
                        mm(b_[:, 0:256], kttok[:], vtok[s2][:], True, True, ["kttok", vk], [bk])
                        V(("tensor_tensor", _a(out=utmp[:], in0=b_[:, 0:256], in1=bm4[:], op=ALU.mult)), [bk, "bm4"], ["utmp"])
                        V(("scalar_tensor_tensor", _a(out=S32[:], in0=S32[:], scalar=epos[:, 127:128], in1=utmp[:], op0=ALU.mult, op1=ALU.add)),
                          ["S32", "epos", "utmp"], ["S32"])
                        if boundary:
                            V(("tensor_scalar", _a(out=S32[:], in0=S32[:], scalar1=flag[:, 0:1], scalar2=None, op0=ALU.mult)),
                              ["S32", "flag"], ["S32"])
                        G(("tensor_copy", _a(out=S16[:], in_=S32[:])), ["S32"], ["S16"])


                    def rwkv_gen():
                        nrb = 9 if d == 1 else 8
                        bsh = []
                        for g0 in range(0, nrb, 4):
                            b_, bk = bank()
                            bsh.append((b_, bk))
                            for j in range(min(4, nrb - g0)):
                                bb = g0 + j
                                m = 64 if bb in (6, 7) else 128
                                rb = 8 + bb
                                mm(b_[0:m, j * 128:(j + 1) * 128], dsh[0:m, bb, 0, 0:m], rw[0:m, rb, 2:130], True, False, ["dsh"] + rr, [bk])
                                mm(b_[0:m, j * 128:(j + 1) * 128], dsh[0:m, bb, 1, 0:m], rw[0:m, rb, 1:129], False, False, ["dsh"] + rr, [bk])
                                mm(b_[0:m, j * 128:(j + 1) * 128], dsh[0:m, bb, 1, 0:m], rw[0:m, rb, 3:131], False, True, ["dsh"] + rr, [bk])
                        A(("copy", _a(out=rkv[:, 0:4, :], in_=bsh[0][0][:, :].rearrange("p (a t) -> p a t", a=4))), [bsh[0][1]], ["rkv"])
                        V(("tensor_copy", _a(out=rkv[:, 4:6, :], in_=bsh[1][0][:, 0:256].rearrange("p (a t) -> p a t", a=2))), [bsh[1][1]], ["rkv"])
                        A(("activation", _a(out=twT[:], in_=bsh[1][0][0:64, 256:384], func=AF.Tanh)), [bsh[1][1]], ["twT"])
                        V(("tensor_copy", _a(out=afT[:], in_=bsh[1][0][0:64, 384:512])), [bsh[1][1]], ["afT"])
                        if d == 1:
                            A(("activation", _a(out=sgd[:], in_=bsh[2][0][:, 0:128], func=AF.Sigmoid)), [bsh[2][1]], ["sgd"])
                        yield
                        b_, bk = bank()
                        mm(b_[:, 0:256], twT[:], wup[:], True, False, ["twT", "wup"], [bk])
                        mm(b_[:, 0:256], ones32[0:1, :], w0r[:], False, True, ["ones32", "w0r"], [bk])
                        A(("activation", _a(out=nlw[:], in_=b_[:, 0:256], func=AF.Sigmoid)), [bk], ["nlw"])
                        yield
                        b_, bk = bank()
                        for bb in range(2):
                            mm(b_[:, 256 * bb:256 * bb + 256], nlw[:, 128 * bb:128 * bb + 128], tri2f[:], True, True, ["nlw", "tri2"], [bk])
                        cw = b_[:, :].rearrange("p (b x t) -> p b x t", b=2, x=2)
                        LW = 0.6065306597126334
                        A(("activation", _a(out=gin[:], in_=cw[:, :, 0, :], func=AF.Exp, scale=-LW)), [bk], ["gin"])
                        A(("activation", _a(out=gex[:], in_=cw[:, :, 1, :], func=AF.Exp, scale=-LW)), [bk], ["gex"])
                        A(("activation", _a(out=ginv[:], in_=cw[:, :, 0, :], func=AF.Exp, scale=LW)), [bk], ["ginv"])
                        yield
                        b_, bk = bank()
                        for bb in range(2):
                            mm(b_[:, 128 * bb:128 * bb + 128], aup[:, 128 * bb:128 * bb + 128], afT[:], True, True, ["aup", "afT"], [bk])
                        for bb in range(2):
                            A(("activation", _a(out=alpha[:, bb, :], in_=b_[:, 128 * bb:128 * bb + 128], func=AF.Sigmoid,
                                                                 bias=a0[:, bb:bb + 1])), [bk, "a0"], ["alpha"])
                        V(("tensor_tensor", _a(out=kkk[:], in0=rkv[:, 2:4, :], in1=kk3[:, 0, :].unsqueeze(2).to_broadcast([128, 2, 128]), op=ALU.mult)),
                          ["rkv", "kk3"], ["kkk"])
                        G(("tensor_tensor", _a(out=sq[:], in0=kkk[:], in1=kkk[:], op=ALU.mult)), ["kkk"], ["sq"])
                        yield
                        b_, bk = bank()
                        for bb in range(2):
                            mm(b_[:, 128 * bb:128 * bb + 128], bm2b[:], sq[:, bb, :], True, True, ["bm2b", "sq"], [bk])
                        A(("activation", _a(out=rinv[:], in_=b_[:, 0:256].rearrange("p (b t) -> p b t", b=2), func=AF.Sqrt, bias=1e-12)),
                          [bk], ["rinv"])
                        V(("reciprocal", _a(out=rinv[:], in_=rinv[:])), ["rinv"], ["rinv"])
                        V(("tensor_tensor", _a(out=kkn[:], in0=kkk[:], in1=rinv[:], op=ALU.mult)), ["kkk", "rinv"], ["kkn"])
                        G(("tensor_tensor", _a(out=t1[:], in0=alpha[:], in1=kk3[:, 1, :].unsqueeze(2).to_broadcast([128, 2, 128]), op=ALU.mult)),
                          ["alpha", "kk3"], ["t1"])
                        G(("tensor_tensor", _a(out=t1[:], in0=t1[:], in1=omka[:].unsqueeze(2).to_broadcast([128, 2, 128]), op=ALU.add)),
                          ["t1", "omka"], ["t1"])
                        G(("tensor_tensor", _a(out=kdir[:], in0=rkv[:, 2:4, :], in1=t1[:], op=ALU.mult)), ["rkv", "t1"], ["kdir"])
                        V(("tensor_tensor", _a(out=bvec[:], in0=kkn[:], in1=alpha[:], op=ALU.mult)), ["kkn", "alpha"], ["bvec"])
                        V(("tensor_tensor", _a(out=RtT[:], in0=rkv[:, 0:2, :], in1=gin[:], op=ALU.mult)), ["rkv", "gin"], ["RtT"])
                        V(("scalar_tensor_tensor", _a(out=AtT[:], in0=kkn[:], scalar=-1.0, in1=gex[:], op0=ALU.mult, op1=ALU.mult)),
                          ["kkn", "gex"], ["AtT"])
                        V(("tensor_tensor", _a(out=BtT[:], in0=bvec[:], in1=ginv[:], op=ALU.mult)), ["bvec", "ginv"], ["BtT"])
                        G(("tensor_tensor", _a(out=KtT[:], in0=kdir[:], in1=ginv[:], op=ALU.mult)), ["kdir", "ginv"], ["KtT"])
                        G(("tensor_copy", _a(out=fmT[:, 0, :, :], in_=AtT[:])), ["AtT"], ["fmT0"])
                        V(("tensor_tensor", _a(out=fmT[:, 1, :, :], in0=BtT[:], in1=gin[:, :, 127:128].to_broadcast([128, 2, 128]), op=ALU.mult)),
                          ["BtT", "gin"], ["fmT1"])
                        G(("tensor_tensor", _a(out=fmT[:, 2, :, :], in0=KtT[:], in1=gin[:, :, 127:128].to_broadcast([128, 2, 128]), op=ALU.mult)),
                          ["KtT", "gin"], ["fmT2"])
                        G(("tensor_copy", _a(out=fmT[:, 3, :, :], in_=rkv[:, 4:6, :])), ["rkv"], ["fmT3"])
                        yield
                        b_, bk = bank()
                        for a in range(4):
                            for bb in range(2):
                                T(("transpose", _a(bf(b_)[:, (2 * a + bb) * 128:(2 * a + bb + 1) * 128], fmT[:, a, bb, :], identb[:])),
                                  ["fmT%d" % a, "identb"], [bk])
                        V(("tensor_copy", _a(out=tok4[:].rearrange("p a c -> p (a c)"), in_=bf(b_)[:, :])), [bk], ["tok4"])
                        yield
                        for bb in range(2):
                            G(("tensor_tensor", _a(out=RAb[:, bb, :, 0, :], in0=AtT[:, bb:bb + 1, :].to_broadcast([128, 2, 128]),
                                                               in1=m2[:].unsqueeze(2).to_broadcast([128, 2, 128]), op=ALU.mult)), ["AtT", "m2"], ["RAb"])
                            V(("tensor_tensor", _a(out=RAb[:, bb, :, 1, :], in0=RtT[:, bb:bb + 1, :].to_broadcast([128, 2, 128]),
                                                               in1=m2[:].unsqueeze(2).to_broadcast([128, 2, 128]), op=ALU.mult)), ["RtT", "m2"], ["RAb"])
                        for bb in range(2):
                            yield
                            b_, bk = bank()
                            mm(b_[:, :], BtT[:, bb, :], RAbf[:, bb, :], True, True, ["BtT", "RAb"], [bk])
                            V(("tensor_tensor", _a(out=SCB[:, bb, :, :, :], in0=b_[:, :].rearrange("p (h x t) -> p h x t", h=2, x=2),
                                                                    in1=mskB[:], op=ALU.mult)), [bk, "mskB"], ["SCB"])
                            b_, bk = bank()
                            mm(b_[:, :], KtT[:, bb, :], RAbf[:, bb, :], True, True, ["KtT", "RAb"], [bk])
                            V(("tensor_tensor", _a(out=SCK[:, bb, :, :, :], in0=b_[:, :].rearrange("p (h x t) -> p h x t", h=2, x=2),
                                                                    in1=mskK[d][:], op=ALU.mult)), [bk, "mskK%d" % d], ["SCK"])
                        yield
                        N0v = SCB[:].rearrange("p b h x t -> p (b h) x t")[:, :, 0, :]
                        G(("tensor_copy", _a(out=Nk[0][:], in_=N0v)), ["SCB"], ["Nk0"])
                        b_, bk = bank()
                        for h in range(4):
                            T(("transpose", _a(bf(b_)[:, h * 128:(h + 1) * 128], Nk[0][:, h, :], identb[:])), ["Nk0", "identb"], [bk])
                        A(("copy", _a(out=Ak[0][:], in_=bf(b_)[:, 0:512].rearrange("p (h t) -> p h t", h=4))), [bk], ["Ak0"])
                        V(("tensor_tensor", _a(out=T32[:], in0=Nk[0][:], in1=ident32[:].unsqueeze(1).to_broadcast([128, 4, 128]), op=ALU.add)),
                          ["Nk0", "ident32"], ["T32"])
                        G(("tensor_copy", _a(out=T16, in_=T32[:])), ["T32"], ["T16"])
                        yield
                        for lev in range(6):
                            if lev:
                                yield
                            c_, n_ = lev % 2, (lev + 1) % 2
                            ba_, bak = bank()
                            bn_, bnk = bank()
                            for h in range(4):
                                mm(ba_[:, h * 128:(h + 1) * 128], Nk[c_][:, h, :], Ak[c_][:, h, :], True, True, ["Nk%d" % c_, "Ak%d" % c_], [bak])
                            for h in range(4):
                                mm(bn_[:, h * 128:(h + 1) * 128], Ak[c_][:, h, :], Nk[c_][:, h, :], True, True, ["Nk%d" % c_, "Ak%d" % c_], [bnk])
                            A(("copy", _a(out=Ak[n_][:], in_=ba_[:, :].rearrange("p (h t) -> p h t", h=4))), [bak], ["Ak%d" % n_])
                            if lev < 5:
                                V(("tensor_copy", _a(out=Nk[n_][:], in_=bn_[:, :].rearrange("p (h t) -> p h t", h=4))), [bnk], ["Nk%d" % n_])
                            bt_, btk = bank()
                            for h in range(4):
                                mm(bt_[:, h * 128:(h + 1) * 128], Ak[n_][:, h, :], T16[:, h, :], True, True, ["Ak%d" % n_, "T16"], [btk])
                            V(("tensor_tensor", _a(out=T32[:], in0=bt_[:, :].rearrange("p (h t) -> p h t", h=4), in1=T32[:], op=ALU.add)),
                              [btk, "T32"], ["T32"])
                            G(("tensor_copy", _a(out=T16, in_=T32[:])), ["T32"], ["T16"])
                        yield
                        for bb in range(2):
                            b_, bk = bank()
                            mm(b_[:, 0:256], tok4[:, 0, 128 * bb:128 * bb + 128], T16f[:, 256 * bb:256 * bb + 256], True, True, ["tok4", "T16"], [bk])
                            V(("tensor_copy", _a(out=WtT[0:64, bb, :], in_=b_[0:64, 0:128])), [bk], ["WtT"])
                            A(("copy", _a(out=WtT[64:128, bb, :], in_=b_[64:128, 128:256])), [bk], ["WtT"])
                        yield
                        b_, bk = bank()
                        for h in range(4):
                            mm(b_[:, 64 * h:64 * h + 64], SCK[:, h // 2, h % 2, 0, :], tok4[:, 3, 64 * h:64 * h + 64], True, True, ["SCK", "tok4"], [bk])
                        V(("tensor_copy", _a(out=X2[:], in_=b_[:, 0:256])), [bk], ["X2"])
                        yield
                        b_, bk = bank()
                        for h in range(4):
                            mm(b_[:, 64 * h:64 * h + 64], T16[:, h, :], X2[:, 64 * h:64 * h + 64], h == 0, False, ["T16", "X2"], [bk])
                        for bb in range(2):
                            mm(b_[:, 128 * bb:128 * bb + 128], WtT[:, bb, :], R16[:, bb, :], False, bb == 1, ["WtT", "R16"], [bk])
                        A(("copy", _a(out=U16[:], in_=b_[:, 0:256])), [bk], ["U16"])
                        yield
                        while not shared["gla_o"]:
                            yield
                        for bb in range(2):
                            mm(bo_[:, 256 + 128 * bb:256 + 128 * bb + 128], RtT[:, bb, :], R16[:, bb, :], False, False, ["RtT", "R16"], [bok])
                        for h in range(4):
                            mm(bo_[:, 256 + 64 * h:256 + 64 * h + 64], SCB[:, h // 2, h % 2, 1, :], U16[:, 64 * h:64 * h + 64], False, False, ["SCB", "U16"], [bok])
                            mm(bo_[:, 256 + 64 * h:256 + 64 * h + 64], SCK[:, h // 2, h % 2, 1, :], tok4[:, 3, 64 * h:64 * h + 64], False, h == 3, ["SCK", "tok4"], [bok])
                        yield
                        b_, bk = bank()
                        for bb in range(2):
                            mm(b_[:, 128 * bb:128 * bb + 128], tok4[:, 1, 128 * bb:128 * bb + 128], U16[:, 128 * bb:128 * bb + 128], True, False, ["tok4", "U16"], [bk])
                            mm(b_[:, 128 * bb:128 * bb + 128], tok4[:, 2, 128 * bb:128 * bb + 128], tok4[:, 3, 128 * bb:128 * bb + 128], False, True, ["tok4"], [bk])
                        V(("tensor_tensor", _a(out=rtmp[:], in0=b_[:, 0:256].rearrange("p (b t) -> p b t", b=2),
                                                          in1=bm2[:].unsqueeze(1).to_broadcast([128, 2, 128]), op=ALU.mult)), [bk, "bm2"], ["rtmp"])
                        G(("tensor_tensor", _a(out=R32[:], in0=R32[:], in1=gin[:, :, 127:128].to_broadcast([128, 2, 128]), op=ALU.mult)),
                          ["R32", "gin"], ["R32"])
                        G(("tensor_tensor", _a(out=R32[:], in0=R32[:], in1=rtmp[:], op=ALU.add)), ["R32", "rtmp"], ["R32"])
                        if boundary:
                            G(("tensor_scalar", _a(out=R32[:], in0=R32[:], scalar1=flag[:, 0:1], scalar2=None, op0=ALU.mult)),
                              ["R32", "flag"], ["R32"])
                        G(("tensor_copy", _a(out=R16[:], in_=R32[:])), ["R32"], ["R16"])


                    KIL = os.environ.get("KIL", "sr")
                    if KIL == "seq":
                        plan = [[ssd_gen()], [gla_gen()], [rwkv_gen()]]
                    elif KIL == "seq2":
                        plan = [[rwkv_gen()], [ssd_gen()], [gla_gen()]]
                    elif KIL == "sg":
                        plan = [[ssd_gen(), gla_gen()], [rwkv_gen()]]
                    elif KIL == "sr":
                        plan = [[gla_gen()], [rwkv_gen(), ssd_gen()]]
                    else:
                        plan = [[rwkv_gen(), ssd_gen(), gla_gen()]]
                    for gens in plan:
                        while gens:
                            for g_ in list(gens):
                                try:
                                    next(g_)
                                except StopIteration:
                                    gens.remove(g_)
                    yield "mixdone"
                    if d == 0:
                        V(("tensor_copy", _a(out=osb[:], in_=bo_[:, :])), [bok], ["osb"])
                        P.dma("gpsimd", ofd[ci * C:(ci + 1) * C, 0:512], osb[:], ["osb"], [])
                        return
                    if DBG in ("B1", "B1noJ", "B1noDMA"):
                        return
                    A(("copy", _a(out=fa[:, 0:256], in_=bo_[:, 0:256])), [bok], ["fa"])
                    G(("tensor_tensor", _a(out=fb[:, 0:256], in0=fa[:, 0:256], in1=fa[:, 0:256], op=ALU.mult)), ["fa"], ["fb"])
                    V(("tensor_reduce", _a(out=st4[:, 0:4], in_=fb[:, 0:256].rearrange("p (h q) -> p h q", h=4), axis=AX.X, op=ALU.add)),
                      ["fb"], ["st4a"])
                    A(("activation", _a(out=st4[:, 0:4], in_=st4[:, 0:4], func=AF.Sqrt, scale=1.0 / 64, bias=EPS)), ["st4a"], ["st4a"])
                    V(("reciprocal", _a(out=st4[:, 0:4], in_=st4[:, 0:4])), ["st4a"], ["st4a"])
                    V(("tensor_tensor", _a(out=fa[:, 0:256].rearrange("p (h q) -> p h q", h=4), in0=fa[:, 0:256].rearrange("p (h q) -> p h q", h=4),
                                                in1=st4[:, 0:4].unsqueeze(2).to_broadcast([128, 4, 64]), op=ALU.mult)), ["fa", "st4a"], ["fa"])
                    V(("tensor_tensor", _a(out=fa[:, 0:256].rearrange("p (h q) -> p h q", h=4), in0=fa[:, 0:256].rearrange("p (h q) -> p h q", h=4),
                                                in1=glan[:].unsqueeze(1).to_broadcast([128, 4, 64]), op=ALU.mult)), ["fa", "glan"], ["fa"])
                    V(("tensor_tensor", _a(out=mix[:, 0:256], in0=fa[:, 0:256], in1=sgz[i % 3][:, 0:256], op=ALU.mult)), ["fa", "sgz%d" % (i % 3)], ["mixg"])
                    yield
                    ya = fa[:, 256:512]
                    yb = fb[:, 256:512]
                    A(("copy", _a(out=ya, in_=bo_[:, 256:512])), [bok], ["ya"])
                    V(("tensor_reduce", _a(out=st4[:, 4:8], in_=ya.rearrange("p (h q) -> p h q", h=4), axis=AX.X, op=ALU.add)), ["ya"], ["st4b"])
                    V(("tensor_scalar", _a(out=st4[:, 4:8], in0=st4[:, 4:8], scalar1=1.0 / 64, scalar2=None, op0=ALU.mult)), ["st4b"], ["st4b"])
                    V(("tensor_tensor", _a(out=ya.rearrange("p (h q) -> p h q", h=4), in0=ya.rearrange("p (h q) -> p h q", h=4),
                                                in1=st4[:, 4:8].unsqueeze(2).to_broadcast([128, 4, 64]), op=ALU.subtract)), ["ya", "st4b"], ["ya"])
                    G(("tensor_tensor", _a(out=yb, in0=ya, in1=ya, op=ALU.mult)), ["ya"], ["yb"])
                    V(("tensor_reduce", _a(out=st4[:, 8:12], in_=yb.rearrange("p (h q) -> p h q", h=4), axis=AX.X, op=ALU.add)), ["yb"], ["st4c"])
                    A(("activation", _a(out=st4[:, 8:12], in_=st4[:, 8:12], func=AF.Sqrt, scale=1.0 / 64, bias=64e-5)), ["st4c"], ["st4c"])
                    V(("reciprocal", _a(out=st4[:, 8:12], in_=st4[:, 8:12])), ["st4c"], ["st4c"])
                    V(("tensor_tensor", _a(out=ya.rearrange("p (h q) -> p h q", h=4), in0=ya.rearrange("p (h q) -> p h q", h=4),
                                                in1=st4[:, 8:12].unsqueeze(2).to_broadcast([128, 4, 64]), op=ALU.mult)), ["ya", "st4c"], ["ya"])
                    V(("tensor_tensor", _a(out=ya, in0=ya, in1=lng[:], op=ALU.mult)), ["ya", "lng"], ["ya"])
                    V(("tensor_tensor", _a(out=ya, in0=ya, in1=lnb[:], op=ALU.add)), ["ya", "lnb"], ["ya"])
                    yield
                    G(("tensor_tensor", _a(out=rtmp[:], in0=rkv[:, 0:2, :], in1=rkv[:, 2:4, :], op=ALU.mult)), ["rkv", "rtmp"], ["rtmp"])
                    G(("tensor_tensor", _a(out=prodT[:], in0=rtmp[:], in1=kk3[:, 2, :].unsqueeze(2).to_broadcast([128, 2, 128]), op=ALU.mult)),
                      ["rtmp", "kk3"], ["prodT"])
                    b_, bk = bank()
                    for bb in range(2):
                        mm(b_[:, 2 * bb:2 * bb + 2], prodT[:, bb, :], hsel[:], True, True, ["prodT", "hsel"], [bk])
                    V(("tensor_copy", _a(out=st4[:, 12:16], in_=b_[:, 0:4])), [bk], ["st4d"])
                    V(("tensor_tensor", _a(out=yb.rearrange("p (h q) -> p h q", h=4), in0=tok4[:, 3, :].rearrange("p (h q) -> p h q", h=4),
                                                in1=st4[:, 12:16].unsqueeze(2).to_broadcast([128, 4, 64]), op=ALU.mult)), ["tok4", "st4d", "yb"], ["yb"])
                    V(("tensor_tensor", _a(out=ya, in0=ya, in1=yb, op=ALU.add)), ["ya", "yb"], ["ya"])
                    yield
                    b_, bk = bank()
                    mm(b_[:, 0:256], sgd[:], gup[:], True, True, ["sgd", "gup"], [bk])
                    V(("tensor_tensor", _a(out=mix[:, 256:512], in0=ya, in1=b_[:, 0:256], op=ALU.mult)), ["ya", bk], ["mixr"])
                    yield
                    G(("tensor_tensor", _a(out=ytmp[:].rearrange("p (h q) -> p h q", h=8), in0=xsB[:, 0:512].rearrange("p (h q) -> p h q", h=8),
                                                in1=ssdD[:].unsqueeze(2).to_broadcast([128, 8, 64]), op=ALU.mult)), ["xsB", "ssdD", "ytmp"], ["ytmp"])
                    V(("tensor_tensor", _a(out=ysum[:], in0=ysum[:], in1=ytmp[:], op=ALU.add)), ["ysum", "ytmp"], ["ysum"])
                    V(("tensor_tensor", _a(out=ysum[:], in0=ysum[:], in1=sgz[i % 3][:, 256:768], op=ALU.mult)), ["ysum", "sgz%d" % (i % 3)], ["ysum"])
                    A(("activation", _a(out=ytmp[:], in_=ysum[:], func=AF.Square, accum_out=st8[:, 3:4])), ["ysum", "ytmp"], ["ytmp", "st8e"])
                    A(("activation", _a(out=st8[:, 3:4], in_=st8[:, 3:4], func=AF.Sqrt, scale=1.0 / 512, bias=EPS)), ["st8e"], ["st8e"])
                    V(("reciprocal", _a(out=st8[:, 3:4], in_=st8[:, 3:4])), ["st8e"], ["st8e"])
                    V(("scalar_tensor_tensor", _a(out=mix[:, 512:1024], in0=ysum[:], scalar=st8[:, 3:4], in1=ssdn[:], op0=ALU.mult, op1=ALU.mult)),
                      ["ysum", "st8e", "ssdn"], ["mixs"])
                    yield
                    if DBG != "B2":
                        P.dma("sync", mixd[ci * C:(ci + 1) * C, :], mix[:], ["mixg", "mixr", "mixs"], [])

                if DBG == "consts" or DBG3 == "setup":
                    continue
                def drain(g_):
                    for _ in g_:
                        pass

                drain(stageA(0))
                if NCH > 1:
                    drain(stageA(1))
                for i in range(NCH):
                    gb = stageB(i)
                    for tok_ in gb:
                        if tok_ == "mixdone":
                            break
                    gens = [gb] + ([stageA(i + 2)] if i + 2 < NCH else [])
                    if os.environ.get("KSEQ"):
                        for g_ in gens:
                            drain(g_)
                        continue
                    while gens:
                        for g_ in list(gens):
                            try:
                                next(g_)
                            except StopIteration:
                                gens.remove(g_)
                if DBG in ("A", "ssd", "gla", "F"):
                    break
                if DBG == "Bonly":
                    break

        if DBG in ("A", "ssd", "gla", "F", "consts", "B", "B1", "B2", "B1noJ", "B1noDMA", "FF", "Bonly"):
            break
        P.barrier()
        with ExitStack() as ss:
            sb = lambda name, shape, dt: P.sb("L%dC_%s" % (l, name), shape, dt, ss)
            cst = mk_consts(sb, False)
            identb, Jb = cst["identb"], cst["Jb"]
            set_stg(sb, 1024)
            wout = sb("wout", [128, 8, D], BF16)
            wg = sb("wg", [128, 8, DFF], BF16)
            wu = sb("wu", [128, 8, DFF], BF16)
            wd = sb("wd", [128, NFC, D], BF16)
            load_w(wout, "wout", W["wout", l], D, D)
            load_w(wg, "wg", W["wg", l], D, DFF)
            load_w(wu, "wu", W["wu", l], D, DFF)
            load_w(wd, "wd", W["wd", l], DFF, D)
            gffn = sb("gffn", [128, 8], F32)
            P.dma("sync", gffn[:], W["gffn", l][:, :], [], ["gffn"])
            last = (l == DEPTH - 1)
            if last:
                gfin = sb("gfin", [128, D], F32)
                P.dma("sync", gfin[:], W["gfin"][:, :], [], ["gfin"])
            TT = min(int(os.environ.get("KTT", "2")), NCH)
            xc = [sb("xc%d" % i, [128, D], F32) for i in range(2 * TT)]
            mixc = [sb("mixc%d" % i, [128, D], BF16) for i in range(2)]
            mixT = sb("mixT", [128, 8, 128], BF16)
            hTc = sb("hTc", [128, 8, TT * 128], BF16)
            actT = sb("actT", [128, NFC, TT * 128], BF16)
            sgc = [sb("sgc%d" % i, [128, TT * 128], BF16) for i in range(2)]
            hn = sb("hn", [128, D], BF16)
            st8 = sb("st8", [128, 8], F32)
            xo = sb("xo", [128, D], F32)
            dst = y_out if last else xnext
            ntile = (NCH + TT - 1) // TT
            def issue_loads(tj):
                for sub_ in range(min(TT, NCH - tj * TT)):
                    cj = tj * TT + sub_
                    xj = (tj % 2) * TT + sub_
                    P.dma("sync", xc[xj][:], xsrc[cj * C:(cj + 1) * C, :], [], ["xc%d" % xj])
                    P.dma("gpsimd", mixc[cj % 2][:], mixd[cj * C:(cj + 1) * C, :], [], ["mixc%d" % (cj % 2)])

            issue_loads(0)
            for ti in range(ntile):
                nsub = min(TT, NCH - ti * TT)
                NTK = nsub * 128
                xb0 = (ti % 2) * TT
                for sub in range(nsub):
                    ci = ti * TT + sub
                    xk = "xc%d" % (xb0 + sub)
                    mk_ = "mixc%d" % (ci % 2)
                    mc = mixc[ci % 2]
                    b_, bk = bank()
                    for kc in range(8):
                        T(("transpose", _a(bf(b_)[:, kc * 128:(kc + 1) * 128], mc[:, kc * 128:(kc + 1) * 128], Jb[:])),
                          [mk_, "Jb"], [bk])
                    A(("copy", _a(out=mixT[:].rearrange("p a t -> p (a t)"), in_=bf(b_)[:, :])), [bk], ["mixT"])
                    for n in range(2):
                        b_, bk = bank()
                        for kc in range(8):
                            mm(b_[:, :], mixT[:, kc, :], wout[:, kc, 512 * n:512 * n + 512], kc == 0, kc == 7, ["mixT", "wout"], [bk])
                        V(("tensor_tensor", _a(out=xc[xb0 + sub][:, 512 * n:512 * n + 512], in0=b_[:, :], in1=xc[xb0 + sub][:, 512 * n:512 * n + 512], op=ALU.add)),
                          [bk, xk], [xk])
                    A(("activation", _a(out=hn[:], in_=xc[xb0 + sub][:], func=AF.Square, accum_out=st8[:, 0:1])), [xk], ["hn", "st8a"])
                    A(("activation", _a(out=st8[:, 1:2], in_=st8[:, 0:1], func=AF.Sqrt, scale=1.0 / D, bias=EPS)), ["st8a"], ["st8b"])
                    V(("reciprocal", _a(out=st8[:, 2:3], in_=st8[:, 1:2])), ["st8b"], ["st8c"])
                    V(("tensor_scalar", _a(out=hn[:], in0=xc[xb0 + sub][:], scalar1=st8[:, 2:3], scalar2=None, op0=ALU.mult)), [xk, "st8c"], ["hn"])
                    b_, bk = bank()
                    for kc in range(8):
                        T(("transpose", _a(bf(b_)[:, kc * 128:(kc + 1) * 128], hn[:, kc * 128:(kc + 1) * 128], identb[:])), ["hn", "identb"], [bk])
                    V(("tensor_tensor", _a(out=hTc[:, :, sub * 128:(sub + 1) * 128], in0=bf(b_).rearrange("p (a b) -> p a b", a=8),
                                                               in1=gffn[:].unsqueeze(2).to_broadcast([128, 8, 128]), op=ALU.mult)), [bk, "gffn"], ["hTc"])
                if ti + 1 < ntile:
                    issue_loads(ti + 1)
                for fc in range(NFC):
                    bg_, bgk = bank()
                    bu_, buk = bank()
                    for kc in range(8):
                        mm(bg_[:, 0:NTK], wg[:, kc, fc * 128:(fc + 1) * 128], hTc[:, kc, 0:NTK], kc == 0, kc == 7, ["wg", "hTc"], [bgk])
                    for kc in range(8):
                        mm(bu_[:, 0:NTK], wu[:, kc, fc * 128:(fc + 1) * 128], hTc[:, kc, 0:NTK], kc == 0, kc == 7, ["wu", "hTc"], [buk])
                    sgb = sgc[fc % 2]
                    sgk = "sgc%d" % (fc % 2)
                    A(("activation", _a(out=sgb[:, 0:NTK], in_=bg_[:, 0:NTK], func=AF.Silu)), [bgk], [sgk])
                    V(("tensor_tensor", _a(out=actT[:, fc, 0:NTK], in0=bu_[:, 0:NTK], in1=sgb[:, 0:NTK], op=ALU.mult)),
                      [buk, sgk], ["actT"])
                for sub in range(nsub):
                    ci = ti * TT + sub
                    xk = "xc%d" % (xb0 + sub)
                    for n in range(2):
                        b_, bk = bank()
                        for fc in range(NFC):
                            mm(b_[:, :], actT[:, fc, sub * 128:(sub + 1) * 128], wd[:, fc, 512 * n:512 * n + 512], fc == 0, fc == NFC - 1, ["actT", "wd"], [bk])
                        V(("tensor_tensor", _a(out=xo[:, 512 * n:512 * n + 512], in0=b_[:, :], in1=xc[xb0 + sub][:, 512 * n:512 * n + 512], op=ALU.add)),
                          [bk, xk], ["xo"])
                    if last:
                        A(("activation", _a(out=hn[:], in_=xo[:], func=AF.Square, accum_out=st8[:, 4:5])), ["xo", "hn"], ["hn", "st8f"])
                        A(("activation", _a(out=st8[:, 4:5], in_=st8[:, 4:5], func=AF.Sqrt, scale=1.0 / D, bias=EPS)), ["st8f"], ["st8f"])
                        V(("reciprocal", _a(out=st8[:, 4:5], in_=st8[:, 4:5])), ["st8f"], ["st8f"])
                        V(("scalar_tensor_tensor", _a(out=xo[:], in0=xo[:], scalar=st8[:, 4:5], in1=gfin[:], op0=ALU.mult, op1=ALU.mult)),
                          ["xo", "st8f", "gfin"], ["xo"])
                    P.dma("sync", dst[ci * C:(ci + 1) * C, :], xo[:], ["xo"], [])
    P.emit()
    P.stack.close()
    return nc


def make_weight_maps(prm, DEPTH=2):
    f = lambda a: np.ascontiguousarray(np.asarray(a, dtype=np.float32))
    rep = lambda v, n=128: f(np.broadcast_to(np.asarray(v, np.float32).reshape(1, -1), (n, np.asarray(v).size)))
    fmaj = lambda v, nb: f(np.asarray(v, np.float32).reshape(nb, 128).T)
    m = {}
    for l in range(DEPTH):
        win = np.asarray(prm["w_in"][l], np.float32)
        for d in (0, 1):
            m["wfm_%d_%d" % (l, d)] = f(win[:, col_index(fm_blocks(d))])
            m["wtm_%d_%d" % (l, d)] = f(win[:, col_index(tm_cols(d))])
            m["gla_aup_%d_%d" % (l, d)] = f(prm["gla_a_up"][l][d])
            m["gla_ab_%d_%d" % (l, d)] = f(prm["gla_a_bias"][l][d].reshape(1, 128))
            mu = np.asarray(prm["rwkv_mu"][l], np.float32)
            mub = np.zeros((128, 9), np.float32)
            for b in range(6):
                mub[:, b] = mu[128 * b:128 * b + 128]
            mub[0:64, 6] = mu[768 + 64 * d:768 + 64 * d + 64]
            mub[0:64, 7] = mu[896 + 64 * d:896 + 64 * d + 64]
            mub[:, 8] = mu[1024:1152]
            m["mu_%d_%d" % (l, d)] = mub
            m["w0_%d_%d" % (l, d)] = f(prm["rwkv_w0"][l][d].reshape(1, 256))
            m["wup_%d_%d" % (l, d)] = f(prm["rwkv_w_up"][l][d])
            m["a0_%d_%d" % (l, d)] = fmaj(prm["rwkv_a0"][l][d], 2)
            m["aup_%d_%d" % (l, d)] = f(prm["rwkv_a_up"][l][d])
            m["dtb_%d_%d" % (l, d)] = rep(prm["ssd_dt_bias"][l][d])
            m["alog_%d_%d" % (l, d)] = rep(prm["ssd_A_log"][l][d])
        m["wout_%d" % l] = f(prm["w_out"][l])
        m["wg_%d" % l] = f(prm["ffn_gate"][l])
        m["wu_%d" % l] = f(prm["ffn_up"][l])
        m["wd_%d" % l] = f(prm["ffn_down"][l])
        m["gmix_%d" % l] = fmaj(prm["norm_mix"][l], 8)
        m["gffn_%d" % l] = fmaj(prm["norm_ffn"][l], 8)
        m["gla_norm_%d" % l] = rep(prm["gla_norm"][l])
        m["gup_%d" % l] = f(prm["rwkv_g_up"][l])
        kk3 = np.stack([fmaj(prm["rwkv_k_k"][l], 2), fmaj(prm["rwkv_k_a"][l], 2), fmaj(prm["rwkv_r_k"][l], 2)], axis=1)
        m["kk3_%d" % l] = f(kk3.reshape(128, 6))
        m["lng_%d" % l] = rep(prm["rwkv_ln_g"][l])
        m["lnb_%d" % l] = rep(prm["rwkv_ln_b"][l])
        cw = np.asarray(prm["ssd_conv_w"][l], np.float32)
        m["convw_%d" % l] = f(cw.reshape(5, 8, 128).transpose(2, 1, 0).reshape(128, 40))
        m["convb_%d" % l] = fmaj(prm["ssd_conv_b"][l], 8)
        m["ssdD_%d" % l] = rep(prm["ssd_D"][l])
        m["ssdn_%d" % l] = rep(prm["ssd_norm"][l])
    m["gfin"] = rep(prm["final_norm"])
    return m


_CACHE = {}


def run_cores(core_x, core_flag, prm, NSLOT, SLOT, DEPTH=2, runner=None):
    key = (NSLOT, SLOT, DEPTH)
    nc = build_program(NSLOT, SLOT, DEPTH)
    wm = make_weight_maps(prm, DEPTH)
    in_maps = []
    for x, fl in zip(core_x, core_flag):
        mp = dict(wm)
        mp["x_in"] = np.ascontiguousarray(x, dtype=np.float32)
        mp["flag"] = np.full((128, 1), fl, np.float32)
        in_maps.append(mp)
    if runner is None:
        res = run_bass_kernel_spmd(nc, in_maps, core_ids=list(range(len(in_maps))))
        return [r["y_out"] for r in res.results]
    return [r["y_out"] for r in runner(nc, in_maps)]


def kernel(**inputs):
    xp = np.asarray(inputs["x_prompt"], np.float32)
    xs = np.asarray(inputs["x_sample"], np.float32)
    prm = {k: np.asarray(v) for k, v in inputs.items() if k not in ("x_prompt", "x_sample")}
    NSLOT, SLOT = 8, 2048
    NT = NSLOT * SLOT
    assign = [[], []] + [[] for _ in range(6)]
    for b in range(xp.shape[0]):
        assign[2 + b % 6].append(b)
    core_x, core_flag = [], []
    for c in range(8):
        if c < 2:
            core_x.append(xs[c])
            core_flag.append(1.0)
        else:
            buf = np.empty((NT, D), np.float32)
            for s in range(NSLOT):
                b = assign[c][s % len(assign[c])]
                buf[s * SLOT:(s + 1) * SLOT] = xp[b]
            core_x.append(buf)
            core_flag.append(0.0)
    outs = run_cores(core_x, core_flag, prm, NSLOT, SLOT)
    yp = np.zeros_like(xp)
    ys = np.zeros_like(xs)
    for c in range(8):
        if c < 2:
            ys[c] = outs[c]
        else:
            for s, b in enumerate(assign[c]):
                yp[b] = outs[c][s * SLOT:(s + 1) * SLOT]
    return (yp, ys)
```
